# Optimizing a Trainium2 kernel written in Bass

```python
import jax, jax.numpy as jnp
from jax import lax
import numpy as np

D_MODEL = 2048
BATCH = 4
SEQ = 2048
DEPTH = 4
DEC_BATCH = 128
DEC_SEQ = 8
PAST_LEN = 16384
PAGE_SIZE = 128

N_AB_LAYERS = (DEPTH + 1) // 2
N_LRU_LAYERS = DEPTH // 2
CONV_W = 4
H_A = 8
DK_A = 128
DV_A = 128
H_B = 4
DK_B = 128
DV_B = 256
GLA_LOWRANK = 16
GLA_NORMALIZER = 16.0
W_LRU = D_MODEL
LRU_BLOCKS = 16
LRU_BW = W_LRU // LRU_BLOCKS
LRU_C = 8.0
CHUNK_A = 64
CHUNK_B = 16
KA = H_A * DK_A
VA = H_A * DV_A
KB = H_B * DK_B
VB = H_B * DV_B
AB_IN = 2 * KA + VA + 2 * H_A + VA + 2 * KB + VB + GLA_LOWRANK + VB
AB_OUT = VA + VB
EPS = 1e-6

kernel_name = 'hybrid_gdn_gla_rglru_step'


def _rmsnorm(x, g):
    xf = x.astype(jnp.float32)
    y = xf * lax.rsqrt(jnp.mean(xf * xf, axis=-1, keepdims=True) + EPS)
    return (y * g.astype(jnp.float32)).astype(x.dtype)


def _l2norm(x):
    xf = x.astype(jnp.float32)
    return xf * lax.rsqrt(jnp.sum(xf * xf, axis=-1, keepdims=True) + EPS)


def _causal_conv(x, buf, w, b=None):
    K = w.shape[0]
    T = x.shape[1]
    xp = jnp.concatenate([buf.astype(x.dtype), x], axis=1)
    y = xp[:, 0:T] * w[0]
    for j in range(1, K):
        y = y + xp[:, j:j + T] * w[j]
    if b is not None:
        y = y + b
    return y, xp[:, T:]


def _to_chunks(a, C):
    B, T = a.shape[0], a.shape[1]
    N = -(-T // C)
    a = jnp.pad(a, [(0, 0), (0, N * C - T)] + [(0, 0)] * (a.ndim - 2))
    a = a.reshape((B, N, C) + a.shape[2:])
    return jnp.transpose(a, (1, 0, 3, 2) + tuple(range(4, a.ndim)))


def _from_chunks(o, T):
    N, B, H, C, D = o.shape
    return jnp.transpose(o, (1, 0, 3, 2, 4)).reshape(B, N * C, H, D)[:, :T]


def _gated_delta_rule(q, k, v, g, beta, S0):
    T = q.shape[1]
    DV = v.shape[-1]
    C = min(CHUNK_A, T)
    q, k, v, g, beta = [_to_chunks(a.astype(jnp.float32), C) for a in (q, k, v, g, beta)]
    G = jnp.cumsum(g, axis=-1)
    incl = jnp.tril(jnp.ones((C, C), bool))
    strict = jnp.tril(jnp.ones((C, C), bool), -1)
    diff = G[..., :, None] - G[..., None, :]
    decay = jnp.where(incl, jnp.exp(jnp.where(incl, diff, 0.0)), 0.0)
    kb = k * beta[..., None]
    A = jnp.where(strict, jnp.einsum('nbhid,nbhjd->nbhij', kb, k) * decay, 0.0)
    rhs = jnp.concatenate([v * beta[..., None], kb * jnp.exp(G)[..., None]], axis=-1)
    sol = lax.linalg.triangular_solve(A + jnp.eye(C, dtype=jnp.float32), rhs,
                                      left_side=True, lower=True, unit_diagonal=True)
    u, w = sol[..., :DV], sol[..., DV:]
    qk = jnp.einsum('nbhid,nbhjd->nbhij', q, k) * decay
    qg = q * jnp.exp(G)[..., None]

    def step(S, inp):
        qg_c, k_c, u_c, w_c, G_c, qk_c = inp
        v_new = u_c - jnp.einsum('bhck,bhkv->bhcv', w_c, S)
        o = jnp.einsum('bhck,bhkv->bhcv', qg_c, S) + jnp.einsum('bhij,bhjv->bhiv', qk_c, v_new)
        g_last = G_c[..., -1]
        k_dec = k_c * jnp.exp(g_last[..., None] - G_c)[..., None]
        S = S * jnp.exp(g_last)[..., None, None] + jnp.einsum('bhck,bhcv->bhkv', k_dec, v_new)
        return S, o

    S, o = lax.scan(step, S0.astype(jnp.float32), (qg, k, u, w, G, qk))
    return _from_chunks(o, T), S


def _gla(q, k, v, gk, S0):
    T = q.shape[1]
    C = min(CHUNK_B, T)
    q, k, v, gk = [_to_chunks(a.astype(jnp.float32), C) for a in (q, k, v, gk)]
    Bc = jnp.cumsum(gk, axis=-2)
    incl = jnp.tril(jnp.ones((C, C), bool))
    qg = q * jnp.exp(Bc)
    A = jnp.where(incl, jnp.einsum('nbhik,nbhjk->nbhij', qg, k * jnp.exp(-Bc)), 0.0)
    intra = jnp.einsum('nbhij,nbhjv->nbhiv', A, v)

    def step(S, inp):
        qg_c, k_c, v_c, B_c, intra_c = inp
        o = jnp.einsum('bhck,bhkv->bhcv', qg_c, S) + intra_c
        b_last = B_c[..., -1, :]
        k_dec = k_c * jnp.exp(b_last[..., None, :] - B_c)
        S = S * jnp.exp(b_last)[..., None] + jnp.einsum('bhck,bhcv->bhkv', k_dec, v_c)
        return S, o

    S, o = lax.scan(step, S0.astype(jnp.float32), (qg, k, v, Bc, intra))
    return _from_chunks(o, T), S


def _ab_split_points():
    sizes = [2 * KA + VA, H_A, H_A, VA, KB, KB, VB, GLA_LOWRANK, VB]
    return [int(s) for s in np.cumsum(sizes)[:-1]]


def _ab_mixer(x, S_a, conv_a, S_b, w_in, conv_w, a_log, dt_bias, norm_a, w_lr, b_lr, norm_b, w_out):
    B, T, _ = x.shape
    f32 = jnp.float32
    proj = jnp.einsum('btd,de->bte', x, w_in)
    qkv, b_raw, a_raw, z_a, q_b, k_b, v_b, lr_b, z_b = jnp.split(proj, _ab_split_points(), axis=-1)
    qkv, new_conv_a = _causal_conv(qkv, conv_a, conv_w)
    qkv = jax.nn.silu(qkv)
    q_a, k_a, v_a = jnp.split(qkv, [KA, 2 * KA], axis=-1)
    q_a = _l2norm(q_a.reshape(B, T, H_A, DK_A)) * (DK_A ** -0.5)
    k_a = _l2norm(k_a.reshape(B, T, H_A, DK_A))
    v_a = v_a.reshape(B, T, H_A, DV_A)
    beta = jax.nn.sigmoid(b_raw.astype(f32))
    g = -jnp.exp(a_log.astype(f32)) * jax.nn.softplus(a_raw.astype(f32) + dt_bias.astype(f32))
    o_a, S_a_new = _gated_delta_rule(q_a, k_a, v_a, g, beta, S_a)
    o_a = _rmsnorm(o_a.astype(x.dtype), norm_a) * jax.nn.silu(z_a.reshape(B, T, H_A, DV_A))
    q_b = q_b.reshape(B, T, H_B, DK_B) * (DK_B ** -0.5)
    k_b = k_b.reshape(B, T, H_B, DK_B)
    v_b = v_b.reshape(B, T, H_B, DV_B)
    gk = jax.nn.log_sigmoid((jnp.einsum('btr,rk->btk', lr_b, w_lr) + b_lr).astype(f32)) / GLA_NORMALIZER
    o_b, S_b_new = _gla(q_b, k_b, v_b, gk.reshape(B, T, H_B, DK_B), S_b)
    o_b = _rmsnorm(o_b.astype(x.dtype), norm_b) * jax.nn.silu(z_b.reshape(B, T, H_B, DV_B))
    o = jnp.concatenate([o_a.reshape(B, T, VA), o_b.reshape(B, T, VB)], axis=-1)
    return jnp.einsum('bte,ed->btd', o, w_out), S_a_new, new_conv_a, S_b_new


def _lru_mixer(x, h0, conv_buf, w_in, conv_w, conv_b, w_a, b_a, w_x, b_x, lam, w_out, reset_first):
    B, T, _ = x.shape
    f32 = jnp.float32
    xb, gate = jnp.split(jnp.einsum('btd,de->bte', x, w_in), 2, axis=-1)
    xc, new_buf = _causal_conv(xb, conv_buf, conv_w, conv_b)
    xh = xc.reshape(B, T, LRU_BLOCKS, LRU_BW)
    r = jax.nn.sigmoid((jnp.einsum('btnc,ncd->btnd', xh, w_a).reshape(B, T, W_LRU) + b_a).astype(f32))
    i = jax.nn.sigmoid((jnp.einsum('btnc,ncd->btnd', xh, w_x).reshape(B, T, W_LRU) + b_x).astype(f32))
    log_a = -LRU_C * r * jax.nn.softplus(-lam.astype(f32))
    a = jnp.exp(log_a)
    mult = jnp.sqrt(-jnp.expm1(2.0 * log_a))
    if reset_first:
        mult = mult.at[:, 0].set(1.0)
    bx = mult * i * xc.astype(f32)

    def step(h, inp):
        a_t, b_t = inp
        h = a_t * h + b_t
        return h, h

    hT, hs = lax.scan(step, h0.astype(f32), (jnp.swapaxes(a, 0, 1), jnp.swapaxes(bx, 0, 1)))
    y = jnp.swapaxes(hs, 0, 1).astype(x.dtype) * jax.nn.silu(gate)
    return jnp.einsum('btw,wd->btd', y, w_out), hT, new_buf


def _trunk(x, st_delta, st_dconv, st_gla, st_lru, st_lconv, ab_norm, ab_w_in, ab_conv_w, ab_a_log,
           ab_dt_bias, ab_norm_a, ab_gla_w_lr, ab_gla_b_lr, ab_norm_b, ab_w_out, lru_norm, lru_w_in,
           lru_conv_w, lru_conv_b, lru_w_a, lru_b_a, lru_w_x, lru_b_x, lru_lambda, lru_w_out,
           final_norm, reset_first):
    n_delta, n_dconv, n_gla, n_lru, n_lconv = [], [], [], [], []
    for l in range(DEPTH):
        j = l // 2
        if l % 2 == 0:
            h = _rmsnorm(x, ab_norm[j])
            y, sa, ca, sb = _ab_mixer(h, st_delta[j], st_dconv[j], st_gla[j], ab_w_in[j], ab_conv_w[j],
                                      ab_a_log[j], ab_dt_bias[j], ab_norm_a[j], ab_gla_w_lr[j],
                                      ab_gla_b_lr[j], ab_norm_b[j], ab_w_out[j])
            n_delta.append(sa)
            n_dconv.append(ca)
            n_gla.append(sb)
        else:
            h = _rmsnorm(x, lru_norm[j])
            y, hl, cl = _lru_mixer(h, st_lru[j], st_lconv[j], lru_w_in[j], lru_conv_w[j], lru_conv_b[j],
                                   lru_w_a[j], lru_b_a[j], lru_w_x[j], lru_b_x[j], lru_lambda[j],
                                   lru_w_out[j], reset_first)
            n_lru.append(hl)
            n_lconv.append(cl)
        x = x + y
    return (_rmsnorm(x, final_norm), jnp.stack(n_delta), jnp.stack(n_dconv), jnp.stack(n_gla),
            jnp.stack(n_lru), jnp.stack(n_lconv))


def setup_inputs(seed: int = 0) -> dict:
    key = jax.random.key(seed)
    ks = iter(jax.random.split(key, 32))
    f32 = jnp.float32

    def nrm(shape, scale):
        return jax.random.normal(next(ks), shape, f32) * scale

    NA, NL = N_AB_LAYERS, N_LRU_LAYERS
    out_scale = 0.5
    x_prompt = nrm((BATCH, SEQ, D_MODEL), 1.0)
    x_sample = nrm((DEC_BATCH, DEC_SEQ, D_MODEL), 1.0)
    state_delta = nrm((NA, DEC_BATCH, H_A, DK_A, DV_A), DK_A ** -0.5)
    state_delta_conv = nrm((NA, DEC_BATCH, CONV_W - 1, 2 * KA + VA), 1.0)
    state_gla = nrm((NA, DEC_BATCH, H_B, DK_B, DV_B), 0.1)
    state_lru = nrm((NL, DEC_BATCH, W_LRU), 0.5)
    state_lru_conv = nrm((NL, DEC_BATCH, CONV_W - 1, W_LRU), 1.0)
    ab_norm = 1.0 + nrm((NA, D_MODEL), 0.02)
    ab_w_in = nrm((NA, D_MODEL, AB_IN), D_MODEL ** -0.5)
    ab_conv_w = nrm((NA, CONV_W, 2 * KA + VA), CONV_W ** -0.5)
    ab_a_log = jnp.log(jax.random.uniform(next(ks), (NA, H_A), f32, 1.0, 16.0))
    dt = jnp.exp(jax.random.uniform(next(ks), (NA, H_A), f32, float(np.log(1e-3)), float(np.log(1e-1))))
    ab_dt_bias = dt + jnp.log(-jnp.expm1(-dt))
    ab_norm_a = 1.0 + nrm((NA, DV_A), 0.02)
    ab_gla_w_lr = nrm((NA, GLA_LOWRANK, KB), GLA_LOWRANK ** -0.5)
    ab_gla_b_lr = nrm((NA, KB), 0.01)
    ab_norm_b = 1.0 + nrm((NA, DV_B), 0.02)
    ab_w_out = nrm((NA, AB_OUT, D_MODEL), AB_OUT ** -0.5 * out_scale)
    lru_norm = 1.0 + nrm((NL, D_MODEL), 0.02)
    lru_w_in = nrm((NL, D_MODEL, 2 * W_LRU), D_MODEL ** -0.5)
    lru_conv_w = nrm((NL, CONV_W, W_LRU), CONV_W ** -0.5)
    lru_conv_b = nrm((NL, W_LRU), 0.01)
    lru_w_a = nrm((NL, LRU_BLOCKS, LRU_BW, LRU_BW), LRU_BW ** -0.5)
    lru_b_a = nrm((NL, W_LRU), 0.01)
    lru_w_x = nrm((NL, LRU_BLOCKS, LRU_BW, LRU_BW), LRU_BW ** -0.5)
    lru_b_x = nrm((NL, W_LRU), 0.01)
    u = jax.random.uniform(next(ks), (NL, W_LRU), f32, 0.9, 0.999)
    s = u ** (1.0 / LRU_C)
    lru_lambda = jnp.log(s) - jnp.log1p(-s)
    lru_w_out = nrm((NL, W_LRU, D_MODEL), W_LRU ** -0.5 * out_scale)
    final_norm = 1.0 + nrm((D_MODEL,), 0.02)
    return {'x_prompt': x_prompt, 'x_sample': x_sample,
            'state_delta': state_delta, 'state_delta_conv': state_delta_conv, 'state_gla': state_gla,
            'state_lru': state_lru, 'state_lru_conv': state_lru_conv,
            'ab_norm': ab_norm, 'ab_w_in': ab_w_in, 'ab_conv_w': ab_conv_w, 'ab_a_log': ab_a_log,
            'ab_dt_bias': ab_dt_bias, 'ab_norm_a': ab_norm_a, 'ab_gla_w_lr': ab_gla_w_lr,
            'ab_gla_b_lr': ab_gla_b_lr, 'ab_norm_b': ab_norm_b, 'ab_w_out': ab_w_out,
            'lru_norm': lru_norm, 'lru_w_in': lru_w_in, 'lru_conv_w': lru_conv_w, 'lru_conv_b': lru_conv_b,
            'lru_w_a': lru_w_a, 'lru_b_a': lru_b_a, 'lru_w_x': lru_w_x, 'lru_b_x': lru_b_x,
            'lru_lambda': lru_lambda, 'lru_w_out': lru_w_out, 'final_norm': final_norm}


def reference(x_prompt, x_sample, state_delta, state_delta_conv, state_gla, state_lru, state_lru_conv,
              ab_norm, ab_w_in, ab_conv_w, ab_a_log, ab_dt_bias, ab_norm_a, ab_gla_w_lr, ab_gla_b_lr,
              ab_norm_b, ab_w_out, lru_norm, lru_w_in, lru_conv_w, lru_conv_b, lru_w_a, lru_b_a,
              lru_w_x, lru_b_x, lru_lambda, lru_w_out, final_norm):
    weights = (ab_norm, ab_w_in, ab_conv_w, ab_a_log, ab_dt_bias, ab_norm_a, ab_gla_w_lr, ab_gla_b_lr,
               ab_norm_b, ab_w_out, lru_norm, lru_w_in, lru_conv_w, lru_conv_b, lru_w_a, lru_b_a,
               lru_w_x, lru_b_x, lru_lambda, lru_w_out, final_norm)
    Bp = x_prompt.shape[0]
    f32 = jnp.float32
    z_delta = jnp.zeros((N_AB_LAYERS, Bp, H_A, DK_A, DV_A), f32)
    z_dconv = jnp.zeros((N_AB_LAYERS, Bp, CONV_W - 1, 2 * KA + VA), x_prompt.dtype)
    z_gla = jnp.zeros((N_AB_LAYERS, Bp, H_B, DK_B, DV_B), f32)
    z_lru = jnp.zeros((N_LRU_LAYERS, Bp, W_LRU), f32)
    z_lconv = jnp.zeros((N_LRU_LAYERS, Bp, CONV_W - 1, W_LRU), x_prompt.dtype)
    y_prompt, p_delta, p_dconv, p_gla, p_lru, p_lconv = _trunk(
        x_prompt, z_delta, z_dconv, z_gla, z_lru, z_lconv, *weights, True)
    y_sample, s_delta, s_dconv, s_gla, s_lru, s_lconv = _trunk(
        x_sample, state_delta, state_delta_conv, state_gla, state_lru, state_lru_conv, *weights, False)
    return (y_prompt, y_sample, p_delta, p_dconv, p_gla, p_lru, p_lconv,
            s_delta, s_dconv, s_gla, s_lru, s_lconv)
```

```python
import numpy as np
import concourse.bass as bass
import concourse.mybir as mybir
from concourse.bass_utils import run_bass_kernel_spmd

F32 = mybir.dt.float32
BF16 = mybir.dt.bfloat16
AF = mybir.ActivationFunctionType
ALU = mybir.AluOpType

D = 2048
NT = 17
NTOK = NT * 128
SEQ = 2048
EPS = 1e-6
NSEQ = 16
AB_IN = 7200


class Buf:
    __slots__ = ("name", "lw", "rd", "dsem", "dcount", "excl")

    def __init__(self, name="", excl=False):
        self.name = name
        self.excl = excl
        self.lw = None
        self.rd = []
        self.dsem = None
        self.dcount = 0


class T:
    __slots__ = ("ap", "buf")

    def __init__(self, ap, buf):
        self.ap = ap
        self.buf = buf

    def __getitem__(self, k):
        return T(self.ap[k], self.buf)

    def bc(self, shape):
        return T(self.ap.to_broadcast(list(shape)), self.buf)

    def re(self, pat, **kw):
        return T(self.ap.rearrange(pat, **kw), self.buf)

    def bitcast(self, dt):
        return T(self.ap.bitcast(dt), self.buf)

    def with_buf(self, buf):
        return T(self.ap, buf)


class Sched:
    ENG = ("pe", "act", "dve", "pool", "sp")
    SEM_LIMIT = 30000

    def __init__(self, nc, same_engine_sync=True):
        self.nc = nc
        self.prog = {e: [] for e in self.ENG}
        self.sem = {}
        self.cnt = {}
        self.nsem = 0
        for e in self.ENG:
            self._newsem(e)
        self.waited = {e: {} for e in self.ENG}
        self.ses = same_engine_sync
        self.ninstr = {e: 0 for e in self.ENG}
        self.dbufs = []

    def _alloc(self, name):
        self.nsem += 1
        return self.nc.alloc_semaphore(name=f"{name}_{self.nsem}")

    def _newsem(self, e):
        self.sem[e] = self._alloc("s" + e)
        self.cnt[e] = 0

    def _deps(self, eng, reads, writes):
        toks = []
        for b in reads:
            if b.lw is not None:
                toks.append(b.lw)
        for b in writes:
            if b.lw is not None:
                toks.append(b.lw)
            toks.extend(b.rd)
        wd = self.waited[eng]
        mx = {}
        for (sem, val, te) in toks:
            if te == eng and (eng == "pe" or not self.ses):
                continue
            k = id(sem)
            if wd.get(k, 0) >= val:
                continue
            if k not in mx or mx[k][1] < val:
                mx[k] = (sem, val)
        for k, (sem, val) in mx.items():
            wd[k] = val
        return list(mx.values())

    def _mark(self, tok, reads, writes):
        for b in writes:
            b.lw = tok
            b.rd = []
        wset = set(id(b) for b in writes)
        for b in reads:
            if id(b) not in wset:
                b.rd.append(tok)
                if len(b.rd) > 64:
                    last = {}
                    for t in b.rd:
                        k = id(t[0])
                        if k not in last or last[k][1] < t[1]:
                            last[k] = t
                    b.rd = list(last.values())

    def op(self, eng, fn, reads=(), writes=()):
        writes = [b for b in writes] + [b for b in reads if b.excl]
        reads = [b for b in reads if not b.excl]
        waits = self._deps(eng, reads, writes)
        if self.cnt[eng] >= self.SEM_LIMIT:
            self._newsem(eng)
        self.cnt[eng] += 1
        sem = self.sem[eng]
        tok = (sem, self.cnt[eng], eng)
        self.prog[eng].append((fn, waits, (sem, 1)))
        self._mark(tok, reads, writes)
        self.ninstr[eng] += 1

    def dma(self, q, out, in_, reads=(), writes=(), owner=None):
        reads = list(reads)
        writes = list(writes)
        if owner is None:
            owner = writes[0] if len(writes) else reads[0]
        if owner.dsem is None:
            owner.dsem = self._alloc("d")
            owner.dcount = 0
            self.dbufs.append(owner)
        waits = self._deps(q, reads, writes)
        owner.dcount += 16
        tok = (owner.dsem, owner.dcount, "dma")
        self.prog[q].append((lambda e, o=out, i=in_: e.dma_start(out=o, in_=i), waits, (owner.dsem, 16)))
        self._mark(tok, reads, writes)
        self.ninstr[q] += 1

    def barrier(self):
        toks = [(self.sem[e], self.cnt[e], e) for e in self.ENG if self.cnt[e] > 0]
        toks += [(b.dsem, b.dcount, "dma") for b in self.dbufs]
        for e in self.ENG:
            wd = self.waited[e]
            waits = []
            for sem, val, te in toks:
                if te == e:
                    continue
                if wd.get(id(sem), 0) >= val:
                    continue
                wd[id(sem)] = val
                waits.append((sem, val))
            self.prog[e].append((None, waits, None))

    def finish(self):
        waits = []
        for b in self.dbufs:
            waits.append((b.dsem, b.dcount))
        self.prog["sp"].append((None, waits, None))

    def emit(self):
        nc = self.nc
        with nc.Block() as block:
            def run(eng_name):
                def body(e):
                    for fn, waits, inc in self.prog[eng_name]:
                        for sem, val in waits:
                            e.wait_ge(sem, val)
                        if fn is not None:
                            ins = fn(e)
                            if inc is not None:
                                ins.then_inc(inc[0], inc[1])
                return body
            block.tensor(run("pe"))
            block.scalar(run("act"))
            block.vector(run("dve"))
            block.gpsimd(run("pool"))
            block.sync(run("sp"))


def _masks():
    idx = np.arange(128)
    i = idx[None, :]
    j = idx[:, None]
    c = {}
    for typ, sb in (("p", 128), ("s", 8)):
        same = (i // sb) == (j // sb)
        c["U" + typ] = ((j <= i) & same).astype(np.float32)
        c["SLT" + typ] = ((j > i) & same).astype(np.float32)
        c["MTi" + typ] = ((j <= i) & same).astype(np.float32)
        c["MTs" + typ] = ((j < i) & same).astype(np.float32)
        lv = []
        s = 1
        while 2 * s <= sb:
            m = ((i // (2 * s)) == (j // (2 * s))) & ((i % (2 * s)) >= s) & ((j % (2 * s)) < s)
            lv.append(m.astype(np.float32))
            s *= 2
        c["LV" + typ] = np.concatenate(lv, axis=1)
    c["ident"] = np.eye(128, dtype=np.float32)
    c["ones"] = np.ones((128, 128), np.float32)
    bs = (idx[:, None] // 8 == np.arange(16)[None, :]).astype(np.float32)
    c["bsel"] = bs
    c["colmask"] = np.tile(bs.T.reshape(1, 16 * 128), (128, 1))
    ns = np.ones((128, 128), np.float32)
    ns[:, ::8] = 0.0
    c["notstart_s"] = ns
    npm = np.ones((128, 512), np.float32)
    npm[:, ::128] = 0.0
    c["notstart_p"] = npm
    return c


CF_KEYS = ["ident", "ones", "Up", "SLTp", "Us", "SLTs", "bsel", "MTip", "MTsp", "MTis", "MTss",
           "notstart_s"]
CB_KEYS = ["ident", "MTip", "MTis", "LVp", "LVs", "colmask", "bsel"]


def _pack(keys, c):
    offs = {}
    o = 0
    for k in keys:
        offs[k] = (o, c[k].shape[1])
        o += c[k].shape[1]
    arr = np.concatenate([c[k] for k in keys], axis=1).astype(np.float32)
    return arr, offs


def _pp_layout():
    items = [("nw", 4 * 16), ("cwA", 2 * 24 * 4), ("normA", 2), ("normB", 4), ("nblr", 8),
             ("cwL", 2 * 16 * 4), ("cbL", 32), ("ba", 32), ("bx", 32), ("lam", 32)]
    offs = {}
    o = 0
    for k, w in items:
        offs[k] = o
        o += w
    return offs, o


def _chan(v, nch):
    v = np.asarray(v, np.float32)
    lead = v.shape[:-1]
    v = v.reshape(lead + (nch, 128))
    return np.moveaxis(v, -1, 0)


NG = 9
ST = 256


def st_range(g):
    return (g * ST, ST) if g < 8 else (2048, 128)


def build_program(stage=99):
    nc = bass.Bass("TRN2", target_bir_lowering=False)
    S = Sched(nc)
    cst = _masks()
    _, CFO = _pack(CF_KEYS, cst)
    _, CBO = _pack(CB_KEYS, cst)
    NCF = sum(w for _, w in CFO.values())
    NCB = sum(w for _, w in CBO.values())
    PPO, NPP = _pp_layout()
    uid = [0]

    def dram(name, shape, dt=F32, kind="ExternalInput"):
        return nc.dram_tensor(name, list(shape), dt, kind=kind).ap()

    xin = dram("xin", [NTOK, D])
    sd_in = dram("sd_in", [2, NSEQ, 8, 128, 128])
    sdc_in = dram("sdc_in", [2, 48, 3072])
    sg_in = dram("sg_in", [2, NSEQ, 4, 128, 256])
    sl_in = dram("sl_in", [2, NSEQ, 2048])
    slc_in = dram("slc_in", [2, 48, 2048])
    wab = dram("wab", [2, 16, 128, 16, 512])
    wsm = dram("wsm", [2, 128, 16, 32])
    wlru = dram("wlru", [2, 8, 128, 16, 512])
    wout = dram("wout", [4, 4, 128, 16, 512])
    lwa = dram("lwa", [2, 16, 128, 128])
    lwx = dram("lwx", [2, 16, 128, 128])
    wlr = dram("wlr", [2, 16, 512])
    pp_d = dram("pp", [128, NPP])
    br_d = dram("br", [1, 32])
    fn_d = dram("fnorm", [1, D])
    cf_d = dram("cf", [128, NCF])
    cb_d = dram("cb", [128, NCB])

    y_o = dram("y_o", [NTOK, D], kind="ExternalOutput")
    pd_o = dram("pd_o", [2, 8, 128, 128], kind="ExternalOutput")
    pdc_o = dram("pdc_o", [2, 3, 3072], kind="ExternalOutput")
    pg_o = dram("pg_o", [2, 4, 128, 256], kind="ExternalOutput")
    pl_o = dram("pl_o", [2, 2048], kind="ExternalOutput")
    plc_o = dram("plc_o", [2, 3, 2048], kind="ExternalOutput")
    sd_o = dram("sd_o", [2, NSEQ, 8, 128, 128], kind="ExternalOutput")
    sdc_o = dram("sdc_o", [2, 48, 3072], kind="ExternalOutput")
    sg_o = dram("sg_o", [2, NSEQ, 4, 128, 256], kind="ExternalOutput")
    sl_o = dram("sl_o", [2, NSEQ, 2048], kind="ExternalOutput")
    slc_o = dram("slc_o", [2, 48, 2048], kind="ExternalOutput")
    xres = dram("xres", [NTOK, D], kind="ExternalOutput")
    oscr = dram("oscr", [NT, 128, 16, 128], BF16, kind="ExternalOutput")
    b_xres = [Buf(f"xres{t}") for t in range(NT)]
    b_oscr = [Buf(f"oscr{g}") for g in range(NG)]

    def sb(name, shape, dt=F32):
        uid[0] += 1
        return T(nc.alloc_sbuf_tensor(f"{name}_{uid[0]}", list(shape), dt)[:], Buf(name))

    ARENA_F32 = 18688
    arena = nc.alloc_sbuf_tensor("arena", [128, ARENA_F32], F32)[:]
    ar_off = [0]

    def ar(shape, dt=F32, parts=128):
        n = 1
        for s_ in shape[1:]:
            n *= s_
        words = n if dt == F32 else (n + 1) // 2
        words = (words + 7) // 8 * 8
        o = ar_off[0]
        assert o + words <= ARENA_F32, ("arena overflow", o, words)
        ar_off[0] = o + words
        a = arena[0:parts, o:o + words]
        if dt != F32:
            a = a.bitcast(dt)
        a = a[:, 0:n]
        if len(shape) == 3:
            a = a.rearrange("p (a b) -> p a b", a=shape[1])
        return T(a, Buf("ar"))

    class Rot:
        def __init__(self, shape, dt, n, alloc=None, parts=128):
            if alloc is None:
                self.ts = [ar(shape, dt, parts) for _ in range(n)]
            else:
                self.ts = [alloc(f"rot{i}", shape, dt) for i in range(n)]
            self.i = 0

        def next(self):
            t = self.ts[self.i % len(self.ts)]
            self.i += 1
            return t

    psum_all = nc.alloc_psum_tensor("psum_all", [128, 4096], F32)[:]

    def bank(b):
        return psum_all[:, b * 512:(b + 1) * 512]

    bank_buf = [Buf(f"bank{b}", excl=True) for b in range(8)]

    class PRot:
        def __init__(self, aps):
            self.ts = [T(a, bank_buf[b]) for a, b in aps]
            self.i = 0

        def next(self):
            t = self.ts[self.i % len(self.ts)]
            self.i += 1
            return t

    ps_acc = PRot([(bank(0), 0), (bank(1), 1)])
    ps_q = PRot([(bank(b)[:, q * 128:(q + 1) * 128], b) for q in range(4) for b in (2, 3, 7)])
    ps_h = PRot([(bank(4)[:, 0:256], 4), (bank(4)[:, 256:512], 4)])
    ps_g = PRot([(bank(5), 5), (bank(6), 6)])
    ps_n = PRot([(psum_all[:, 4 * 512:6 * 512], 4), (psum_all[:, 6 * 512:8 * 512], 6)])

    def bufs(*ts_):
        return [t.buf for t in ts_ if isinstance(t, T)]

    def apof(x):
        return x.ap if isinstance(x, T) else x

    def mm(out, lhsT, rhs, start=True, stop=True):
        S.op("pe", lambda e, o=out.ap, l=lhsT.ap, r=rhs.ap, st=start, sp=stop: e.matmul(o, lhsT=l, rhs=r, start=st, stop=sp),
             reads=bufs(lhsT, rhs), writes=bufs(out))

    def tr(out, in_, ident):
        S.op("pe", lambda e, o=out.ap, i=in_.ap, d=ident.ap: e.transpose(out=o, in_=i, identity=d),
             reads=bufs(in_, ident), writes=bufs(out))

    def tt(eng, out, in0, in1, op):
        S.op(eng, lambda e, o=out.ap, a=in0.ap, b=in1.ap, p=op: e.tensor_tensor(out=o, in0=a, in1=b, op=p),
             reads=bufs(in0, in1), writes=bufs(out))

    def ts(eng, out, in0, s1, op0, s2=None, op1=None):
        kw = dict(out=out.ap, in0=in0.ap, scalar1=apof(s1), scalar2=apof(s2), op0=op0)
        if op1 is not None:
            kw["op1"] = op1
        S.op(eng, lambda e, kw=kw: e.tensor_scalar(**kw), reads=bufs(in0, s1, s2), writes=bufs(out))

    def stt(out, in0, scalar, in1, op0, op1):
        S.op("dve", lambda e, o=out.ap, a=in0.ap, s=apof(scalar), b=in1.ap, p0=op0, p1=op1:
             e.scalar_tensor_tensor(out=o, in0=a, scalar=s, in1=b, op0=p0, op1=p1),
             reads=bufs(in0, scalar, in1), writes=bufs(out))

    def act(out, in_, func, scale=None, bias=None, accum=None):
        kw = dict(out=out.ap, in_=in_.ap, func=func)
        if scale is not None:
            kw["scale"] = apof(scale)
        if bias is not None:
            kw["bias"] = apof(bias)
        if accum is not None:
            kw["accum_out"] = accum.ap
        S.op("act", lambda e, kw=kw: e.activation(**kw), reads=bufs(in_, scale, bias),
             writes=bufs(out) + (bufs(accum) if accum is not None else []))

    def cp(eng, out, in_):
        if eng == "act":
            act(out, in_, AF.Copy)
        else:
            S.op(eng, lambda e, o=out.ap, i=in_.ap: e.tensor_copy(out=o, in_=i), reads=bufs(in_), writes=bufs(out))

    def memset(eng, out, val):
        S.op(eng, lambda e, o=out.ap, v=val: e.memset(o, v), writes=bufs(out))

    def recip(out, in_):
        S.op("dve", lambda e, o=out.ap, i=in_.ap: e.reciprocal(out=o, in_=i), reads=bufs(in_), writes=bufs(out))

    def scan(out, d0, d1, init):
        S.op("dve", lambda e, o=out.ap, a=d0.ap, b=d1.ap, i=apof(init):
             e.tensor_tensor_scan(out=o, data0=a, data1=b, initial=i, op0=ALU.mult, op1=ALU.add),
             reads=bufs(d0, d1, init), writes=bufs(out))

    def load(q, out_t, in_ap, extra_reads=()):
        S.dma(q, out_t.ap, in_ap, reads=list(extra_reads), writes=[out_t.buf])

    def store(q, out_ap, in_t, dwrites=()):
        S.dma(q, out_ap, in_t.ap, reads=[in_t.buf], writes=list(dwrites), owner=in_t.buf)

    def store_o(kc, g, ob_t, src_ap):
        o0, n = st_range(g)
        t0, nt = o0 // 128, n // 128
        S.dma("pool", oscr[t0:t0 + nt, :, kc, :].rearrange("t p c -> p t c"), src_ap.rearrange("p (t c) -> p t c", c=128),
              reads=[ob_t.buf], writes=[b_oscr[g]], owner=ob_t.buf)

    CF = sb("CF", [128, NCF])
    CB = sb("CB", [128, NCB], BF16)
    PP = sb("PP", [128, NPP])
    BR = sb("BR", [128, 32])
    load("sp", CF, cf_d)
    for c0 in range(0, NCB, 1024):
        c1 = min(NCB, c0 + 1024)
        S.dma("pool", CB.ap[:, c0:c1], cb_d[:, c0:c1], writes=[CB.buf])
    load("sp", PP, pp_d)
    load("sp", BR, br_d.partition_broadcast(128))

    def cf(k):
        o, w = CFO[k]
        return CF[:, o:o + w]

    def cb(k):
        o, w = CBO[k]
        return CB[:, o:o + w]

    identf = cf("ident")
    identb = cb("ident")
    onesf = cf("ones")
    eps_t = sb("eps", [128, 1])
    memset("pool", eps_t, EPS)
    one_t = sb("one", [128, 1])
    memset("pool", one_t, 1.0)

    def pp(k, off=0, w=1):
        o = PPO[k] + off
        return PP[:, o:o + w]

    xnT_all = nc.alloc_sbuf_tensor("xnT", [128, 16, NTOK], BF16)[:]
    b_xn = [Buf(f"xn{g}") for g in range(NG)]

    def xn_st(g):
        o, n = st_range(g)
        return T(xnT_all[:, :, o:o + n], b_xn[g])

    def xn_tile(t):
        return T(xnT_all[:, :, t * 128:(t + 1) * 128], b_xn[min(t // 2, 8)])

    Wrot = Rot([128, 16, 512], BF16, 2, alloc=sb)

    def load_w(src_ap, ncols=512):
        w = Wrot.next()
        for q4 in range(4):
            S.dma("pool", w.ap[:, q4 * 4:(q4 + 1) * 4, 0:ncols], src_ap[:, q4 * 4:(q4 + 1) * 4, 0:ncols], writes=[w.buf])
        return w

    Sf = sb("Sf", [128, 256])
    Sb = sb("Sb", [128, 256], BF16)
    bas = sb("bas", [128, NT, 16])
    beta_all = sb("beta", [128, NT, 8])
    nbeta_all = sb("nbeta", [128, NT, 8])
    g_all = sb("gall", [128, NT, 8])
    eG_all = sb("eG", [128, NT, 8])
    eGr_all = sb("eGr", [128, NT, 8])
    eGl_p = sb("eGlp", [128, 16, 8])
    eGl_s = sb("eGls", [128, 8, 16])
    gsel = sb("gsel", [128, 8, 16])
    nea = sb("nea", [128, 8])
    lrT = sb("lrT", [16, NTOK], BF16)
    wlr_b = sb("wlrb", [16, 512], BF16)
    wsm_b = sb("wsmb", [128, 16, 32], BF16)
    c8 = sb("c8", [128, 32])
    hcar2 = sb("hcar", [128, 4])
    hl_col = sb("hlcol", [128, 16])

    ar_off[0] = 0
    ext_b = {k: ar([128, ST + 4]) for k in "qkv"}
    Fp = Rot([128, ST], F32, 12)
    Hp = Rot([128, ST], BF16, 12)
    sz_rot = Rot([128, 2, ST], BF16, 4)
    ob_rot = Rot([128, 2, ST], BF16, 3)
    hist_rot = Rot([48, 128], F32, 2, parts=48)
    stg_rot = Rot([48, 128], F32, 2, parts=48)
    f_rot = Rot([128, 128], F32, 7)
    b_rot = Rot([128, 128], BF16, 4)
    m_rot = Rot([128, 4, 128], BF16, 4)
    sm_rot = Rot([128, 8], F32, 4)
    wg_rot = Rot([128, 128], BF16, 8)
    h0_rot = Rot([16, 128], F32, 2, parts=16)
    ab_only_start = ar_off[0]

    class Lane:
        pass

    lanes = []
    for _ln in range(2):
        L_ = Lane()
        L_.f = [ar([128, 128]) for _ in range(5)]
        L_.atl = ar([128, 7, 128], BF16)
        L_.T = [ar([128, 128], BF16) for _ in range(2)]
        L_.Tt = [ar([128, 128], BF16) for _ in range(2)]
        L_.Yb = [ar([128, 128], BF16) for _ in range(2)]
        L_.kg = ar([128, 128], BF16)
        L_.vtok = ar([128, 128], BF16)
        L_.hand = [dict(qkT=ar([128, 128], BF16), kdec=ar([128, 128], BF16), xkT=ar([128, 128], BF16),
                        bXv=ar([128, 128])) for _ in range(2)]
        lanes.append(L_)
    r_o1 = Rot([128, 128], F32, 2)
    r_b = Rot([128, 128], BF16, 6)
    ob16_rot = Rot([128, 256], BF16, 4)
    vt_rot = Rot([128, 256], BF16, 2)
    Sbs = Rot([128, 16, 128], BF16, 1)
    Sfs = Rot([128, 4, 256], F32, 1)
    mixer_top = ar_off[0]
    ar_off[0] = ab_only_start
    FpL = [ar([128, ST]) for _ in range(12)]
    ext_l = [ar([128, ST + 4]) for _ in range(2)]
    assert ar_off[0] <= mixer_top
    ar_off[0] = 0
    XR = Rot([128, D], F32, 2)
    ot_rot = Rot([128, 16, 128], BF16, 2)
    ssq_rot = Rot([128, 8], F32, 4)
    fnw_t = ar([128, D])
    junkB = ar([128, D], BF16)

    def typ_of(t):
        return "s" if t == 16 else "p"

    def norm_tile(t, xt, layer_next):
        sq = ssq_rot.next()
        act(junkB, xt, AF.Square, accum=sq[:, 0:1])
        act(sq[:, 1:2], sq[:, 0:1], AF.Sqrt, scale=1.0 / D, bias=eps_t)
        recip(sq[:, 1:2], sq[:, 1:2])
        if layer_next < 4:
            act(xt, xt, AF.Identity, scale=sq[:, 1:2])
            nw = pp("nw", layer_next * 16, 16)
            dst = xn_tile(t)
            for hf in range(2):
                pn = ps_n.next()
                for k8 in range(8):
                    kc = hf * 8 + k8
                    tr(pn[:, k8 * 128:(k8 + 1) * 128], xt[:, kc * 128:(kc + 1) * 128], identf)
                tt("dve", dst[:, hf * 8:(hf + 1) * 8, :], pn.re("p (k t) -> p k t", k=8),
                   nw[:, hf * 8:(hf + 1) * 8].re("p (k o) -> p k o", o=1).bc([128, 8, 128]), ALU.mult)
        else:
            stt(xt, xt, sq[:, 1:2], fnw_t, ALU.mult, ALU.mult)
            store("pool", y_o[t * 128:(t + 1) * 128, :], xt)

    def phase_O(l, last):
        src = xin if l == 0 else xres
        steps = [(c, t) for c in range(4) for t in range(NT)]
        wgs = {0: load_w(wout[l, 0])}

        def prefetch(idx):
            c, t = steps[idx]
            rows = slice(t * 128, (t + 1) * 128)
            cs_ = slice(c * 512, (c + 1) * 512)
            ot = ot_rot.next()
            S.dma("sp", ot.ap, oscr[t], reads=[b_oscr[min(t // 2, 8)]], writes=[ot.buf])
            xt = XR.next()
            if c < 3:
                S.dma("sp", xt.ap[:, cs_], src[rows, cs_], reads=[b_xres[t]], writes=[xt.buf])
            else:
                S.dma("sp", xt.ap[:, 0:1536], xres[rows, 0:1536], reads=[b_xres[t]], writes=[xt.buf])
                S.dma("sp", xt.ap[:, 1536:2048], src[rows, 1536:2048], reads=[b_xres[t]], writes=[xt.buf])
            return ot, xt

        def compute(idx, ot, xt):
            c, t = steps[idx]
            wg = wgs[c]
            rows = slice(t * 128, (t + 1) * 128)
            cs_ = slice(c * 512, (c + 1) * 512)
            pa = ps_acc.next()
            for ec in range(16):
                mm(pa, ot[:, ec, :], wg[:, ec, :], start=(ec == 0), stop=(ec == 15))
            if t == 0 and c + 1 < 4:
                wgs[c + 1] = load_w(wout[l, c + 1])
            tt("dve", xt[:, cs_], pa, xt[:, cs_], ALU.add)
            if c < 3 or not last:
                S.dma("pool", xres[rows, cs_], xt.ap[:, cs_], reads=[xt.buf], writes=[b_xres[t]], owner=xt.buf)
            if c == 3:
                norm_tile(t, xt, l + 1)

        pend = prefetch(0)
        for idx in range(len(steps)):
            cur = pend
            if idx + 1 < len(steps):
                pend = prefetch(idx + 1)
            compute(idx, *cur)

    def ab_layer(j, l):
        load("pool", wsm_b, wsm[j])
        load("pool", wlr_b, wlr[j])
        for t in range(NT):
            pa = ps_q.next()
            xt_ = xn_tile(t)
            for kc in range(16):
                mm(pa[:, 0:16], xt_[:, kc, :], wsm_b[:, kc, 0:16], start=(kc == 0), stop=(kc == 15))
            cp("act", bas[:, t, :], pa[:, 0:16])
        for g in range(NG):
            o0, n = st_range(g)
            pa = ps_acc.next()
            xg = xn_st(g)
            for kc in range(16):
                mm(pa[0:16, 0:n], wsm_b[:, kc, 16:32], xg[:, kc, :], start=(kc == 0), stop=(kc == 15))
            cp("act", lrT[:, o0:o0 + n], pa[0:16, 0:n])
        act(beta_all, bas[:, :, 0:8], AF.Sigmoid)
        ts("pool", nbeta_all, beta_all, -1.0, ALU.mult)
        act(nea, BR[:, j * 8:(j + 1) * 8], AF.Exp)
        ts("pool", nea, nea, -1.0, ALU.mult)
        tt("dve", g_all, bas[:, :, 8:16], BR[:, 16 + j * 8:16 + (j + 1) * 8].re("p (o h) -> p o h", o=1).bc([128, NT, 8]), ALU.add)
        act(g_all, g_all, AF.Exp)
        act(g_all, g_all, AF.Ln, bias=one_t)
        tt("dve", g_all, g_all, nea.re("p (o h) -> p o h", o=1).bc([128, NT, 8]), ALU.mult)
        for t in range(NT):
            ty = typ_of(t)
            pa = ps_q.next()
            mm(pa[:, 0:8], cf("U" + ty), g_all[:, t, :])
            act(eG_all[:, t, :], pa[:, 0:8], AF.Exp)
            pa = ps_q.next()
            mm(pa[:, 0:8], cf("SLT" + ty), g_all[:, t, :])
            act(eGr_all[:, t, :], pa[:, 0:8], AF.Exp)
            pa = ps_q.next()
            if ty == "p":
                mm(pa[:, 0:8], onesf, g_all[:, t, :])
                act(eGl_p[:, t, :], pa[:, 0:8], AF.Exp)
            else:
                tt("pool", gsel, g_all[:, t, :].re("p (h o) -> p h o", o=1).bc([128, 8, 16]),
                   cf("bsel").re("p (o s) -> p o s", o=1).bc([128, 8, 16]), ALU.mult)
                mm(pa, onesf, gsel.re("p h s -> p (h s)"))
                act(eGl_s.re("p h s -> p (h s)"), pa, AF.Exp)
        if stage >= 2:
            Wn = load_w(wab[j, 0])
            for h in range(8):
                Wc = Wn
                if h + 1 < 8:
                    Wn = load_w(wab[j, h + 1])
                gdn_head(j, h, Wc)
        if stage >= 3:
            for hb in range(4):
                gla_head(j, hb)

    def conv_feature(wfn, ext, g, bias=None):
        o0, n = st_range(g)
        co = Fp.next()
        if g < 8:
            src3 = lambda tap: ext[:, tap:tap + n]
            dst = co[:, 0:n]
        else:
            e3 = ext[:, 0:176].re("p (s c) -> p s c", c=11)
            src3 = lambda tap: e3[:, :, tap:tap + 8]
            dst = co[:, 0:128].re("p (s c) -> p s c", c=8)
        if bias is None:
            ts("dve", dst, src3(0), wfn(0), ALU.mult)
        else:
            ts("dve", dst, src3(0), wfn(0), ALU.mult, bias, ALU.add)
        for tap in range(1, 4):
            stt(dst, src3(tap), wfn(tap), dst, ALU.mult, ALU.add)
        return co

    def proj_fm(W, c0, g, dst_fn):
        o0, n = st_range(g)
        pa = ps_acc.next()
        xg = xn_st(g)
        for kc in range(16):
            mm(pa[:, 0:n], W[:, kc, c0:c0 + 128], xg[:, kc, :], start=(kc == 0), stop=(kc == 15))
        dst_fn(pa[:, 0:n])

    def fill_ext(hist_src, ext, g, psum_src):
        o0, n = st_range(g)
        if g < 8:
            if g == 0:
                memset("pool", ext[:, 0:3], 0.0)
            else:
                cp("pool", ext[:, 0:3], ext[:, ST:ST + 3])
            cp("act", ext[:, 3:3 + n], psum_src)
        else:
            e3 = ext[:, 0:176].re("p (s c) -> p s c", c=11)
            hs = hist_rot.next()
            load("sp", hs, hist_src)
            pq = ps_q.next()
            tr(pq[:, 0:48], hs, identf[0:48, 0:48])
            cp("dve", e3[:, :, 0:3], pq[:, 0:48].re("p (s c) -> p s c", c=3))
            cp("act", e3[:, :, 3:11], psum_src.re("p (s c) -> p s c", c=8))

    def save_conv_state(dst_p, dst_s, ext, g):
        if g == 7:
            pq = ps_q.next()
            tr(pq[0:3, :], ext[:, ST:ST + 3], identf)
            sg_ = stg_rot.next()
            cp("dve", sg_[0:3, :], pq[0:3, :])
            store("sp", dst_p, sg_[0:3, :])
        elif g == 8:
            e3 = ext[:, 0:176].re("p (s c) -> p s c", c=11)
            tmp = f_rot.next()
            cp("pool", tmp[:, 0:48].re("p (s c) -> p s c", c=3), e3[:, :, 8:11])
            pq = ps_q.next()
            tr(pq[0:48, :], tmp[:, 0:48], identf)
            sg_ = stg_rot.next()
            cp("dve", sg_, pq[0:48, :])
            store("sp", dst_s, sg_)

    def l2norm_fm(src, n):
        sq = Fp.next()
        act(sq[:, 0:n], src[:, 0:n], AF.Square)
        pa = ps_acc.next()
        mm(pa[:, 0:n], onesf, sq[:, 0:n])
        rs = Fp.next()
        act(rs[:, 0:n], pa[:, 0:n], AF.Sqrt, bias=eps_t)
        recip(rs[:, 0:n], rs[:, 0:n])
        return rs

    def interleave(gens):
        gens = [g_ for g_ in gens if g_ is not None]
        while gens:
            for g_ in list(gens):
                try:
                    next(g_)
                except StopIteration:
                    gens.remove(g_)

    def drain(gen):
        for _ in gen:
            pass

    def gdn_A(j, h, g, W, res):
        o0, n = st_range(g)
        for ci, k in enumerate("qkv"):
            chunk = ci * 8 + h
            csl = slice(chunk * 128, (chunk + 1) * 128)
            proj_fm(W, ci * 128, g, lambda p, k=k, csl=csl: fill_ext(sdc_in[j, :, csl], ext_b[k], g, p))
            yield
            save_conv_state(pdc_o[j, :, csl], sdc_o[j, :, csl], ext_b[k], g)
        sz = sz_rot.next()
        proj_fm(W, 384, g, lambda p: act(sz[:, 0, 0:n], p, AF.Silu))
        yield
        acts = {}
        for ci, k in enumerate("qkv"):
            chunk = ci * 8 + h
            co = conv_feature(lambda tap, chunk=chunk: pp("cwA", (j * 24 + chunk) * 4 + tap, 1), ext_b[k], g)
            a_ = Hp.next() if k == "v" else Fp.next()
            act(a_[:, 0:n], co[:, 0:n], AF.Silu)
            acts[k] = a_
            yield
        rsq = l2norm_fm(acts["q"], n)
        qT = Hp.next()
        stt(qT[:, 0:n], acts["q"][:, 0:n], 128.0 ** -0.5, rsq[:, 0:n], ALU.mult, ALU.mult)
        yield
        rsk = l2norm_fm(acts["k"], n)
        kT = Hp.next()
        tt("pool", kT[:, 0:n], acts["k"][:, 0:n], rsk[:, 0:n], ALU.mult)
        res[g] = dict(qT=qT, kT=kT, av=acts["v"], sz=sz)
        yield

    def gdn_prep(j, h, t, ti, A, lane, par):
        ty = typ_of(t)
        L = 7 if ty == "p" else 3
        cs = slice(ti * 128, (ti + 1) * 128)
        qT, kT, av = A["qT"], A["kT"], A["av"]
        H = lane.hand[par]
        bet = beta_all[:, t, h:h + 1]
        gU, E, Es, Ei, AT = lane.f
        ts("pool", gU, cf("U" + ty), g_all[:, t, h:h + 1], ALU.mult)
        pD = ps_q.next()
        mm(pD, cf("SLT" + ty), gU)
        act(E, pD, AF.Exp)
        pk = ps_q.next()
        pkb = pk.bitcast(BF16)[:, 0:128]
        tr(pkb, kT[:, cs], identb)
        act(lane.kg, pkb, AF.Identity, scale=eG_all[:, t, h:h + 1])
        ts("dve", H["kdec"], pkb, eGr_all[:, t, h:h + 1], ALU.mult)
        yield
        tt("pool", Es, E, cf("MTs" + ty), ALU.mult)
        tt("pool", Ei, E, cf("MTi" + ty), ALU.mult)
        pK = ps_q.next()
        mm(pK, kT[:, cs], kT[:, cs])
        stt(AT, pK, bet, Es, ALU.mult, ALU.mult)
        pv = ps_q.next()
        pvb = pv.bitcast(BF16)[:, 0:128]
        tr(pvb, av[:, cs], identb)
        cp("act", lane.vtok, pvb)
        yield
        atl = lane.atl
        tt("pool", atl[:, 0:L, :], AT.re("p (o i) -> p o i", o=1).bc([128, L, 128]),
           cb("LV" + ty).re("p (l i) -> p l i", l=L), ALU.mult)
        Tt = lane.Tt[0]
        Tm = lane.T[0]
        tt("pool", Tt, identb, atl[:, 0, :], ALU.subtract)
        pY = ps_q.next()
        mm(pY, atl[:, 0, :], identb)
        tt("dve", Tm, identb, pY, ALU.subtract)
        pQ = ps_q.next()
        mm(pQ, kT[:, cs], qT[:, cs])
        tt("dve", H["qkT"], pQ, Ei, ALU.mult)
        yield
        for lv in range(1, L):
            pY = ps_q.next()
            mm(pY, atl[:, lv, :], Tm)
            Yb = lane.Yb[lv % 2]
            cp("act", Yb, pY)
            yield
            pZt = ps_q.next()
            mm(pZt, Yb, Tt)
            Ttn = lane.Tt[lv % 2]
            tt("dve", Ttn, Tt, pZt, ALU.subtract)
            if lv < L - 1:
                pZ = ps_q.next()
                mm(pZ, Tt, Yb)
                Tn = lane.T[lv % 2]
                tt("dve", Tn, Tm, pZ, ALU.subtract)
                Tm = Tn
            Tt = Ttn
            yield
        pX = ps_q.next()
        mm(pX, Tt, lane.vtok)
        act(H["bXv"], pX, AF.Identity, scale=bet)
        pXk = ps_q.next()
        mm(pXk, lane.kg, Tt)
        cp("dve", H["xkT"], pXk)
        yield

    def gdn_R(j, h, g, A, hands):
        o0, n = st_range(g)
        qT, sz = A["qT"], A["sz"]
        ob = ob_rot.next()
        for ti in range(n // 128):
            t = o0 // 128 + ti
            ty = typ_of(t)
            H = hands[ti]
            cs = slice(ti * 128, (ti + 1) * 128)
            nbet = nbeta_all[:, t, h:h + 1]
            qTt = qT[:, cs]
            qkT, kdec, xkT, bXv = H["qkT"], H["kdec"], H["xkT"], H["bXv"]
            o1 = r_o1.next()
            vnew = r_b.next()
            if ty == "p":
                p2 = ps_g.next()
                mm(p2[:, 0:128], xkT, Sb[:, 0:128])
                stt(vnew, p2[:, 0:128], nbet, bXv, ALU.mult, ALU.add)
                pS = ps_q.next()
                mm(pS, kdec, vnew)
                p3 = ps_g.next()
                mm(p3[:, 0:128], qTt, Sb[:, 0:128])
                stt(Sf[:, 0:128], Sf[:, 0:128], eGl_p[:, t, h:h + 1], pS, ALU.mult, ALU.add)
                act(o1, p3[:, 0:128], AF.Identity, scale=eG_all[:, t, h:h + 1])
                cp("act", Sb[:, 0:128], Sf[:, 0:128])
                yield
            else:
                sbs = Sbs.next()
                for q4 in range(4):
                    S.dma("pool", sbs.ap[:, q4 * 4:(q4 + 1) * 4, :], sd_in[j, q4 * 4:(q4 + 1) * 4, h].rearrange("s k v -> k s v"), writes=[sbs.buf])
                p2 = ps_g.next()
                p3 = ps_g.next()
                for gq in range(4):
                    mk = m_rot.next()
                    mq = m_rot.next()
                    cm = cb("colmask")[:, gq * 512:(gq + 1) * 512].re("p (s t) -> p s t", s=4)
                    tt("pool", mk, xkT.re("p (o t) -> p o t", o=1).bc([128, 4, 128]), cm, ALU.mult)
                    tt("pool", mq, qTt.re("p (o t) -> p o t", o=1).bc([128, 4, 128]), cm, ALU.mult)
                    for s4 in range(4):
                        s_ = gq * 4 + s4
                        mm(p2[:, 0:128], mk[:, s4, :], sbs[:, s_, :], start=(s_ == 0), stop=(s_ == 15))
                    for s4 in range(4):
                        s_ = gq * 4 + s4
                        mm(p3[:, 0:128], mq[:, s4, :], sbs[:, s_, :], start=(s_ == 0), stop=(s_ == 15))
                    yield
                stt(vnew, p2[:, 0:128], nbet, bXv, ALU.mult, ALU.add)
                act(o1, p3[:, 0:128], AF.Identity, scale=eG_all[:, t, h:h + 1])
                for gq in range(4):
                    sf = Sfs.next()
                    S.dma("sp", sf.ap[:, :, 0:128], sd_in[j, gq * 4:(gq + 1) * 4, h].rearrange("s k v -> k s v"), writes=[sf.buf])
                    mkd = m_rot.next()
                    tt("pool", mkd, kdec.re("p (o k) -> p o k", o=1).bc([128, 4, 128]),
                       cb("bsel")[:, gq * 4:(gq + 1) * 4].re("p (s o) -> p s o", o=1).bc([128, 4, 128]), ALU.mult)
                    for s4 in range(4):
                        s_ = gq * 4 + s4
                        pS = ps_q.next()
                        mm(pS, mkd[:, s4, :], vnew)
                        stt(sf[:, s4, 0:128], sf[:, s4, 0:128], eGl_s[:, h, s_:s_ + 1], pS, ALU.mult, ALU.add)
                    S.dma("sp", sd_o[j, gq * 4:(gq + 1) * 4, h].rearrange("s k v -> k s v"), sf.ap[:, :, 0:128],
                          reads=[sf.buf], writes=[], owner=sf.buf)
                    yield
            p4 = ps_q.next()
            mm(p4, qkT, vnew)
            tt("dve", o1, o1, p4, ALU.add)
            yield
            sm = sm_rot.next()
            jk = r_b.next()
            act(jk, o1, AF.Square, accum=sm[:, 0:1])
            act(sm[:, 1:2], sm[:, 0:1], AF.Sqrt, scale=1.0 / 128, bias=eps_t)
            recip(sm[:, 1:2], sm[:, 1:2])
            osb = r_b.next()
            act(osb, o1, AF.Identity, scale=sm[:, 1:2])
            yield
            po = ps_q.next()
            pob = po.bitcast(BF16)[:, 0:128]
            tr(pob, osb, identb)
            stt(ob[:, 0, cs], pob, pp("normA", j, 1), sz[:, 0, cs], ALU.mult, ALU.mult)
            yield
        store_o(h, g, ob, ob.ap[:, 0, 0:n])
        if g == 7:
            store("sp", pd_o[j, h], Sf[:, 0:128])
        yield

    def gdn_head(j, h, W):
        memset("pool", Sf[:, 0:128], 0.0)
        memset("pool", Sb[:, 0:128], 0.0)
        res = {}
        drain(gdn_A(j, h, 0, W, res))
        hands_prev = None
        for g in range(NG):
            o0, n = st_range(g)
            streams = []
            if g + 1 < NG:
                streams.append(gdn_A(j, h, g + 1, W, res))
            hands = []
            for ti in range(n // 128):
                t = o0 // 128 + ti
                streams.append(gdn_prep(j, h, t, ti, res[g], lanes[ti], g % 2))
                hands.append(lanes[ti].hand[g % 2])
            if g >= 1:
                streams.append(gdn_R(j, h, g - 1, res[g - 1], hands_prev))
            interleave(streams)
            hands_prev = hands
        drain(gdn_R(j, h, NG - 1, res[NG - 1], hands_prev))

    def gla_A(j, hb, g, WA, WB, res):
        o0, n = st_range(g)
        pa = ps_acc.next()
        mm(pa[:, 0:n], wlr_b[:, hb * 128:(hb + 1) * 128], lrT[:, o0:o0 + n])
        e1 = Fp.next()
        act(e1[:, 0:n], pa[:, 0:n], AF.Exp, scale=-1.0, bias=pp("nblr", j * 4 + hb, 1))
        act(e1[:, 0:n], e1[:, 0:n], AF.Ln, bias=one_t)
        yield
        csum = Fp.next()
        if g < 8:
            for ti in range(2):
                scan(csum[:, ti * 128:(ti + 1) * 128], onesf, e1[:, ti * 128:(ti + 1) * 128], 0.0)
        else:
            scan(csum[:, 0:128], cf("notstart_s"), e1[:, 0:128], 0.0)
        EQ = Fp.next()
        EK = Fp.next()
        act(EQ[:, 0:n], csum[:, 0:n], AF.Exp, scale=-1.0 / 16)
        act(EK[:, 0:n], csum[:, 0:n], AF.Exp, scale=1.0 / 16)
        yield
        qgT = Hp.next()
        proj_fm(WA, 0, g, lambda p: stt(qgT[:, 0:n], p, 128.0 ** -0.5, EQ[:, 0:n], ALU.mult, ALU.mult))
        yield
        kg32 = Fp.next()
        proj_fm(WA, 128, g, lambda p: tt("dve", kg32[:, 0:n], p, EK[:, 0:n], ALU.mult))
        yield
        kgT = Hp.next()
        cp("pool", kgT[:, 0:n], kg32[:, 0:n])
        kdT = Hp.next()
        sbk = 128 if g < 8 else 8
        nb = n // sbk
        EQl = EQ[:, 0:n].re("p (b c) -> p b c", c=sbk)[:, :, sbk - 1:sbk]
        tt("pool", kdT[:, 0:n].re("p (b c) -> p b c", c=sbk), kg32[:, 0:n].re("p (b c) -> p b c", c=sbk),
           EQl.bc([128, nb, sbk]), ALU.mult)
        sz = sz_rot.next()
        proj_fm(WB, 0, g, lambda p: act(sz[:, 0, 0:n], p, AF.Silu))
        yield
        proj_fm(WB, 128, g, lambda p: act(sz[:, 1, 0:n], p, AF.Silu))
        res[g] = dict(qgT=qgT, kgT=kgT, kdT=kdT, EQ=EQ, sz=sz)
        yield

    def gla_tiles(j, hb, g, WA, A):
        o0, n = st_range(g)
        qgT, kgT, kdT, EQ, sz = A["qgT"], A["kgT"], A["kdT"], A["EQ"], A["sz"]
        ob = ob_rot.next()
        for ti in range(n // 128):
            t = o0 // 128 + ti
            ty = typ_of(t)
            cs = slice(ti * 128, (ti + 1) * 128)
            pv = ps_h.next()
            xt_ = xn_tile(t)
            for kc in range(16):
                mm(pv, xt_[:, kc, :], WA[:, kc, 256:512], start=(kc == 0), stop=(kc == 15))
            vtok = vt_rot.next()
            cp("act", vtok, pv)
            yield
            pA = ps_q.next()
            mm(pA, kgT[:, cs], qgT[:, cs])
            ATb = b_rot.next()
            tt("dve", ATb, pA, cf("MTi" + ty), ALU.mult)
            yield
            pk = ps_q.next()
            pkb = pk.bitcast(BF16)[:, 0:128]
            tr(pkb, kdT[:, cs], identb)
            kdec = b_rot.next()
            cp("act", kdec, pkb)
            yield
            P = ps_g.next()
            if ty == "p":
                mm(P[:, 0:256], qgT[:, cs], Sb, start=True, stop=False)
                mm(P[:, 0:256], ATb, vtok, start=False, stop=True)
                pS = ps_h.next()
                mm(pS, kdec, vtok)
                stt(Sf, Sf, EQ[:, ti * 128 + 127:ti * 128 + 128], pS, ALU.mult, ALU.add)
                cp("act", Sb, Sf)
                yield
            else:
                first = True
                for gq in range(4):
                    sbs = Sbs.next()
                    sview = sbs.ap.rearrange("p s v -> p (s v)")[:, 0:1024].rearrange("p (s v) -> p s v", s=4)
                    S.dma("pool", sview, sg_in[j, gq * 4:(gq + 1) * 4, hb].rearrange("s k v -> k s v"), writes=[sbs.buf])
                    sv = T(sview, sbs.buf)
                    mq = m_rot.next()
                    cm = cb("colmask")[:, gq * 512:(gq + 1) * 512].re("p (s t) -> p s t", s=4)
                    tt("pool", mq, qgT[:, cs].re("p (o t) -> p o t", o=1).bc([128, 4, 128]), cm, ALU.mult)
                    for s4 in range(4):
                        mm(P[:, 0:256], mq[:, s4, :], sv[:, s4, :], start=first, stop=False)
                        first = False
                    yield
                mm(P[:, 0:256], ATb, vtok, start=False, stop=True)
                for gq in range(4):
                    sf = Sfs.next()
                    S.dma("sp", sf.ap, sg_in[j, gq * 4:(gq + 1) * 4, hb].rearrange("s k v -> k s v"), writes=[sf.buf])
                    mkd = m_rot.next()
                    tt("pool", mkd, kdec.re("p (o k) -> p o k", o=1).bc([128, 4, 128]),
                       cb("bsel")[:, gq * 4:(gq + 1) * 4].re("p (s o) -> p s o", o=1).bc([128, 4, 128]), ALU.mult)
                    for s4 in range(4):
                        s_ = gq * 4 + s4
                        pS = ps_h.next()
                        mm(pS, mkd[:, s4, :], vtok)
                        stt(sf[:, s4, :], sf[:, s4, :], EQ[:, s_ * 8 + 7:s_ * 8 + 8], pS, ALU.mult, ALU.add)
                    S.dma("sp", sg_o[j, gq * 4:(gq + 1) * 4, hb].rearrange("s k v -> k s v"), sf.ap,
                          reads=[sf.buf], writes=[], owner=sf.buf)
                    yield
            sm = sm_rot.next()
            jk = ob16_rot.next()
            act(jk, P[:, 0:256], AF.Square, accum=sm[:, 0:1])
            act(sm[:, 1:2], sm[:, 0:1], AF.Sqrt, scale=1.0 / 256, bias=eps_t)
            recip(sm[:, 1:2], sm[:, 1:2])
            osb = ob16_rot.next()
            act(osb, P[:, 0:256], AF.Identity, scale=sm[:, 1:2])
            yield
            for c in range(2):
                po = ps_q.next()
                pob = po.bitcast(BF16)[:, 0:128]
                tr(pob, osb[:, c * 128:(c + 1) * 128], identb)
                stt(ob[:, c, cs], pob, pp("normB", j * 2 + c, 1), sz[:, c, cs], ALU.mult, ALU.mult)
                yield
        for c in range(2):
            store_o(8 + 2 * hb + c, g, ob, ob.ap[:, c, 0:n])
        if g == 7:
            store("sp", pg_o[j, hb], Sf)
        yield

    def gla_head(j, hb):
        WA = load_w(wab[j, 8 + 2 * hb])
        WB = load_w(wab[j, 9 + 2 * hb], ncols=256)
        memset("pool", Sf, 0.0)
        memset("pool", Sb, 0.0)
        res = {}
        drain(gla_A(j, hb, 0, WA, WB, res))
        for g in range(NG):
            streams = [gla_tiles(j, hb, g, WA, res[g])]
            if g + 1 < NG:
                streams.append(gla_A(j, hb, g + 1, WA, WB, res))
            interleave(streams)

    def lru_layer(j, l):
        lam = pp("lam", j * 16, 16)
        act(c8[:, 0:16], lam, AF.Exp, scale=-1.0)
        act(c8[:, 0:16], c8[:, 0:16], AF.Ln, bias=one_t)
        ts("pool", c8[:, 16:32], c8[:, 0:16], -16.0, ALU.mult)
        ts("pool", c8[:, 0:16], c8[:, 0:16], -8.0, ALU.mult)
        def lanepair(pi, delay):
            for _ in range(delay):
                yield
            for n2 in range(pi, 8, 2):
                W = load_w(wlru[j, n2])
                gens = [lru_chan(j, 2 * n2 + half, W, half * 256, 2 * pi + half) for half in range(2)]
                while gens:
                    for g_ in list(gens):
                        try:
                            next(g_)
                        except StopIteration:
                            gens.remove(g_)
                    yield

        interleave([lanepair(0, 0), lanepair(1, 32)])
        pq = ps_q.next()
        tr(pq[0:16, :], hl_col, identf)
        sg_ = stg_rot.next()
        cp("dve", sg_[0:16, :], pq[0:16, :])
        store("sp", pl_o[j].rearrange("(n p) -> n p", p=128), sg_[0:16, :])

    def lru_chan(j, n, W, c0, ln):
        ext = ext_b["qk"[ln]] if ln < 2 else ext_l[ln - 2]
        fb = Fp.ts[6 * ln:6 * ln + 6] if ln < 2 else FpL[6 * (ln - 2):6 * (ln - 2) + 6]
        hc = hcar2[:, ln:ln + 1]
        csl = slice(n * 128, (n + 1) * 128)
        wa_b = wg_rot.next()
        wx_b = wg_rot.next()
        load("pool", wa_b, lwa[j, n])
        load("pool", wx_b, lwx[j, n])
        memset("pool", hc, 0.0)
        for g in range(NG):
            o0, ntk = st_range(g)
            sg, xc, r, ig, a2, hsb = fb
            proj_fm(W, c0, g, lambda p: fill_ext(slc_in[j, :, csl], ext, g, p))
            yield
            save_conv_state(plc_o[j, :, csl], slc_o[j, :, csl], ext, g)
            proj_fm(W, c0 + 128, g, lambda p: act(sg[:, 0:ntk], p, AF.Silu))
            yield
            wfn = lambda tap: pp("cwL", (j * 16 + n) * 4 + tap, 1)
            cbias = pp("cbL", j * 16 + n, 1)
            if g < 8:
                src3 = lambda tap: ext[:, tap:tap + ntk]
                dst = xc[:, 0:ntk]
            else:
                e3 = ext[:, 0:176].re("p (s c) -> p s c", c=11)
                src3 = lambda tap: e3[:, :, tap:tap + 8]
                dst = xc[:, 0:128].re("p (s c) -> p s c", c=8)
            ts("dve", dst, src3(0), wfn(0), ALU.mult, cbias, ALU.add)
            for tap in range(1, 4):
                stt(dst, src3(tap), wfn(tap), dst, ALU.mult, ALU.add)
            xcb = Hp.next()
            cp("pool", xcb[:, 0:ntk], xc[:, 0:ntk])
            yield
            pr = ps_acc.next()
            mm(pr[:, 0:ntk], wa_b, xcb[:, 0:ntk])
            act(r[:, 0:ntk], pr[:, 0:ntk], AF.Sigmoid, bias=pp("ba", j * 16 + n, 1))
            yield
            pi = ps_acc.next()
            mm(pi[:, 0:ntk], wx_b, xcb[:, 0:ntk])
            act(ig[:, 0:ntk], pi[:, 0:ntk], AF.Sigmoid, bias=pp("bx", j * 16 + n, 1))
            yield
            act(a2[:, 0:ntk], r[:, 0:ntk], AF.Exp, scale=c8[:, 16 + n:17 + n])
            act(a2[:, 0:ntk], a2[:, 0:ntk], AF.Sqrt, scale=-1.0, bias=one_t)
            if g == 0:
                memset("pool", a2[:, 0:1], 1.0)
            a = r
            act(a[:, 0:ntk], r[:, 0:ntk], AF.Exp, scale=c8[:, n:n + 1])
            bx = ig
            tt("pool", bx[:, 0:ntk], xc[:, 0:ntk], ig[:, 0:ntk], ALU.mult)
            tt("pool", bx[:, 0:ntk], bx[:, 0:ntk], a2[:, 0:ntk], ALU.mult)
            yield
            if g < 8:
                scan(hsb[:, 0:ntk], a[:, 0:ntk], bx[:, 0:ntk], hc)
                cp("pool", hc, hsb[:, ST - 1:ST])
                if g == 7:
                    cp("pool", hl_col[:, n:n + 1], hsb[:, ST - 1:ST])
            else:
                h0 = h0_rot.next()
                load("sp", h0, sl_in[j, :, csl])
                pq = ps_q.next()
                tr(pq[:, 0:16], h0, identf[0:16, 0:16])
                h0T = f_rot.next()
                cp("dve", h0T[:, 0:16], pq[:, 0:16])
                a3 = a[:, 0:128].re("p (s c) -> p s c", c=8)
                b3 = bx[:, 0:128].re("p (s c) -> p s c", c=8)
                tmp16 = f_rot.next()
                tt("pool", tmp16[:, 0:16].re("p (s o) -> p s o", o=1), a3[:, :, 0:1], h0T[:, 0:16].re("p (s o) -> p s o", o=1), ALU.mult)
                tt("pool", b3[:, :, 0:1], b3[:, :, 0:1], tmp16[:, 0:16].re("p (s o) -> p s o", o=1), ALU.add)
                am = xc
                tt("pool", am[:, 0:128], a[:, 0:128], cf("notstart_s"), ALU.mult)
                yield
                scan(hsb[:, 0:128], am[:, 0:128], bx[:, 0:128], 0.0)
                t16 = f_rot.next()
                cp("pool", t16[:, 0:16].re("p (s o) -> p s o", o=1), hsb[:, 0:128].re("p (s c) -> p s c", c=8)[:, :, 7:8])
                pq2 = ps_q.next()
                tr(pq2[0:16, :], t16[:, 0:16], identf)
                sg_ = stg_rot.next()
                cp("dve", sg_[0:16, :], pq2[0:16, :])
                store("sp", sl_o[j, :, csl], sg_[0:16, :])
            ob = ob_rot.next()
            tt("pool", ob[:, 0, 0:ntk], hsb[:, 0:ntk], sg[:, 0:ntk], ALU.mult)
            store_o(n, g, ob, ob.ap[:, 0, 0:ntk])
            yield

    for t in range(NT):
        xt = XR.next()
        load("sp", xt, xin[t * 128:(t + 1) * 128, :])
        norm_tile(t, xt, 0)
    nlayers = {0: 0, 1: 1, 2: 1, 3: 1, 4: 2}.get(stage, 4)
    for l in range(nlayers):
        j = l // 2
        S.barrier()
        if l % 2 == 0:
            ab_layer(j, l)
        else:
            lru_layer(j, l)
        S.barrier()
        last = (l == 3)
        if last:
            load("sp", fnw_t, fn_d.partition_broadcast(128))
        if stage >= 99 or l < nlayers - 1:
            phase_O(l, last)
    S.finish()
    S.emit()
    return nc, S


_CACHE = {}


def _prep_weights(inp):
    f = np.float32
    ab_w_in = np.asarray(inp["ab_w_in"], f)
    out = {}

    def grp(mat):
        m = np.zeros((2048, 512), f)
        m[:, :mat.shape[1]] = mat
        return m.reshape(16, 128, 512).transpose(1, 0, 2)

    wab = np.zeros((2, 16, 128, 16, 512), f)
    wsm = np.zeros((2, 128, 16, 32), f)
    for j in range(2):
        W = ab_w_in[j]
        q0, k0, v0 = 0, 1024, 2048
        b0, a0, z0 = 3072, 3080, 3088
        qb0 = 4112
        kb0 = qb0 + 512
        vb0 = kb0 + 512
        lr0 = vb0 + 1024
        zb0 = lr0 + 16
        for h in range(8):
            cols = np.concatenate([np.arange(q0 + h * 128, q0 + (h + 1) * 128), np.arange(k0 + h * 128, k0 + (h + 1) * 128),
                                   np.arange(v0 + h * 128, v0 + (h + 1) * 128), np.arange(z0 + h * 128, z0 + (h + 1) * 128)])
            wab[j, h] = grp(W[:, cols])
        for hb in range(4):
            cols = np.concatenate([np.arange(qb0 + hb * 128, qb0 + (hb + 1) * 128), np.arange(kb0 + hb * 128, kb0 + (hb + 1) * 128),
                                   np.arange(vb0 + hb * 256, vb0 + (hb + 1) * 256)])
            wab[j, 8 + 2 * hb] = grp(W[:, cols])
            wab[j, 9 + 2 * hb] = grp(W[:, zb0 + hb * 256: zb0 + (hb + 1) * 256])
        sm = np.concatenate([W[:, b0:b0 + 8], W[:, a0:a0 + 8], W[:, lr0:lr0 + 16]], axis=1)
        wsm[j] = sm.reshape(16, 128, 32).transpose(1, 0, 2)
    out["wab"] = wab
    out["wsm"] = wsm
    lru_w_in = np.asarray(inp["lru_w_in"], f)
    wl = np.zeros((2, 8, 128, 16, 512), f)
    for j in range(2):
        W = lru_w_in[j]
        for n2 in range(8):
            cols = []
            for n in (2 * n2, 2 * n2 + 1):
                cols += [np.arange(n * 128, (n + 1) * 128), np.arange(2048 + n * 128, 2048 + (n + 1) * 128)]
            wl[j, n2] = grp(W[:, np.concatenate(cols)])
    out["wlru"] = wl
    wo = np.zeros((4, 4, 128, 16, 512), f)
    for l in range(4):
        W = np.asarray(inp["ab_w_out"] if l % 2 == 0 else inp["lru_w_out"], f)[l // 2]
        for c in range(4):
            wo[l, c] = grp(W[:, c * 512:(c + 1) * 512])
    out["wout"] = wo
    out["lwa"] = np.ascontiguousarray(np.asarray(inp["lru_w_a"], f))
    out["lwx"] = np.ascontiguousarray(np.asarray(inp["lru_w_x"], f))
    out["wlr"] = np.ascontiguousarray(np.asarray(inp["ab_gla_w_lr"], f))
    PPO, NPP = _pp_layout()
    PPa = np.zeros((128, NPP), f)

    def put(k, arr):
        arr = np.asarray(arr, f).reshape(128, -1)
        PPa[:, PPO[k]:PPO[k] + arr.shape[1]] = arr

    nw = np.stack([np.asarray(inp["ab_norm"], f)[0], np.asarray(inp["lru_norm"], f)[0],
                   np.asarray(inp["ab_norm"], f)[1], np.asarray(inp["lru_norm"], f)[1]])
    put("nw", _chan(nw, 16))
    cw = np.asarray(inp["ab_conv_w"], f)
    put("cwA", np.moveaxis(_chan(cw, 24), 2, 3))
    put("normA", np.asarray(inp["ab_norm_a"], f).T)
    put("normB", _chan(np.asarray(inp["ab_norm_b"], f), 2))
    put("nblr", -_chan(np.asarray(inp["ab_gla_b_lr"], f), 4))
    cwl = np.asarray(inp["lru_conv_w"], f)
    put("cwL", np.moveaxis(_chan(cwl, 16), 2, 3))
    put("cbL", _chan(np.asarray(inp["lru_conv_b"], f), 16))
    put("ba", _chan(np.asarray(inp["lru_b_a"], f), 16))
    put("bx", _chan(np.asarray(inp["lru_b_x"], f), 16))
    put("lam", _chan(np.asarray(inp["lru_lambda"], f), 16))
    out["pp"] = PPa
    out["br"] = np.concatenate([np.asarray(inp["ab_a_log"], f).reshape(-1), np.asarray(inp["ab_dt_bias"], f).reshape(-1)])[None, :]
    out["fnorm"] = np.asarray(inp["final_norm"], f)[None, :]
    cst = _masks()
    out["cf"], _ = _pack(CF_KEYS, cst)
    out["cb"], _ = _pack(CB_KEYS, cst)
    return out


def run(inputs, stage=99, cores=8, trace=False):
    f = np.float32
    key = stage
    if key not in _CACHE:
        _CACHE[key] = build_program(stage)
    nc, S = _CACHE[key]
    shared = _prep_weights(inputs)
    xp = np.asarray(inputs["x_prompt"], f)
    xs = np.asarray(inputs["x_sample"], f)
    in_maps = []
    for c in range(cores):
        m = dict(shared)
        sl = slice(16 * c, 16 * (c + 1))
        m["xin"] = np.concatenate([xp[c % 4], xs[sl].reshape(128, D)], axis=0)
        m["sd_in"] = np.ascontiguousarray(np.asarray(inputs["state_delta"], f)[:, sl])
        m["sdc_in"] = np.ascontiguousarray(np.asarray(inputs["state_delta_conv"], f)[:, sl]).reshape(2, 48, 3072)
        m["sg_in"] = np.ascontiguousarray(np.asarray(inputs["state_gla"], f)[:, sl])
        m["sl_in"] = np.ascontiguousarray(np.asarray(inputs["state_lru"], f)[:, sl])
        m["slc_in"] = np.ascontiguousarray(np.asarray(inputs["state_lru_conv"], f)[:, sl]).reshape(2, 48, 2048)
        in_maps.append(m)
    res = run_bass_kernel_spmd(nc, in_maps, core_ids=list(range(cores)), trace=trace)
    if trace:
        print('EXEC_TIME_NS', res.exec_time_ns)
        global LAST_RES
        LAST_RES = res
    R = list(res.results)
    while len(R) < 8:
        R.append(R[0])
    y_prompt = np.stack([R[c]["y_o"][:SEQ] for c in range(4)])
    y_sample = np.concatenate([R[c]["y_o"][SEQ:].reshape(16, 8, D) for c in range(8)], axis=0)
    p_delta = np.stack([R[c]["pd_o"] for c in range(4)], axis=1)
    p_dconv = np.stack([R[c]["pdc_o"] for c in range(4)], axis=1)
    p_gla = np.stack([R[c]["pg_o"] for c in range(4)], axis=1)
    p_lru = np.stack([R[c]["pl_o"] for c in range(4)], axis=1)
    p_lconv = np.stack([R[c]["plc_o"] for c in range(4)], axis=1)
    s_delta = np.concatenate([R[c]["sd_o"] for c in range(8)], axis=1)
    s_dconv = np.concatenate([R[c]["sdc_o"].reshape(2, 16, 3, 3072) for c in range(8)], axis=1)
    s_gla = np.concatenate([R[c]["sg_o"] for c in range(8)], axis=1)
    s_lru = np.concatenate([R[c]["sl_o"] for c in range(8)], axis=1)
    s_lconv = np.concatenate([R[c]["slc_o"].reshape(2, 16, 3, 2048) for c in range(8)], axis=1)
    outs = (y_prompt, y_sample, p_delta, p_dconv, p_gla, p_lru, p_lconv, s_delta, s_dconv, s_gla, s_lru, s_lconv)
    return tuple(np.ascontiguousarray(o, dtype=f) for o in outs)


def kernel(**inputs):
    return run(inputs, stage=99)
```

```python
import numpy as np
import concourse.bass as bass
import concourse.mybir as mybir
from concourse.bass_utils import run_bass_kernel_spmd

F32 = mybir.dt.float32
BF16 = mybir.dt.bfloat16
AF = mybir.ActivationFunctionType
ALU = mybir.AluOpType

D = 2048
NT = 17
NTOK = NT * 128
SEQ = 2048
EPS = 1e-6
NSEQ = 16
AB_IN = 7200


class Buf:
    __slots__ = ("name", "lw", "rd", "dsem", "dcount", "excl")

    def __init__(self, name="", excl=False):
        self.name = name
        self.excl = excl
        self.lw = None
        self.rd = []
        self.dsem = None
        self.dcount = 0


class T:
    __slots__ = ("ap", "buf")

    def __init__(self, ap, buf):
        self.ap = ap
        self.buf = buf

    def __getitem__(self, k):
        return T(self.ap[k], self.buf)

    def bc(self, shape):
        return T(self.ap.to_broadcast(list(shape)), self.buf)

    def re(self, pat, **kw):
        return T(self.ap.rearrange(pat, **kw), self.buf)

    def bitcast(self, dt):
        return T(self.ap.bitcast(dt), self.buf)

    def with_buf(self, buf):
        return T(self.ap, buf)


class Sched:
    ENG = ("pe", "act", "dve", "pool", "sp")
    SEM_LIMIT = 30000

    def __init__(self, nc, same_engine_sync=True):
        self.nc = nc
        self.prog = {e: [] for e in self.ENG}
        self.sem = {}
        self.cnt = {}
        self.nsem = 0
        for e in self.ENG:
            self._newsem(e)
        self.waited = {e: {} for e in self.ENG}
        self.ses = same_engine_sync
        self.ninstr = {e: 0 for e in self.ENG}
        self.dbufs = []

    def _alloc(self, name):
        self.nsem += 1
        return self.nc.alloc_semaphore(name=f"{name}_{self.nsem}")

    def _newsem(self, e):
        self.sem[e] = self._alloc("s" + e)
        self.cnt[e] = 0

    def _deps(self, eng, reads, writes):
        toks = []
        for b in reads:
            if b.lw is not None:
                toks.append(b.lw)
        for b in writes:
            if b.lw is not None:
                toks.append(b.lw)
            toks.extend(b.rd)
        wd = self.waited[eng]
        mx = {}
        for (sem, val, te) in toks:
            if te == eng and (eng == "pe" or not self.ses):
                continue
            k = id(sem)
            if wd.get(k, 0) >= val:
                continue
            if k not in mx or mx[k][1] < val:
                mx[k] = (sem, val)
        for k, (sem, val) in mx.items():
            wd[k] = val
        return list(mx.values())

    def _mark(self, tok, reads, writes):
        for b in writes:
            b.lw = tok
            b.rd = []
        wset = set(id(b) for b in writes)
        for b in reads:
            if id(b) not in wset:
                b.rd.append(tok)
                if len(b.rd) > 64:
                    last = {}
                    for t in b.rd:
                        k = id(t[0])
                        if k not in last or last[k][1] < t[1]:
                            last[k] = t
                    b.rd = list(last.values())

    def op(self, eng, fn, reads=(), writes=()):
        writes = [b for b in writes] + [b for b in reads if b.excl]
        reads = [b for b in reads if not b.excl]
        waits = self._deps(eng, reads, writes)
        if self.cnt[eng] >= self.SEM_LIMIT:
            self._newsem(eng)
        self.cnt[eng] += 1
        sem = self.sem[eng]
        tok = (sem, self.cnt[eng], eng)
        self.prog[eng].append((fn, waits, (sem, 1)))
        self._mark(tok, reads, writes)
        self.ninstr[eng] += 1

    def dma(self, q, out, in_, reads=(), writes=(), owner=None):
        reads = list(reads)
        writes = list(writes)
        if owner is None:
            owner = writes[0] if len(writes) else reads[0]
        if owner.dsem is None:
            owner.dsem = self._alloc("d")
            owner.dcount = 0
            self.dbufs.append(owner)
        waits = self._deps(q, reads, writes)
        owner.dcount += 16
        tok = (owner.dsem, owner.dcount, "dma")
        self.prog[q].append((lambda e, o=out, i=in_: e.dma_start(out=o, in_=i), waits, (owner.dsem, 16)))
        self._mark(tok, reads, writes)
        self.ninstr[q] += 1

    def barrier(self):
        toks = [(self.sem[e], self.cnt[e], e) for e in self.ENG if self.cnt[e] > 0]
        toks += [(b.dsem, b.dcount, "dma") for b in self.dbufs]
        for e in self.ENG:
            wd = self.waited[e]
            waits = []
            for sem, val, te in toks:
                if te == e:
                    continue
                if wd.get(id(sem), 0) >= val:
                    continue
                wd[id(sem)] = val
                waits.append((sem, val))
            self.prog[e].append((None, waits, None))

    def finish(self):
        waits = []
        for b in self.dbufs:
            waits.append((b.dsem, b.dcount))
        self.prog["sp"].append((None, waits, None))

    def emit(self):
        nc = self.nc
        with nc.Block() as block:
            def run(eng_name):
                def body(e):
                    for fn, waits, inc in self.prog[eng_name]:
                        for sem, val in waits:
                            e.wait_ge(sem, val)
                        if fn is not None:
                            ins = fn(e)
                            if inc is not None:
                                ins.then_inc(inc[0], inc[1])
                return body
            block.tensor(run("pe"))
            block.scalar(run("act"))
            block.vector(run("dve"))
            block.gpsimd(run("pool"))
            block.sync(run("sp"))


def _masks():
    idx = np.arange(128)
    i = idx[None, :]
    j = idx[:, None]
    c = {}
    for typ, sb in (("p", 128), ("s", 8)):
        same = (i // sb) == (j // sb)
        c["U" + typ] = ((j <= i) & same).astype(np.float32)
        c["SLT" + typ] = ((j > i) & same).astype(np.float32)
        c["MTi" + typ] = ((j <= i) & same).astype(np.float32)
        c["MTs" + typ] = ((j < i) & same).astype(np.float32)
        lv = []
        s = 1
        while 2 * s <= sb:
            m = ((i // (2 * s)) == (j // (2 * s))) & ((i % (2 * s)) >= s) & ((j % (2 * s)) < s)
            lv.append(m.astype(np.float32))
            s *= 2
        c["LV" + typ] = np.concatenate(lv, axis=1)
    c["ident"] = np.eye(128, dtype=np.float32)
    c["ones"] = np.ones((128, 128), np.float32)
    bs = (idx[:, None] // 8 == np.arange(16)[None, :]).astype(np.float32)
    c["bsel"] = bs
    c["colmask"] = np.tile(bs.T.reshape(1, 16 * 128), (128, 1))
    ns = np.ones((128, 128), np.float32)
    ns[:, ::8] = 0.0
    c["notstart_s"] = ns
    npm = np.ones((128, 512), np.float32)
    npm[:, ::128] = 0.0
    c["notstart_p"] = npm
    return c


CF_KEYS = ["ident", "ones", "Up", "SLTp", "Us", "SLTs", "bsel", "MTip", "MTsp", "MTis", "MTss",
           "notstart_s"]
CB_KEYS = ["ident", "MTip", "MTis", "LVp", "LVs", "colmask", "bsel"]


def _pack(keys, c):
    offs = {}
    o = 0
    for k in keys:
        offs[k] = (o, c[k].shape[1])
        o += c[k].shape[1]
    arr = np.concatenate([c[k] for k in keys], axis=1).astype(np.float32)
    return arr, offs


def _pp_layout():
    items = [("nw", 4 * 16), ("cwA", 2 * 24 * 4), ("normA", 2), ("normB", 4), ("nblr", 8),
             ("cwL", 2 * 16 * 4), ("cbL", 32), ("ba", 32), ("bx", 32), ("lam", 32)]
    offs = {}
    o = 0
    for k, w in items:
        offs[k] = o
        o += w
    return offs, o


def _chan(v, nch):
    v = np.asarray(v, np.float32)
    lead = v.shape[:-1]
    v = v.reshape(lead + (nch, 128))
    return np.moveaxis(v, -1, 0)


NG = 9
ST = 256


def st_range(g):
    return (g * ST, ST) if g < 8 else (2048, 128)


def build_program(stage=99):
    nc = bass.Bass("TRN2", target_bir_lowering=False)
    S = Sched(nc)
    cst = _masks()
    _, CFO = _pack(CF_KEYS, cst)
    _, CBO = _pack(CB_KEYS, cst)
    NCF = sum(w for _, w in CFO.values())
    NCB = sum(w for _, w in CBO.values())
    PPO, NPP = _pp_layout()
    uid = [0]

    def dram(name, shape, dt=F32, kind="ExternalInput"):
        return nc.dram_tensor(name, list(shape), dt, kind=kind).ap()

    xin = dram("xin", [NTOK, D])
    sd_in = dram("sd_in", [2, NSEQ, 8, 128, 128])
    sdc_in = dram("sdc_in", [2, 48, 3072])
    sg_in = dram("sg_in", [2, NSEQ, 4, 128, 256])
    sl_in = dram("sl_in", [2, NSEQ, 2048])
    slc_in = dram("slc_in", [2, 48, 2048])
    wab = dram("wab", [2, 16, 128, 16, 512])
    wsm = dram("wsm", [2, 128, 16, 32])
    wlru = dram("wlru", [2, 8, 128, 16, 512])
    wout = dram("wout", [4, 4, 128, 16, 512])
    lwa = dram("lwa", [2, 16, 128, 128])
    lwx = dram("lwx", [2, 16, 128, 128])
    wlr = dram("wlr", [2, 16, 512])
    pp_d = dram("pp", [128, NPP])
    br_d = dram("br", [1, 32])
    fn_d = dram("fnorm", [1, D])
    cf_d = dram("cf", [128, NCF])
    cb_d = dram("cb", [128, NCB])

    y_o = dram("y_o", [NTOK, D], kind="ExternalOutput")
    pd_o = dram("pd_o", [2, 8, 128, 128], kind="ExternalOutput")
    pdc_o = dram("pdc_o", [2, 3, 3072], kind="ExternalOutput")
    pg_o = dram("pg_o", [2, 4, 128, 256], kind="ExternalOutput")
    pl_o = dram("pl_o", [2, 2048], kind="ExternalOutput")
    plc_o = dram("plc_o", [2, 3, 2048], kind="ExternalOutput")
    sd_o = dram("sd_o", [2, NSEQ, 8, 128, 128], kind="ExternalOutput")
    sdc_o = dram("sdc_o", [2, 48, 3072], kind="ExternalOutput")
    sg_o = dram("sg_o", [2, NSEQ, 4, 128, 256], kind="ExternalOutput")
    sl_o = dram("sl_o", [2, NSEQ, 2048], kind="ExternalOutput")
    slc_o = dram("slc_o", [2, 48, 2048], kind="ExternalOutput")
    xres = dram("xres", [NTOK, D], kind="ExternalOutput")
    oscr = dram("oscr", [NT, 128, 16, 128], BF16, kind="ExternalOutput")
    b_xres = [Buf(f"xres{t}") for t in range(NT)]
    b_oscr = [Buf(f"oscr{g}") for g in range(NG)]

    def sb(name, shape, dt=F32):
        uid[0] += 1
        return T(nc.alloc_sbuf_tensor(f"{name}_{uid[0]}", list(shape), dt)[:], Buf(name))

    ARENA_F32 = 18688
    arena = nc.alloc_sbuf_tensor("arena", [128, ARENA_F32], F32)[:]
    ar_off = [0]

    def ar(shape, dt=F32, parts=128):
        n = 1
        for s_ in shape[1:]:
            n *= s_
        words = n if dt == F32 else (n + 1) // 2
        words = (words + 7) // 8 * 8
        o = ar_off[0]
        assert o + words <= ARENA_F32, ("arena overflow", o, words)
        ar_off[0] = o + words
        a = arena[0:parts, o:o + words]
        if dt != F32:
            a = a.bitcast(dt)
        a = a[:, 0:n]
        if len(shape) == 3:
            a = a.rearrange("p (a b) -> p a b", a=shape[1])
        return T(a, Buf("ar"))

    class Rot:
        def __init__(self, shape, dt, n, alloc=None, parts=128):
            if alloc is None:
                self.ts = [ar(shape, dt, parts) for _ in range(n)]
            else:
                self.ts = [alloc(f"rot{i}", shape, dt) for i in range(n)]
            self.i = 0

        def next(self):
            t = self.ts[self.i % len(self.ts)]
            self.i += 1
            return t

    psum_all = nc.alloc_psum_tensor("psum_all", [128, 4096], F32)[:]

    def bank(b):
        return psum_all[:, b * 512:(b + 1) * 512]

    bank_buf = [Buf(f"bank{b}", excl=True) for b in range(8)]

    class PRot:
        def __init__(self, aps):
            self.ts = [T(a, bank_buf[b]) for a, b in aps]
            self.i = 0

        def next(self):
            t = self.ts[self.i % len(self.ts)]
            self.i += 1
            return t

    ps_acc = PRot([(bank(0), 0), (bank(1), 1)])
    ps_q = PRot([(bank(b)[:, q * 128:(q + 1) * 128], b) for q in range(4) for b in (2, 3, 7)])
    ps_h = PRot([(bank(4)[:, 0:256], 4), (bank(4)[:, 256:512], 4)])
    ps_g = PRot([(bank(5), 5), (bank(6), 6)])
    ps_n = PRot([(psum_all[:, 4 * 512:6 * 512], 4), (psum_all[:, 6 * 512:8 * 512], 6)])

    def bufs(*ts_):
        return [t.buf for t in ts_ if isinstance(t, T)]

    def apof(x):
        return x.ap if isinstance(x, T) else x

    def mm(out, lhsT, rhs, start=True, stop=True):
        S.op("pe", lambda e, o=out.ap, l=lhsT.ap, r=rhs.ap, st=start, sp=stop: e.matmul(o, lhsT=l, rhs=r, start=st, stop=sp),
             reads=bufs(lhsT, rhs), writes=bufs(out))

    def tr(out, in_, ident):
        S.op("pe", lambda e, o=out.ap, i=in_.ap, d=ident.ap: e.transpose(out=o, in_=i, identity=d),
             reads=bufs(in_, ident), writes=bufs(out))

    def tt(eng, out, in0, in1, op):
        S.op(eng, lambda e, o=out.ap, a=in0.ap, b=in1.ap, p=op: e.tensor_tensor(out=o, in0=a, in1=b, op=p),
             reads=bufs(in0, in1), writes=bufs(out))

    def ts(eng, out, in0, s1, op0, s2=None, op1=None):
        kw = dict(out=out.ap, in0=in0.ap, scalar1=apof(s1), scalar2=apof(s2), op0=op0)
        if op1 is not None:
            kw["op1"] = op1
        S.op(eng, lambda e, kw=kw: e.tensor_scalar(**kw), reads=bufs(in0, s1, s2), writes=bufs(out))

    def stt(out, in0, scalar, in1, op0, op1):
        S.op("dve", lambda e, o=out.ap, a=in0.ap, s=apof(scalar), b=in1.ap, p0=op0, p1=op1:
             e.scalar_tensor_tensor(out=o, in0=a, scalar=s, in1=b, op0=p0, op1=p1),
             reads=bufs(in0, scalar, in1), writes=bufs(out))

    def act(out, in_, func, scale=None, bias=None, accum=None):
        kw = dict(out=out.ap, in_=in_.ap, func=func)
        if scale is not None:
            kw["scale"] = apof(scale)
        if bias is not None:
            kw["bias"] = apof(bias)
        if accum is not None:
            kw["accum_out"] = accum.ap
        S.op("act", lambda e, kw=kw: e.activation(**kw), reads=bufs(in_, scale, bias),
             writes=bufs(out) + (bufs(accum) if accum is not None else []))

    def cp(eng, out, in_):
        if eng == "act":
            act(out, in_, AF.Copy)
        else:
            S.op(eng, lambda e, o=out.ap, i=in_.ap: e.tensor_copy(out=o, in_=i), reads=bufs(in_), writes=bufs(out))

    def memset(eng, out, val):
        S.op(eng, lambda e, o=out.ap, v=val: e.memset(o, v), writes=bufs(out))

    def recip(out, in_):
        S.op("dve", lambda e, o=out.ap, i=in_.ap: e.reciprocal(out=o, in_=i), reads=bufs(in_), writes=bufs(out))

    def scan(out, d0, d1, init):
        S.op("dve", lambda e, o=out.ap, a=d0.ap, b=d1.ap, i=apof(init):
             e.tensor_tensor_scan(out=o, data0=a, data1=b, initial=i, op0=ALU.mult, op1=ALU.add),
             reads=bufs(d0, d1, init), writes=bufs(out))

    def load(q, out_t, in_ap, extra_reads=()):
        S.dma(q, out_t.ap, in_ap, reads=list(extra_reads), writes=[out_t.buf])

    def store(q, out_ap, in_t, dwrites=()):
        S.dma(q, out_ap, in_t.ap, reads=[in_t.buf], writes=list(dwrites), owner=in_t.buf)

    def store_o(kc, g, ob_t, src_ap):
        o0, n = st_range(g)
        t0, nt = o0 // 128, n // 128
        S.dma("pool", oscr[t0:t0 + nt, :, kc, :].rearrange("t p c -> p t c"), src_ap.rearrange("p (t c) -> p t c", c=128),
              reads=[ob_t.buf], writes=[b_oscr[g]], owner=ob_t.buf)

    CF = sb("CF", [128, NCF])
    CB = sb("CB", [128, NCB], BF16)
    PP = sb("PP", [128, NPP])
    BR = sb("BR", [128, 32])
    load("sp", CF, cf_d)
    for c0 in range(0, NCB, 1024):
        c1 = min(NCB, c0 + 1024)
        S.dma("pool", CB.ap[:, c0:c1], cb_d[:, c0:c1], writes=[CB.buf])
    load("sp", PP, pp_d)
    load("sp", BR, br_d.partition_broadcast(128))

    def cf(k):
        o, w = CFO[k]
        return CF[:, o:o + w]

    def cb(k):
        o, w = CBO[k]
        return CB[:, o:o + w]

    identf = cf("ident")
    identb = cb("ident")
    onesf = cf("ones")
    eps_t = sb("eps", [128, 1])
    memset("pool", eps_t, EPS)
    one_t = sb("one", [128, 1])
    memset("pool", one_t, 1.0)

    def pp(k, off=0, w=1):
        o = PPO[k] + off
        return PP[:, o:o + w]

    xnT_all = nc.alloc_sbuf_tensor("xnT", [128, 16, NTOK], BF16)[:]
    b_xn = [Buf(f"xn{g}") for g in range(NG)]

    def xn_st(g):
        o, n = st_range(g)
        return T(xnT_all[:, :, o:o + n], b_xn[g])

    def xn_tile(t):
        return T(xnT_all[:, :, t * 128:(t + 1) * 128], b_xn[min(t // 2, 8)])

    Wrot = Rot([128, 16, 512], BF16, 2, alloc=sb)

    def load_w(src_ap, ncols=512):
        w = Wrot.next()
        for q4 in range(4):
            S.dma("pool", w.ap[:, q4 * 4:(q4 + 1) * 4, 0:ncols], src_ap[:, q4 * 4:(q4 + 1) * 4, 0:ncols], writes=[w.buf])
        return w

    Sf = sb("Sf", [128, 256])
    Sb = sb("Sb", [128, 256], BF16)
    bas = sb("bas", [128, NT, 16])
    beta_all = sb("beta", [128, NT, 8])
    nbeta_all = sb("nbeta", [128, NT, 8])
    g_all = sb("gall", [128, NT, 8])
    eG_all = sb("eG", [128, NT, 8])
    eGr_all = sb("eGr", [128, NT, 8])
    eGl_p = sb("eGlp", [128, 16, 8])
    eGl_s = sb("eGls", [128, 8, 16])
    gsel = sb("gsel", [128, 8, 16])
    nea = sb("nea", [128, 8])
    lrT = sb("lrT", [16, NTOK], BF16)
    wlr_b = sb("wlrb", [16, 512], BF16)
    wsm_b = sb("wsmb", [128, 16, 32], BF16)
    c8 = sb("c8", [128, 32])
    hcar2 = sb("hcar", [128, 4])
    hl_col = sb("hlcol", [128, 16])

    ar_off[0] = 0
    ext_b = {k: ar([128, ST + 4]) for k in "qkv"}
    Fp = Rot([128, ST], F32, 12)
    Hp = Rot([128, ST], BF16, 12)
    sz_rot = Rot([128, 2, ST], BF16, 4)
    ob_rot = Rot([128, 2, ST], BF16, 3)
    hist_rot = Rot([48, 128], F32, 2, parts=48)
    stg_rot = Rot([48, 128], F32, 2, parts=48)
    f_rot = Rot([128, 128], F32, 7)
    b_rot = Rot([128, 128], BF16, 4)
    m_rot = Rot([128, 4, 128], BF16, 4)
    sm_rot = Rot([128, 8], F32, 4)
    wg_rot = Rot([128, 128], BF16, 8)
    h0_rot = Rot([16, 128], F32, 2, parts=16)
    ab_only_start = ar_off[0]

    class Lane:
        pass

    lanes = []
    for _ln in range(2):
        L_ = Lane()
        L_.f = [ar([128, 128]) for _ in range(5)]
        L_.atl = ar([128, 7, 128], BF16)
        L_.T = [ar([128, 128], BF16) for _ in range(2)]
        L_.Tt = [ar([128, 128], BF16) for _ in range(2)]
        L_.Yb = [ar([128, 128], BF16) for _ in range(2)]
        L_.kg = ar([128, 128], BF16)
        L_.vtok = ar([128, 128], BF16)
        L_.hand = [dict(qkT=ar([128, 128], BF16), kdec=ar([128, 128], BF16), xkT=ar([128, 128], BF16),
                        bXv=ar([128, 128])) for _ in range(2)]
        lanes.append(L_)
    r_o1 = Rot([128, 128], F32, 2)
    r_b = Rot([128, 128], BF16, 6)
    ob16_rot = Rot([128, 256], BF16, 4)
    vt_rot = Rot([128, 256], BF16, 2)
    Sbs = Rot([128, 16, 128], BF16, 1)
    Sfs = Rot([128, 4, 256], F32, 1)
    mixer_top = ar_off[0]
    ar_off[0] = ab_only_start
    FpL = [ar([128, ST]) for _ in range(12)]
    ext_l = [ar([128, ST + 4]) for _ in range(2)]
    assert ar_off[0] <= mixer_top
    ar_off[0] = 0
    XR = Rot([128, D], F32, 2)
    ot_rot = Rot([128, 16, 128], BF16, 2)
    ssq_rot = Rot([128, 8], F32, 4)
    fnw_t = ar([128, D])
    junkB = ar([128, D], BF16)

    def typ_of(t):
        return "s" if t == 16 else "p"

    def norm_tile(t, xt, layer_next):
        sq = ssq_rot.next()
        act(junkB, xt, AF.Square, accum=sq[:, 0:1])
        act(sq[:, 1:2], sq[:, 0:1], AF.Sqrt, scale=1.0 / D, bias=eps_t)
        recip(sq[:, 1:2], sq[:, 1:2])
        if layer_next < 4:
            act(xt, xt, AF.Identity, scale=sq[:, 1:2])
            nw = pp("nw", layer_next * 16, 16)
            dst = xn_tile(t)
            for hf in range(2):
                pn = ps_n.next()
                for k8 in range(8):
                    kc = hf * 8 + k8
                    tr(pn[:, k8 * 128:(k8 + 1) * 128], xt[:, kc * 128:(kc + 1) * 128], identf)
                tt("dve", dst[:, hf * 8:(hf + 1) * 8, :], pn.re("p (k t) -> p k t", k=8),
                   nw[:, hf * 8:(hf + 1) * 8].re("p (k o) -> p k o", o=1).bc([128, 8, 128]), ALU.mult)
        else:
            stt(xt, xt, sq[:, 1:2], fnw_t, ALU.mult, ALU.mult)
            store("pool", y_o[t * 128:(t + 1) * 128, :], xt)

    def phase_O(l, last):
        src = xin if l == 0 else xres
        steps = [(c, t) for c in range(4) for t in range(NT)]
        wgs = {0: load_w(wout[l, 0])}

        def prefetch(idx):
            c, t = steps[idx]
            rows = slice(t * 128, (t + 1) * 128)
            cs_ = slice(c * 512, (c + 1) * 512)
            ot = ot_rot.next()
            S.dma("sp", ot.ap, oscr[t], reads=[b_oscr[min(t // 2, 8)]], writes=[ot.buf])
            xt = XR.next()
            if c < 3:
                S.dma("sp", xt.ap[:, cs_], src[rows, cs_], reads=[b_xres[t]], writes=[xt.buf])
            else:
                S.dma("sp", xt.ap[:, 0:1536], xres[rows, 0:1536], reads=[b_xres[t]], writes=[xt.buf])
                S.dma("sp", xt.ap[:, 1536:2048], src[rows, 1536:2048], reads=[b_xres[t]], writes=[xt.buf])
            return ot, xt

        def compute(idx, ot, xt):
            c, t = steps[idx]
            wg = wgs[c]
            rows = slice(t * 128, (t + 1) * 128)
            cs_ = slice(c * 512, (c + 1) * 512)
            pa = ps_acc.next()
            for ec in range(16):
                mm(pa, ot[:, ec, :], wg[:, ec, :], start=(ec == 0), stop=(ec == 15))
            if t == 0 and c + 1 < 4:
                wgs[c + 1] = load_w(wout[l, c + 1])
            tt("dve", xt[:, cs_], pa, xt[:, cs_], ALU.add)
            if c < 3 or not last:
                S.dma("pool", xres[rows, cs_], xt.ap[:, cs_], reads=[xt.buf], writes=[b_xres[t]], owner=xt.buf)
            if c == 3:
                norm_tile(t, xt, l + 1)

        pend = prefetch(0)
        for idx in range(len(steps)):
            cur = pend
            if idx + 1 < len(steps):
                pend = prefetch(idx + 1)
            compute(idx, *cur)

    def ab_layer(j, l):
        load("pool", wsm_b, wsm[j])
        load("pool", wlr_b, wlr[j])
        for t in range(NT):
            pa = ps_q.next()
            xt_ = xn_tile(t)
            for kc in range(16):
                mm(pa[:, 0:16], xt_[:, kc, :], wsm_b[:, kc, 0:16], start=(kc == 0), stop=(kc == 15))
            cp("act", bas[:, t, :], pa[:, 0:16])
        for g in range(NG):
            o0, n = st_range(g)
            pa = ps_acc.next()
            xg = xn_st(g)
            for kc in range(16):
                mm(pa[0:16, 0:n], wsm_b[:, kc, 16:32], xg[:, kc, :], start=(kc == 0), stop=(kc == 15))
            cp("act", lrT[:, o0:o0 + n], pa[0:16, 0:n])
        act(beta_all, bas[:, :, 0:8], AF.Sigmoid)
        ts("pool", nbeta_all, beta_all, -1.0, ALU.mult)
        act(nea, BR[:, j * 8:(j + 1) * 8], AF.Exp)
        ts("pool", nea, nea, -1.0, ALU.mult)
        tt("dve", g_all, bas[:, :, 8:16], BR[:, 16 + j * 8:16 + (j + 1) * 8].re("p (o h) -> p o h", o=1).bc([128, NT, 8]), ALU.add)
        act(g_all, g_all, AF.Exp)
        act(g_all, g_all, AF.Ln, bias=one_t)
        tt("dve", g_all, g_all, nea.re("p (o h) -> p o h", o=1).bc([128, NT, 8]), ALU.mult)
        for t in range(NT):
            ty = typ_of(t)
            pa = ps_q.next()
            mm(pa[:, 0:8], cf("U" + ty), g_all[:, t, :])
            act(eG_all[:, t, :], pa[:, 0:8], AF.Exp)
            pa = ps_q.next()
            mm(pa[:, 0:8], cf("SLT" + ty), g_all[:, t, :])
            act(eGr_all[:, t, :], pa[:, 0:8], AF.Exp)
            pa = ps_q.next()
            if ty == "p":
                mm(pa[:, 0:8], onesf, g_all[:, t, :])
                act(eGl_p[:, t, :], pa[:, 0:8], AF.Exp)
            else:
                tt("pool", gsel, g_all[:, t, :].re("p (h o) -> p h o", o=1).bc([128, 8, 16]),
                   cf("bsel").re("p (o s) -> p o s", o=1).bc([128, 8, 16]), ALU.mult)
                mm(pa, onesf, gsel.re("p h s -> p (h s)"))
                act(eGl_s.re("p h s -> p (h s)"), pa, AF.Exp)
        if stage >= 2:
            Wn = load_w(wab[j, 0])
            for h in range(8):
                Wc = Wn
                if h + 1 < 8:
                    Wn = load_w(wab[j, h + 1])
                tail = gdn_head(j, h, Wc, tail if h > 0 else None)
            drain(tail)
        if stage >= 3:
            for hb in range(4):
                gla_head(j, hb)

    def conv_feature(wfn, ext, g, bias=None):
        o0, n = st_range(g)
        co = Fp.next()
        if g < 8:
            src3 = lambda tap: ext[:, tap:tap + n]
            dst = co[:, 0:n]
        else:
            e3 = ext[:, 0:176].re("p (s c) -> p s c", c=11)
            src3 = lambda tap: e3[:, :, tap:tap + 8]
            dst = co[:, 0:128].re("p (s c) -> p s c", c=8)
        if bias is None:
            ts("dve", dst, src3(0), wfn(0), ALU.mult)
        else:
            ts("dve", dst, src3(0), wfn(0), ALU.mult, bias, ALU.add)
        for tap in range(1, 4):
            stt(dst, src3(tap), wfn(tap), dst, ALU.mult, ALU.add)
        return co

    def proj_fm(W, c0, g, dst_fn):
        o0, n = st_range(g)
        pa = ps_acc.next()
        xg = xn_st(g)
        for kc in range(16):
            mm(pa[:, 0:n], W[:, kc, c0:c0 + 128], xg[:, kc, :], start=(kc == 0), stop=(kc == 15))
        dst_fn(pa[:, 0:n])

    def fill_ext(hist_src, ext, g, psum_src):
        o0, n = st_range(g)
        if g < 8:
            if g == 0:
                memset("pool", ext[:, 0:3], 0.0)
            else:
                cp("pool", ext[:, 0:3], ext[:, ST:ST + 3])
            cp("act", ext[:, 3:3 + n], psum_src)
        else:
            e3 = ext[:, 0:176].re("p (s c) -> p s c", c=11)
            hs = hist_rot.next()
            load("sp", hs, hist_src)
            pq = ps_q.next()
            tr(pq[:, 0:48], hs, identf[0:48, 0:48])
            cp("dve", e3[:, :, 0:3], pq[:, 0:48].re("p (s c) -> p s c", c=3))
            cp("act", e3[:, :, 3:11], psum_src.re("p (s c) -> p s c", c=8))

    def save_conv_state(dst_p, dst_s, ext, g):
        if g == 7:
            pq = ps_q.next()
            tr(pq[0:3, :], ext[:, ST:ST + 3], identf)
            sg_ = stg_rot.next()
            cp("dve", sg_[0:3, :], pq[0:3, :])
            store("sp", dst_p, sg_[0:3, :])
        elif g == 8:
            e3 = ext[:, 0:176].re("p (s c) -> p s c", c=11)
            tmp = f_rot.next()
            cp("pool", tmp[:, 0:48].re("p (s c) -> p s c", c=3), e3[:, :, 8:11])
            pq = ps_q.next()
            tr(pq[0:48, :], tmp[:, 0:48], identf)
            sg_ = stg_rot.next()
            cp("dve", sg_, pq[0:48, :])
            store("sp", dst_s, sg_)

    def l2norm_fm(src, n):
        sq = Fp.next()
        act(sq[:, 0:n], src[:, 0:n], AF.Square)
        pa = ps_acc.next()
        mm(pa[:, 0:n], onesf, sq[:, 0:n])
        rs = Fp.next()
        act(rs[:, 0:n], pa[:, 0:n], AF.Sqrt, bias=eps_t)
        recip(rs[:, 0:n], rs[:, 0:n])
        return rs

    def interleave(gens):
        gens = [g_ for g_ in gens if g_ is not None]
        while gens:
            for g_ in list(gens):
                try:
                    next(g_)
                except StopIteration:
                    gens.remove(g_)

    def drain(gen):
        for _ in gen:
            pass

    def gdn_A(j, h, g, W, res):
        o0, n = st_range(g)
        for ci, k in enumerate("qkv"):
            chunk = ci * 8 + h
            csl = slice(chunk * 128, (chunk + 1) * 128)
            proj_fm(W, ci * 128, g, lambda p, k=k, csl=csl: fill_ext(sdc_in[j, :, csl], ext_b[k], g, p))
            yield
            save_conv_state(pdc_o[j, :, csl], sdc_o[j, :, csl], ext_b[k], g)
        sz = sz_rot.next()
        proj_fm(W, 384, g, lambda p: act(sz[:, 0, 0:n], p, AF.Silu))
        yield
        acts = {}
        for ci, k in enumerate("qkv"):
            chunk = ci * 8 + h
            co = conv_feature(lambda tap, chunk=chunk: pp("cwA", (j * 24 + chunk) * 4 + tap, 1), ext_b[k], g)
            a_ = Hp.next() if k == "v" else Fp.next()
            act(a_[:, 0:n], co[:, 0:n], AF.Silu)
            acts[k] = a_
            yield
        rsq = l2norm_fm(acts["q"], n)
        qT = Hp.next()
        stt(qT[:, 0:n], acts["q"][:, 0:n], 128.0 ** -0.5, rsq[:, 0:n], ALU.mult, ALU.mult)
        yield
        rsk = l2norm_fm(acts["k"], n)
        kT = Hp.next()
        tt("pool", kT[:, 0:n], acts["k"][:, 0:n], rsk[:, 0:n], ALU.mult)
        res[g] = dict(qT=qT, kT=kT, av=acts["v"], sz=sz)
        yield

    def gdn_prep(j, h, t, ti, A, lane, par):
        ty = typ_of(t)
        L = 7 if ty == "p" else 3
        cs = slice(ti * 128, (ti + 1) * 128)
        qT, kT, av = A["qT"], A["kT"], A["av"]
        H = lane.hand[par]
        bet = beta_all[:, t, h:h + 1]
        gU, E, Es, Ei, AT = lane.f
        ts("pool", gU, cf("U" + ty), g_all[:, t, h:h + 1], ALU.mult)
        pD = ps_q.next()
        mm(pD, cf("SLT" + ty), gU)
        act(E, pD, AF.Exp)
        pk = ps_q.next()
        pkb = pk.bitcast(BF16)[:, 0:128]
        tr(pkb, kT[:, cs], identb)
        act(lane.kg, pkb, AF.Identity, scale=eG_all[:, t, h:h + 1])
        ts("dve", H["kdec"], pkb, eGr_all[:, t, h:h + 1], ALU.mult)
        yield
        tt("pool", Es, E, cf("MTs" + ty), ALU.mult)
        tt("pool", Ei, E, cf("MTi" + ty), ALU.mult)
        pK = ps_q.next()
        mm(pK, kT[:, cs], kT[:, cs])
        stt(AT, pK, bet, Es, ALU.mult, ALU.mult)
        pv = ps_q.next()
        pvb = pv.bitcast(BF16)[:, 0:128]
        tr(pvb, av[:, cs], identb)
        cp("act", lane.vtok, pvb)
        yield
        atl = lane.atl
        tt("pool", atl[:, 0:L, :], AT.re("p (o i) -> p o i", o=1).bc([128, L, 128]),
           cb("LV" + ty).re("p (l i) -> p l i", l=L), ALU.mult)
        Tt = lane.Tt[0]
        Tm = lane.T[0]
        tt("pool", Tt, identb, atl[:, 0, :], ALU.subtract)
        pY = ps_q.next()
        mm(pY, atl[:, 0, :], identb)
        tt("dve", Tm, identb, pY, ALU.subtract)
        pQ = ps_q.next()
        mm(pQ, kT[:, cs], qT[:, cs])
        tt("dve", H["qkT"], pQ, Ei, ALU.mult)
        yield
        for lv in range(1, L):
            pY = ps_q.next()
            mm(pY, atl[:, lv, :], Tm)
            Yb = lane.Yb[lv % 2]
            cp("act", Yb, pY)
            yield
            pZt = ps_q.next()
            mm(pZt, Yb, Tt)
            Ttn = lane.Tt[lv % 2]
            tt("dve", Ttn, Tt, pZt, ALU.subtract)
            if lv < L - 1:
                pZ = ps_q.next()
                mm(pZ, Tt, Yb)
                Tn = lane.T[lv % 2]
                tt("dve", Tn, Tm, pZ, ALU.subtract)
                Tm = Tn
            Tt = Ttn
            yield
        pX = ps_q.next()
        mm(pX, Tt, lane.vtok)
        act(H["bXv"], pX, AF.Identity, scale=bet)
        pXk = ps_q.next()
        mm(pXk, lane.kg, Tt)
        cp("dve", H["xkT"], pXk)
        yield

    def gdn_R(j, h, g, A, hands):
        o0, n = st_range(g)
        qT, sz = A["qT"], A["sz"]
        ob = ob_rot.next()
        for ti in range(n // 128):
            t = o0 // 128 + ti
            ty = typ_of(t)
            H = hands[ti]
            cs = slice(ti * 128, (ti + 1) * 128)
            nbet = nbeta_all[:, t, h:h + 1]
            qTt = qT[:, cs]
            qkT, kdec, xkT, bXv = H["qkT"], H["kdec"], H["xkT"], H["bXv"]
            o1 = r_o1.next()
            vnew = r_b.next()
            if ty == "p":
                p2 = ps_g.next()
                mm(p2[:, 0:128], xkT, Sb[:, 0:128])
                stt(vnew, p2[:, 0:128], nbet, bXv, ALU.mult, ALU.add)
                pS = ps_q.next()
                mm(pS, kdec, vnew)
                p3 = ps_g.next()
                mm(p3[:, 0:128], qTt, Sb[:, 0:128])
                stt(Sf[:, 0:128], Sf[:, 0:128], eGl_p[:, t, h:h + 1], pS, ALU.mult, ALU.add)
                act(o1, p3[:, 0:128], AF.Identity, scale=eG_all[:, t, h:h + 1])
                cp("act", Sb[:, 0:128], Sf[:, 0:128])
                yield
            else:
                sbs = Sbs.next()
                for q4 in range(4):
                    S.dma("pool", sbs.ap[:, q4 * 4:(q4 + 1) * 4, :], sd_in[j, q4 * 4:(q4 + 1) * 4, h].rearrange("s k v -> k s v"), writes=[sbs.buf])
                p2 = ps_g.next()
                p3 = ps_g.next()
                for gq in range(4):
                    mk = m_rot.next()
                    mq = m_rot.next()
                    cm = cb("colmask")[:, gq * 512:(gq + 1) * 512].re("p (s t) -> p s t", s=4)
                    tt("pool", mk, xkT.re("p (o t) -> p o t", o=1).bc([128, 4, 128]), cm, ALU.mult)
                    tt("pool", mq, qTt.re("p (o t) -> p o t", o=1).bc([128, 4, 128]), cm, ALU.mult)
                    for s4 in range(4):
                        s_ = gq * 4 + s4
                        mm(p2[:, 0:128], mk[:, s4, :], sbs[:, s_, :], start=(s_ == 0), stop=(s_ == 15))
                    for s4 in range(4):
                        s_ = gq * 4 + s4
                        mm(p3[:, 0:128], mq[:, s4, :], sbs[:, s_, :], start=(s_ == 0), stop=(s_ == 15))
                    yield
                stt(vnew, p2[:, 0:128], nbet, bXv, ALU.mult, ALU.add)
                act(o1, p3[:, 0:128], AF.Identity, scale=eG_all[:, t, h:h + 1])
                for gq in range(4):
                    sf = Sfs.next()
                    S.dma("sp", sf.ap[:, :, 0:128], sd_in[j, gq * 4:(gq + 1) * 4, h].rearrange("s k v -> k s v"), writes=[sf.buf])
                    mkd = m_rot.next()
                    tt("pool", mkd, kdec.re("p (o k) -> p o k", o=1).bc([128, 4, 128]),
                       cb("bsel")[:, gq * 4:(gq + 1) * 4].re("p (s o) -> p s o", o=1).bc([128, 4, 128]), ALU.mult)
                    for s4 in range(4):
                        s_ = gq * 4 + s4
                        pS = ps_q.next()
                        mm(pS, mkd[:, s4, :], vnew)
                        stt(sf[:, s4, 0:128], sf[:, s4, 0:128], eGl_s[:, h, s_:s_ + 1], pS, ALU.mult, ALU.add)
                    S.dma("sp", sd_o[j, gq * 4:(gq + 1) * 4, h].rearrange("s k v -> k s v"), sf.ap[:, :, 0:128],
                          reads=[sf.buf], writes=[], owner=sf.buf)
                    yield
            p4 = ps_q.next()
            mm(p4, qkT, vnew)
            tt("dve", o1, o1, p4, ALU.add)
            yield
            sm = sm_rot.next()
            jk = r_b.next()
            act(jk, o1, AF.Square, accum=sm[:, 0:1])
            act(sm[:, 1:2], sm[:, 0:1], AF.Sqrt, scale=1.0 / 128, bias=eps_t)
            recip(sm[:, 1:2], sm[:, 1:2])
            osb = r_b.next()
            act(osb, o1, AF.Identity, scale=sm[:, 1:2])
            yield
            po = ps_q.next()
            pob = po.bitcast(BF16)[:, 0:128]
            tr(pob, osb, identb)
            stt(ob[:, 0, cs], pob, pp("normA", j, 1), sz[:, 0, cs], ALU.mult, ALU.mult)
            yield
        store_o(h, g, ob, ob.ap[:, 0, 0:n])
        if g == 7:
            store("sp", pd_o[j, h], Sf[:, 0:128])
        yield

    def gdn_head(j, h, W, tail_prev=None):
        memset("pool", Sf[:, 0:128], 0.0)
        memset("pool", Sb[:, 0:128], 0.0)
        res = {}
        interleave([tail_prev, gdn_A(j, h, 0, W, res)])
        hands_prev = None
        for g in range(NG):
            o0, n = st_range(g)
            streams = []
            if g + 1 < NG:
                streams.append(gdn_A(j, h, g + 1, W, res))
            hands = []
            for ti in range(n // 128):
                t = o0 // 128 + ti
                streams.append(gdn_prep(j, h, t, ti, res[g], lanes[ti], g % 2))
                hands.append(lanes[ti].hand[g % 2])
            if g >= 1:
                streams.append(gdn_R(j, h, g - 1, res[g - 1], hands_prev))
            interleave(streams)
            hands_prev = hands
        return gdn_R(j, h, NG - 1, res[NG - 1], hands_prev)

    def gla_A(j, hb, g, WA, WB, res):
        o0, n = st_range(g)
        pa = ps_acc.next()
        mm(pa[:, 0:n], wlr_b[:, hb * 128:(hb + 1) * 128], lrT[:, o0:o0 + n])
        e1 = Fp.next()
        act(e1[:, 0:n], pa[:, 0:n], AF.Exp, scale=-1.0, bias=pp("nblr", j * 4 + hb, 1))
        act(e1[:, 0:n], e1[:, 0:n], AF.Ln, bias=one_t)
        yield
        csum = Fp.next()
        if g < 8:
            for ti in range(2):
                scan(csum[:, ti * 128:(ti + 1) * 128], onesf, e1[:, ti * 128:(ti + 1) * 128], 0.0)
        else:
            scan(csum[:, 0:128], cf("notstart_s"), e1[:, 0:128], 0.0)
        EQ = Fp.next()
        EK = Fp.next()
        act(EQ[:, 0:n], csum[:, 0:n], AF.Exp, scale=-1.0 / 16)
        act(EK[:, 0:n], csum[:, 0:n], AF.Exp, scale=1.0 / 16)
        yield
        qgT = Hp.next()
        proj_fm(WA, 0, g, lambda p: stt(qgT[:, 0:n], p, 128.0 ** -0.5, EQ[:, 0:n], ALU.mult, ALU.mult))
        yield
        kg32 = Fp.next()
        proj_fm(WA, 128, g, lambda p: tt("dve", kg32[:, 0:n], p, EK[:, 0:n], ALU.mult))
        yield
        kgT = Hp.next()
        cp("pool", kgT[:, 0:n], kg32[:, 0:n])
        kdT = Hp.next()
        sbk = 128 if g < 8 else 8
        nb = n // sbk
        EQl = EQ[:, 0:n].re("p (b c) -> p b c", c=sbk)[:, :, sbk - 1:sbk]
        tt("pool", kdT[:, 0:n].re("p (b c) -> p b c", c=sbk), kg32[:, 0:n].re("p (b c) -> p b c", c=sbk),
           EQl.bc([128, nb, sbk]), ALU.mult)
        sz = sz_rot.next()
        proj_fm(WB, 0, g, lambda p: act(sz[:, 0, 0:n], p, AF.Silu))
        yield
        proj_fm(WB, 128, g, lambda p: act(sz[:, 1, 0:n], p, AF.Silu))
        res[g] = dict(qgT=qgT, kgT=kgT, kdT=kdT, EQ=EQ, sz=sz)
        yield

    def gla_tiles(j, hb, g, WA, A):
        o0, n = st_range(g)
        qgT, kgT, kdT, EQ, sz = A["qgT"], A["kgT"], A["kdT"], A["EQ"], A["sz"]
        ob = ob_rot.next()
        for ti in range(n // 128):
            t = o0 // 128 + ti
            ty = typ_of(t)
            cs = slice(ti * 128, (ti + 1) * 128)
            pv = ps_h.next()
            xt_ = xn_tile(t)
            for kc in range(16):
                mm(pv, xt_[:, kc, :], WA[:, kc, 256:512], start=(kc == 0), stop=(kc == 15))
            vtok = vt_rot.next()
            cp("act", vtok, pv)
            yield
            pA = ps_q.next()
            mm(pA, kgT[:, cs], qgT[:, cs])
            ATb = b_rot.next()
            tt("dve", ATb, pA, cf("MTi" + ty), ALU.mult)
            yield
            pk = ps_q.next()
            pkb = pk.bitcast(BF16)[:, 0:128]
            tr(pkb, kdT[:, cs], identb)
            kdec = b_rot.next()
            cp("act", kdec, pkb)
            yield
            P = ps_g.next()
            if ty == "p":
                mm(P[:, 0:256], qgT[:, cs], Sb, start=True, stop=False)
                mm(P[:, 0:256], ATb, vtok, start=False, stop=True)
                pS = ps_h.next()
                mm(pS, kdec, vtok)
                stt(Sf, Sf, EQ[:, ti * 128 + 127:ti * 128 + 128], pS, ALU.mult, ALU.add)
                cp("act", Sb, Sf)
                yield
            else:
                first = True
                for gq in range(4):
                    sbs = Sbs.next()
                    sview = sbs.ap.rearrange("p s v -> p (s v)")[:, 0:1024].rearrange("p (s v) -> p s v", s=4)
                    S.dma("pool", sview, sg_in[j, gq * 4:(gq + 1) * 4, hb].rearrange("s k v -> k s v"), writes=[sbs.buf])
                    sv = T(sview, sbs.buf)
                    mq = m_rot.next()
                    cm = cb("colmask")[:, gq * 512:(gq + 1) * 512].re("p (s t) -> p s t", s=4)
                    tt("pool", mq, qgT[:, cs].re("p (o t) -> p o t", o=1).bc([128, 4, 128]), cm, ALU.mult)
                    for s4 in range(4):
                        mm(P[:, 0:256], mq[:, s4, :], sv[:, s4, :], start=first, stop=False)
                        first = False
                    yield
                mm(P[:, 0:256], ATb, vtok, start=False, stop=True)
                for gq in range(4):
                    sf = Sfs.next()
                    S.dma("sp", sf.ap, sg_in[j, gq * 4:(gq + 1) * 4, hb].rearrange("s k v -> k s v"), writes=[sf.buf])
                    mkd = m_rot.next()
                    tt("pool", mkd, kdec.re("p (o k) -> p o k", o=1).bc([128, 4, 128]),
                       cb("bsel")[:, gq * 4:(gq + 1) * 4].re("p (s o) -> p s o", o=1).bc([128, 4, 128]), ALU.mult)
                    for s4 in range(4):
                        s_ = gq * 4 + s4
                        pS = ps_h.next()
                        mm(pS, mkd[:, s4, :], vtok)
                        stt(sf[:, s4, :], sf[:, s4, :], EQ[:, s_ * 8 + 7:s_ * 8 + 8], pS, ALU.mult, ALU.add)
                    S.dma("sp", sg_o[j, gq * 4:(gq + 1) * 4, hb].rearrange("s k v -> k s v"), sf.ap,
                          reads=[sf.buf], writes=[], owner=sf.buf)
                    yield
            sm = sm_rot.next()
            jk = ob16_rot.next()
            act(jk, P[:, 0:256], AF.Square, accum=sm[:, 0:1])
            act(sm[:, 1:2], sm[:, 0:1], AF.Sqrt, scale=1.0 / 256, bias=eps_t)
            recip(sm[:, 1:2], sm[:, 1:2])
            osb = ob16_rot.next()
            act(osb, P[:, 0:256], AF.Identity, scale=sm[:, 1:2])
            yield
            for c in range(2):
                po = ps_q.next()
                pob = po.bitcast(BF16)[:, 0:128]
                tr(pob, osb[:, c * 128:(c + 1) * 128], identb)
                stt(ob[:, c, cs], pob, pp("normB", j * 2 + c, 1), sz[:, c, cs], ALU.mult, ALU.mult)
                yield
        for c in range(2):
            store_o(8 + 2 * hb + c, g, ob, ob.ap[:, c, 0:n])
        if g == 7:
            store("sp", pg_o[j, hb], Sf)
        yield

    def gla_head(j, hb):
        WA = load_w(wab[j, 8 + 2 * hb])
        WB = load_w(wab[j, 9 + 2 * hb], ncols=256)
        memset("pool", Sf, 0.0)
        memset("pool", Sb, 0.0)
        res = {}
        drain(gla_A(j, hb, 0, WA, WB, res))
        for g in range(NG):
            streams = [gla_tiles(j, hb, g, WA, res[g])]
            if g + 1 < NG:
                streams.append(gla_A(j, hb, g + 1, WA, WB, res))
            interleave(streams)

    def lru_layer(j, l):
        lam = pp("lam", j * 16, 16)
        act(c8[:, 0:16], lam, AF.Exp, scale=-1.0)
        act(c8[:, 0:16], c8[:, 0:16], AF.Ln, bias=one_t)
        ts("pool", c8[:, 16:32], c8[:, 0:16], -16.0, ALU.mult)
        ts("pool", c8[:, 0:16], c8[:, 0:16], -8.0, ALU.mult)
        for pair in range(4):
            Ws = [load_w(wlru[j, 2 * pair]), load_w(wlru[j, 2 * pair + 1])]
            interleave([lru_chan(j, 4 * pair + k, Ws[k // 2], (k % 2) * 256, k) for k in range(4)])
        pq = ps_q.next()
        tr(pq[0:16, :], hl_col, identf)
        sg_ = stg_rot.next()
        cp("dve", sg_[0:16, :], pq[0:16, :])
        store("sp", pl_o[j].rearrange("(n p) -> n p", p=128), sg_[0:16, :])

    def lru_chan(j, n, W, c0, ln):
        ext = ext_b["qk"[ln]] if ln < 2 else ext_l[ln - 2]
        fb = Fp.ts[6 * ln:6 * ln + 6] if ln < 2 else FpL[6 * (ln - 2):6 * (ln - 2) + 6]
        hc = hcar2[:, ln:ln + 1]
        csl = slice(n * 128, (n + 1) * 128)
        wa_b = wg_rot.next()
        wx_b = wg_rot.next()
        load("pool", wa_b, lwa[j, n])
        load("pool", wx_b, lwx[j, n])
        memset("pool", hc, 0.0)
        for g in range(NG):
            o0, ntk = st_range(g)
            sg, xc, r, ig, a2, hsb = fb
            proj_fm(W, c0, g, lambda p: fill_ext(slc_in[j, :, csl], ext, g, p))
            yield
            save_conv_state(plc_o[j, :, csl], slc_o[j, :, csl], ext, g)
            proj_fm(W, c0 + 128, g, lambda p: act(sg[:, 0:ntk], p, AF.Silu))
            yield
            wfn = lambda tap: pp("cwL", (j * 16 + n) * 4 + tap, 1)
            cbias = pp("cbL", j * 16 + n, 1)
            if g < 8:
                src3 = lambda tap: ext[:, tap:tap + ntk]
                dst = xc[:, 0:ntk]
            else:
                e3 = ext[:, 0:176].re("p (s c) -> p s c", c=11)
                src3 = lambda tap: e3[:, :, tap:tap + 8]
                dst = xc[:, 0:128].re("p (s c) -> p s c", c=8)
            ts("dve", dst, src3(0), wfn(0), ALU.mult, cbias, ALU.add)
            for tap in range(1, 4):
                stt(dst, src3(tap), wfn(tap), dst, ALU.mult, ALU.add)
            xcb = Hp.next()
            cp("pool", xcb[:, 0:ntk], xc[:, 0:ntk])
            yield
            pr = ps_acc.next()
            mm(pr[:, 0:ntk], wa_b, xcb[:, 0:ntk])
            act(r[:, 0:ntk], pr[:, 0:ntk], AF.Sigmoid, bias=pp("ba", j * 16 + n, 1))
            yield
            pi = ps_acc.next()
            mm(pi[:, 0:ntk], wx_b, xcb[:, 0:ntk])
            act(ig[:, 0:ntk], pi[:, 0:ntk], AF.Sigmoid, bias=pp("bx", j * 16 + n, 1))
            yield
            act(a2[:, 0:ntk], r[:, 0:ntk], AF.Exp, scale=c8[:, 16 + n:17 + n])
            act(a2[:, 0:ntk], a2[:, 0:ntk], AF.Sqrt, scale=-1.0, bias=one_t)
            if g == 0:
                memset("pool", a2[:, 0:1], 1.0)
            a = r
            act(a[:, 0:ntk], r[:, 0:ntk], AF.Exp, scale=c8[:, n:n + 1])
            bx = ig
            tt("pool", bx[:, 0:ntk], xc[:, 0:ntk], ig[:, 0:ntk], ALU.mult)
            tt("pool", bx[:, 0:ntk], bx[:, 0:ntk], a2[:, 0:ntk], ALU.mult)
            yield
            if g < 8:
                scan(hsb[:, 0:ntk], a[:, 0:ntk], bx[:, 0:ntk], hc)
                cp("pool", hc, hsb[:, ST - 1:ST])
                if g == 7:
                    cp("pool", hl_col[:, n:n + 1], hsb[:, ST - 1:ST])
            else:
                h0 = h0_rot.next()
                load("sp", h0, sl_in[j, :, csl])
                pq = ps_q.next()
                tr(pq[:, 0:16], h0, identf[0:16, 0:16])
                h0T = f_rot.next()
                cp("dve", h0T[:, 0:16], pq[:, 0:16])
                a3 = a[:, 0:128].re("p (s c) -> p s c", c=8)
                b3 = bx[:, 0:128].re("p (s c) -> p s c", c=8)
                tmp16 = f_rot.next()
                tt("pool", tmp16[:, 0:16].re("p (s o) -> p s o", o=1), a3[:, :, 0:1], h0T[:, 0:16].re("p (s o) -> p s o", o=1), ALU.mult)
                tt("pool", b3[:, :, 0:1], b3[:, :, 0:1], tmp16[:, 0:16].re("p (s o) -> p s o", o=1), ALU.add)
                am = xc
                tt("pool", am[:, 0:128], a[:, 0:128], cf("notstart_s"), ALU.mult)
                yield
                scan(hsb[:, 0:128], am[:, 0:128], bx[:, 0:128], 0.0)
                t16 = f_rot.next()
                cp("pool", t16[:, 0:16].re("p (s o) -> p s o", o=1), hsb[:, 0:128].re("p (s c) -> p s c", c=8)[:, :, 7:8])
                pq2 = ps_q.next()
                tr(pq2[0:16, :], t16[:, 0:16], identf)
                sg_ = stg_rot.next()
                cp("dve", sg_[0:16, :], pq2[0:16, :])
                store("sp", sl_o[j, :, csl], sg_[0:16, :])
            ob = ob_rot.next()
            tt("pool", ob[:, 0, 0:ntk], hsb[:, 0:ntk], sg[:, 0:ntk], ALU.mult)
            store_o(n, g, ob, ob.ap[:, 0, 0:ntk])
            yield

    for t in range(NT):
        xt = XR.next()
        load("sp", xt, xin[t * 128:(t + 1) * 128, :])
        norm_tile(t, xt, 0)
    nlayers = {0: 0, 1: 1, 2: 1, 3: 1, 4: 2}.get(stage, 4)
    for l in range(nlayers):
        j = l // 2
        S.barrier()
        if l % 2 == 0:
            ab_layer(j, l)
        else:
            lru_layer(j, l)
        S.barrier()
        last = (l == 3)
        if last:
            load("sp", fnw_t, fn_d.partition_broadcast(128))
        if stage >= 99 or l < nlayers - 1:
            phase_O(l, last)
    S.finish()
    S.emit()
    return nc, S


_CACHE = {}


def _prep_weights(inp):
    f = np.float32
    ab_w_in = np.asarray(inp["ab_w_in"], f)
    out = {}

    def grp(mat):
        m = np.zeros((2048, 512), f)
        m[:, :mat.shape[1]] = mat
        return m.reshape(16, 128, 512).transpose(1, 0, 2)

    wab = np.zeros((2, 16, 128, 16, 512), f)
    wsm = np.zeros((2, 128, 16, 32), f)
    for j in range(2):
        W = ab_w_in[j]
        q0, k0, v0 = 0, 1024, 2048
        b0, a0, z0 = 3072, 3080, 3088
        qb0 = 4112
        kb0 = qb0 + 512
        vb0 = kb0 + 512
        lr0 = vb0 + 1024
        zb0 = lr0 + 16
        for h in range(8):
            cols = np.concatenate([np.arange(q0 + h * 128, q0 + (h + 1) * 128), np.arange(k0 + h * 128, k0 + (h + 1) * 128),
                                   np.arange(v0 + h * 128, v0 + (h + 1) * 128), np.arange(z0 + h * 128, z0 + (h + 1) * 128)])
            wab[j, h] = grp(W[:, cols])
        for hb in range(4):
            cols = np.concatenate([np.arange(qb0 + hb * 128, qb0 + (hb + 1) * 128), np.arange(kb0 + hb * 128, kb0 + (hb + 1) * 128),
                                   np.arange(vb0 + hb * 256, vb0 + (hb + 1) * 256)])
            wab[j, 8 + 2 * hb] = grp(W[:, cols])
            wab[j, 9 + 2 * hb] = grp(W[:, zb0 + hb * 256: zb0 + (hb + 1) * 256])
        sm = np.concatenate([W[:, b0:b0 + 8], W[:, a0:a0 + 8], W[:, lr0:lr0 + 16]], axis=1)
        wsm[j] = sm.reshape(16, 128, 32).transpose(1, 0, 2)
    out["wab"] = wab
    out["wsm"] = wsm
    lru_w_in = np.asarray(inp["lru_w_in"], f)
    wl = np.zeros((2, 8, 128, 16, 512), f)
    for j in range(2):
        W = lru_w_in[j]
        for n2 in range(8):
            cols = []
            for n in (2 * n2, 2 * n2 + 1):
                cols += [np.arange(n * 128, (n + 1) * 128), np.arange(2048 + n * 128, 2048 + (n + 1) * 128)]
            wl[j, n2] = grp(W[:, np.concatenate(cols)])
    out["wlru"] = wl
    wo = np.zeros((4, 4, 128, 16, 512), f)
    for l in range(4):
        W = np.asarray(inp["ab_w_out"] if l % 2 == 0 else inp["lru_w_out"], f)[l // 2]
        for c in range(4):
            wo[l, c] = grp(W[:, c * 512:(c + 1) * 512])
    out["wout"] = wo
    out["lwa"] = np.ascontiguousarray(np.asarray(inp["lru_w_a"], f))
    out["lwx"] = np.ascontiguousarray(np.asarray(inp["lru_w_x"], f))
    out["wlr"] = np.ascontiguousarray(np.asarray(inp["ab_gla_w_lr"], f))
    PPO, NPP = _pp_layout()
    PPa = np.zeros((128, NPP), f)

    def put(k, arr):
        arr = np.asarray(arr, f).reshape(128, -1)
        PPa[:, PPO[k]:PPO[k] + arr.shape[1]] = arr

    nw = np.stack([np.asarray(inp["ab_norm"], f)[0], np.asarray(inp["lru_norm"], f)[0],
                   np.asarray(inp["ab_norm"], f)[1], np.asarray(inp["lru_norm"], f)[1]])
    put("nw", _chan(nw, 16))
    cw = np.asarray(inp["ab_conv_w"], f)
    put("cwA", np.moveaxis(_chan(cw, 24), 2, 3))
    put("normA", np.asarray(inp["ab_norm_a"], f).T)
    put("normB", _chan(np.asarray(inp["ab_norm_b"], f), 2))
    put("nblr", -_chan(np.asarray(inp["ab_gla_b_lr"], f), 4))
    cwl = np.asarray(inp["lru_conv_w"], f)
    put("cwL", np.moveaxis(_chan(cwl, 16), 2, 3))
    put("cbL", _chan(np.asarray(inp["lru_conv_b"], f), 16))
    put("ba", _chan(np.asarray(inp["lru_b_a"], f), 16))
    put("bx", _chan(np.asarray(inp["lru_b_x"], f), 16))
    put("lam", _chan(np.asarray(inp["lru_lambda"], f), 16))
    out["pp"] = PPa
    out["br"] = np.concatenate([np.asarray(inp["ab_a_log"], f).reshape(-1), np.asarray(inp["ab_dt_bias"], f).reshape(-1)])[None, :]
    out["fnorm"] = np.asarray(inp["final_norm"], f)[None, :]
    cst = _masks()
    out["cf"], _ = _pack(CF_KEYS, cst)
    out["cb"], _ = _pack(CB_KEYS, cst)
    return out


def run(inputs, stage=99, cores=8, trace=False):
    f = np.float32
    key = stage
    if key not in _CACHE:
        _CACHE[key] = build_program(stage)
    nc, S = _CACHE[key]
    shared = _prep_weights(inputs)
    xp = np.asarray(inputs["x_prompt"], f)
    xs = np.asarray(inputs["x_sample"], f)
    in_maps = []
    for c in range(cores):
        m = dict(shared)
        sl = slice(16 * c, 16 * (c + 1))
        m["xin"] = np.concatenate([xp[c % 4], xs[sl].reshape(128, D)], axis=0)
        m["sd_in"] = np.ascontiguousarray(np.asarray(inputs["state_delta"], f)[:, sl])
        m["sdc_in"] = np.ascontiguousarray(np.asarray(inputs["state_delta_conv"], f)[:, sl]).reshape(2, 48, 3072)
        m["sg_in"] = np.ascontiguousarray(np.asarray(inputs["state_gla"], f)[:, sl])
        m["sl_in"] = np.ascontiguousarray(np.asarray(inputs["state_lru"], f)[:, sl])
        m["slc_in"] = np.ascontiguousarray(np.asarray(inputs["state_lru_conv"], f)[:, sl]).reshape(2, 48, 2048)
        in_maps.append(m)
    res = run_bass_kernel_spmd(nc, in_maps, core_ids=list(range(cores)), trace=trace)
    if trace:
        print('EXEC_TIME_NS', res.exec_time_ns)
        global LAST_RES
        LAST_RES = res
    R = list(res.results)
    while len(R) < 8:
        R.append(R[0])
    y_prompt = np.stack([R[c]["y_o"][:SEQ] for c in range(4)])
    y_sample = np.concatenate([R[c]["y_o"][SEQ:].reshape(16, 8, D) for c in range(8)], axis=0)
    p_delta = np.stack([R[c]["pd_o"] for c in range(4)], axis=1)
    p_dconv = np.stack([R[c]["pdc_o"] for c in range(4)], axis=1)
    p_gla = np.stack([R[c]["pg_o"] for c in range(4)], axis=1)
    p_lru = np.stack([R[c]["pl_o"] for c in range(4)], axis=1)
    p_lconv = np.stack([R[c]["plc_o"] for c in range(4)], axis=1)
    s_delta = np.concatenate([R[c]["sd_o"] for c in range(8)], axis=1)
    s_dconv = np.concatenate([R[c]["sdc_o"].reshape(2, 16, 3, 3072) for c in range(8)], axis=1)
    s_gla = np.concatenate([R[c]["sg_o"] for c in range(8)], axis=1)
    s_lru = np.concatenate([R[c]["sl_o"] for c in range(8)], axis=1)
    s_lconv = np.concatenate([R[c]["slc_o"].reshape(2, 16, 3, 2048) for c in range(8)], axis=1)
    outs = (y_prompt, y_sample, p_delta, p_dconv, p_gla, p_lru, p_lconv, s_delta, s_dconv, s_gla, s_lru, s_lconv)
    return tuple(np.ascontiguousarray(o, dtype=f) for o in outs)


def kernel(**inputs):
    return run(inputs, stage=99)
```

```python
import numpy as np
import concourse.bass as bass
import concourse.mybir as mybir
from concourse.bass_utils import run_bass_kernel_spmd

F32 = mybir.dt.float32
BF16 = mybir.dt.bfloat16
AF = mybir.ActivationFunctionType
ALU = mybir.AluOpType

D = 2048
NT = 17
NTOK = NT * 128
SEQ = 2048
EPS = 1e-6
NSEQ = 16
AB_IN = 7200


class Buf:
    __slots__ = ("name", "lw", "rd", "dsem", "dcount", "excl")

    def __init__(self, name="", excl=False):
        self.name = name
        self.excl = excl
        self.lw = None
        self.rd = []
        self.dsem = None
        self.dcount = 0


class T:
    __slots__ = ("ap", "buf")

    def __init__(self, ap, buf):
        self.ap = ap
        self.buf = buf

    def __getitem__(self, k):
        return T(self.ap[k], self.buf)

    def bc(self, shape):
        return T(self.ap.to_broadcast(list(shape)), self.buf)

    def re(self, pat, **kw):
        return T(self.ap.rearrange(pat, **kw), self.buf)

    def bitcast(self, dt):
        return T(self.ap.bitcast(dt), self.buf)

    def with_buf(self, buf):
        return T(self.ap, buf)


class Sched:
    ENG = ("pe", "act", "dve", "pool", "sp")
    SEM_LIMIT = 30000

    def __init__(self, nc, same_engine_sync=True):
        self.nc = nc
        self.prog = {e: [] for e in self.ENG}
        self.sem = {}
        self.cnt = {}
        self.nsem = 0
        for e in self.ENG:
            self._newsem(e)
        self.waited = {e: {} for e in self.ENG}
        self.ses = same_engine_sync
        self.ninstr = {e: 0 for e in self.ENG}
        self.dbufs = []

    def _alloc(self, name):
        self.nsem += 1
        return self.nc.alloc_semaphore(name=f"{name}_{self.nsem}")

    def _newsem(self, e):
        self.sem[e] = self._alloc("s" + e)
        self.cnt[e] = 0

    def _deps(self, eng, reads, writes):
        toks = []
        for b in reads:
            if b.lw is not None:
                toks.append(b.lw)
        for b in writes:
            if b.lw is not None:
                toks.append(b.lw)
            toks.extend(b.rd)
        wd = self.waited[eng]
        mx = {}
        for (sem, val, te) in toks:
            if te == eng and (eng == "pe" or not self.ses):
                continue
            k = id(sem)
            if wd.get(k, 0) >= val:
                continue
            if k not in mx or mx[k][1] < val:
                mx[k] = (sem, val)
        for k, (sem, val) in mx.items():
            wd[k] = val
        return list(mx.values())

    def _mark(self, tok, reads, writes):
        for b in writes:
            b.lw = tok
            b.rd = []
        wset = set(id(b) for b in writes)
        for b in reads:
            if id(b) not in wset:
                b.rd.append(tok)
                if len(b.rd) > 64:
                    last = {}
                    for t in b.rd:
                        k = id(t[0])
                        if k not in last or last[k][1] < t[1]:
                            last[k] = t
                    b.rd = list(last.values())

    def op(self, eng, fn, reads=(), writes=()):
        writes = [b for b in writes] + [b for b in reads if b.excl]
        reads = [b for b in reads if not b.excl]
        waits = self._deps(eng, reads, writes)
        if self.cnt[eng] >= self.SEM_LIMIT:
            self._newsem(eng)
        self.cnt[eng] += 1
        sem = self.sem[eng]
        tok = (sem, self.cnt[eng], eng)
        self.prog[eng].append((fn, waits, (sem, 1)))
        self._mark(tok, reads, writes)
        self.ninstr[eng] += 1

    def dma(self, q, out, in_, reads=(), writes=(), owner=None):
        reads = list(reads)
        writes = list(writes)
        if owner is None:
            owner = writes[0] if len(writes) else reads[0]
        if owner.dsem is None:
            owner.dsem = self._alloc("d")
            owner.dcount = 0
            self.dbufs.append(owner)
        waits = self._deps(q, reads, writes)
        owner.dcount += 16
        tok = (owner.dsem, owner.dcount, "dma")
        self.prog[q].append((lambda e, o=out, i=in_: e.dma_start(out=o, in_=i), waits, (owner.dsem, 16)))
        self._mark(tok, reads, writes)
        self.ninstr[q] += 1

    def barrier(self):
        toks = [(self.sem[e], self.cnt[e], e) for e in self.ENG if self.cnt[e] > 0]
        toks += [(b.dsem, b.dcount, "dma") for b in self.dbufs]
        for e in self.ENG:
            wd = self.waited[e]
            waits = []
            for sem, val, te in toks:
                if te == e:
                    continue
                if wd.get(id(sem), 0) >= val:
                    continue
                wd[id(sem)] = val
                waits.append((sem, val))
            self.prog[e].append((None, waits, None))

    def finish(self):
        waits = []
        for b in self.dbufs:
            waits.append((b.dsem, b.dcount))
        self.prog["sp"].append((None, waits, None))

    def emit(self):
        nc = self.nc
        with nc.Block() as block:
            def run(eng_name):
                def body(e):
                    for fn, waits, inc in self.prog[eng_name]:
                        for sem, val in waits:
                            e.wait_ge(sem, val)
                        if fn is not None:
                            ins = fn(e)
                            if inc is not None:
                                ins.then_inc(inc[0], inc[1])
                return body
            block.tensor(run("pe"))
            block.scalar(run("act"))
            block.vector(run("dve"))
            block.gpsimd(run("pool"))
            block.sync(run("sp"))


def _masks():
    idx = np.arange(128)
    i = idx[None, :]
    j = idx[:, None]
    c = {}
    for typ, sb in (("p", 128), ("s", 8)):
        same = (i // sb) == (j // sb)
        c["U" + typ] = ((j <= i) & same).astype(np.float32)
        c["SLT" + typ] = ((j > i) & same).astype(np.float32)
        c["MTi" + typ] = ((j <= i) & same).astype(np.float32)
        c["MTs" + typ] = ((j < i) & same).astype(np.float32)
        lv = []
        s = 1
        while 2 * s <= sb:
            m = ((i // (2 * s)) == (j // (2 * s))) & ((i % (2 * s)) >= s) & ((j % (2 * s)) < s)
            lv.append(m.astype(np.float32))
            s *= 2
        c["LV" + typ] = np.concatenate(lv, axis=1)
    c["ident"] = np.eye(128, dtype=np.float32)
    c["ones"] = np.ones((128, 128), np.float32)
    bs = (idx[:, None] // 8 == np.arange(16)[None, :]).astype(np.float32)
    c["bsel"] = bs
    c["colmask"] = np.tile(bs.T.reshape(1, 16 * 128), (128, 1))
    ns = np.ones((128, 128), np.float32)
    ns[:, ::8] = 0.0
    c["notstart_s"] = ns
    npm = np.ones((128, 512), np.float32)
    npm[:, ::128] = 0.0
    c["notstart_p"] = npm
    return c


CF_KEYS = ["ident", "ones", "Up", "SLTp", "Us", "SLTs", "bsel", "MTip", "MTsp", "MTis", "MTss",
           "notstart_s"]
CB_KEYS = ["ident", "MTip", "MTis", "LVp", "LVs", "colmask", "bsel"]


def _pack(keys, c):
    offs = {}
    o = 0
    for k in keys:
        offs[k] = (o, c[k].shape[1])
        o += c[k].shape[1]
    arr = np.concatenate([c[k] for k in keys], axis=1).astype(np.float32)
    return arr, offs


def _pp_layout():
    items = [("nw", 4 * 16), ("cwA", 2 * 24 * 4), ("normA", 2), ("normB", 4), ("nblr", 8),
             ("cwL", 2 * 16 * 4), ("cbL", 32), ("ba", 32), ("bx", 32), ("lam", 32)]
    offs = {}
    o = 0
    for k, w in items:
        offs[k] = o
        o += w
    return offs, o


def _chan(v, nch):
    v = np.asarray(v, np.float32)
    lead = v.shape[:-1]
    v = v.reshape(lead + (nch, 128))
    return np.moveaxis(v, -1, 0)


NG = 9
ST = 256


def st_range(g):
    return (g * ST, ST) if g < 8 else (2048, 128)


def build_program(stage=99):
    nc = bass.Bass("TRN2", target_bir_lowering=False)
    S = Sched(nc)
    cst = _masks()
    _, CFO = _pack(CF_KEYS, cst)
    _, CBO = _pack(CB_KEYS, cst)
    NCF = sum(w for _, w in CFO.values())
    NCB = sum(w for _, w in CBO.values())
    PPO, NPP = _pp_layout()
    uid = [0]

    def dram(name, shape, dt=F32, kind="ExternalInput"):
        return nc.dram_tensor(name, list(shape), dt, kind=kind).ap()

    xin = dram("xin", [NTOK, D])
    sd_in = dram("sd_in", [2, NSEQ, 8, 128, 128])
    sdc_in = dram("sdc_in", [2, 48, 3072])
    sg_in = dram("sg_in", [2, NSEQ, 4, 128, 256])
    sl_in = dram("sl_in", [2, NSEQ, 2048])
    slc_in = dram("slc_in", [2, 48, 2048])
    wab = dram("wab", [2, 16, 128, 16, 512])
    wsm = dram("wsm", [2, 128, 16, 32])
    wlru = dram("wlru", [2, 8, 128, 16, 512])
    wout = dram("wout", [4, 4, 128, 16, 512])
    lwa = dram("lwa", [2, 16, 128, 128])
    lwx = dram("lwx", [2, 16, 128, 128])
    wlr = dram("wlr", [2, 16, 512])
    pp_d = dram("pp", [128, NPP])
    br_d = dram("br", [1, 32])
    fn_d = dram("fnorm", [1, D])
    cf_d = dram("cf", [128, NCF])
    cb_d = dram("cb", [128, NCB])

    y_o = dram("y_o", [NTOK, D], kind="ExternalOutput")
    pd_o = dram("pd_o", [2, 8, 128, 128], kind="ExternalOutput")
    pdc_o = dram("pdc_o", [2, 3, 3072], kind="ExternalOutput")
    pg_o = dram("pg_o", [2, 4, 128, 256], kind="ExternalOutput")
    pl_o = dram("pl_o", [2, 2048], kind="ExternalOutput")
    plc_o = dram("plc_o", [2, 3, 2048], kind="ExternalOutput")
    sd_o = dram("sd_o", [2, NSEQ, 8, 128, 128], kind="ExternalOutput")
    sdc_o = dram("sdc_o", [2, 48, 3072], kind="ExternalOutput")
    sg_o = dram("sg_o", [2, NSEQ, 4, 128, 256], kind="ExternalOutput")
    sl_o = dram("sl_o", [2, NSEQ, 2048], kind="ExternalOutput")
    slc_o = dram("slc_o", [2, 48, 2048], kind="ExternalOutput")
    xres = dram("xres", [NTOK, D], kind="ExternalOutput")
    oscr = dram("oscr", [NT, 128, 16, 128], BF16, kind="ExternalOutput")
    b_xres = [Buf(f"xres{t}") for t in range(NT)]
    b_oscr = [Buf(f"oscr{g}") for g in range(NG)]

    def sb(name, shape, dt=F32):
        uid[0] += 1
        return T(nc.alloc_sbuf_tensor(f"{name}_{uid[0]}", list(shape), dt)[:], Buf(name))

    ARENA_F32 = 18688
    arena = nc.alloc_sbuf_tensor("arena", [128, ARENA_F32], F32)[:]
    ar_off = [0]

    def ar(shape, dt=F32, parts=128):
        n = 1
        for s_ in shape[1:]:
            n *= s_
        words = n if dt == F32 else (n + 1) // 2
        words = (words + 7) // 8 * 8
        o = ar_off[0]
        assert o + words <= ARENA_F32, ("arena overflow", o, words)
        ar_off[0] = o + words
        a = arena[0:parts, o:o + words]
        if dt != F32:
            a = a.bitcast(dt)
        a = a[:, 0:n]
        if len(shape) == 3:
            a = a.rearrange("p (a b) -> p a b", a=shape[1])
        return T(a, Buf("ar"))

    class Rot:
        def __init__(self, shape, dt, n, alloc=None, parts=128):
            if alloc is None:
                self.ts = [ar(shape, dt, parts) for _ in range(n)]
            else:
                self.ts = [alloc(f"rot{i}", shape, dt) for i in range(n)]
            self.i = 0

        def next(self):
            t = self.ts[self.i % len(self.ts)]
            self.i += 1
            return t

    psum_all = nc.alloc_psum_tensor("psum_all", [128, 4096], F32)[:]

    def bank(b):
        return psum_all[:, b * 512:(b + 1) * 512]

    bank_buf = [Buf(f"bank{b}", excl=True) for b in range(8)]

    class PRot:
        def __init__(self, aps):
            self.ts = [T(a, bank_buf[b]) for a, b in aps]
            self.i = 0

        def next(self):
            t = self.ts[self.i % len(self.ts)]
            self.i += 1
            return t

    ps_acc = PRot([(bank(0), 0), (bank(1), 1)])
    ps_q = PRot([(bank(b)[:, q * 128:(q + 1) * 128], b) for q in range(4) for b in (2, 3, 7)])
    ps_h = PRot([(bank(4)[:, 0:256], 4), (bank(4)[:, 256:512], 4)])
    ps_g = PRot([(bank(5), 5), (bank(6), 6)])
    ps_n = PRot([(psum_all[:, 4 * 512:6 * 512], 4), (psum_all[:, 6 * 512:8 * 512], 6)])

    def bufs(*ts_):
        return [t.buf for t in ts_ if isinstance(t, T)]

    def apof(x):
        return x.ap if isinstance(x, T) else x

    def mm(out, lhsT, rhs, start=True, stop=True):
        S.op("pe", lambda e, o=out.ap, l=lhsT.ap, r=rhs.ap, st=start, sp=stop: e.matmul(o, lhsT=l, rhs=r, start=st, stop=sp),
             reads=bufs(lhsT, rhs), writes=bufs(out))

    def tr(out, in_, ident):
        S.op("pe", lambda e, o=out.ap, i=in_.ap, d=ident.ap: e.transpose(out=o, in_=i, identity=d),
             reads=bufs(in_, ident), writes=bufs(out))

    def tt(eng, out, in0, in1, op):
        S.op(eng, lambda e, o=out.ap, a=in0.ap, b=in1.ap, p=op: e.tensor_tensor(out=o, in0=a, in1=b, op=p),
             reads=bufs(in0, in1), writes=bufs(out))

    def ts(eng, out, in0, s1, op0, s2=None, op1=None):
        kw = dict(out=out.ap, in0=in0.ap, scalar1=apof(s1), scalar2=apof(s2), op0=op0)
        if op1 is not None:
            kw["op1"] = op1
        S.op(eng, lambda e, kw=kw: e.tensor_scalar(**kw), reads=bufs(in0, s1, s2), writes=bufs(out))

    def stt(out, in0, scalar, in1, op0, op1):
        S.op("dve", lambda e, o=out.ap, a=in0.ap, s=apof(scalar), b=in1.ap, p0=op0, p1=op1:
             e.scalar_tensor_tensor(out=o, in0=a, scalar=s, in1=b, op0=p0, op1=p1),
             reads=bufs(in0, scalar, in1), writes=bufs(out))

    def act(out, in_, func, scale=None, bias=None, accum=None):
        kw = dict(out=out.ap, in_=in_.ap, func=func)
        if scale is not None:
            kw["scale"] = apof(scale)
        if bias is not None:
            kw["bias"] = apof(bias)
        if accum is not None:
            kw["accum_out"] = accum.ap
        S.op("act", lambda e, kw=kw: e.activation(**kw), reads=bufs(in_, scale, bias),
             writes=bufs(out) + (bufs(accum) if accum is not None else []))

    def cp(eng, out, in_):
        if eng == "act":
            act(out, in_, AF.Copy)
        else:
            S.op(eng, lambda e, o=out.ap, i=in_.ap: e.tensor_copy(out=o, in_=i), reads=bufs(in_), writes=bufs(out))

    def memset(eng, out, val):
        S.op(eng, lambda e, o=out.ap, v=val: e.memset(o, v), writes=bufs(out))

    def recip(out, in_):
        S.op("dve", lambda e, o=out.ap, i=in_.ap: e.reciprocal(out=o, in_=i), reads=bufs(in_), writes=bufs(out))

    def scan(out, d0, d1, init):
        S.op("dve", lambda e, o=out.ap, a=d0.ap, b=d1.ap, i=apof(init):
             e.tensor_tensor_scan(out=o, data0=a, data1=b, initial=i, op0=ALU.mult, op1=ALU.add),
             reads=bufs(d0, d1, init), writes=bufs(out))

    def load(q, out_t, in_ap, extra_reads=()):
        S.dma(q, out_t.ap, in_ap, reads=list(extra_reads), writes=[out_t.buf])

    def store(q, out_ap, in_t, dwrites=()):
        S.dma(q, out_ap, in_t.ap, reads=[in_t.buf], writes=list(dwrites), owner=in_t.buf)

    def store_o(kc, g, ob_t, src_ap):
        o0, n = st_range(g)
        t0, nt = o0 // 128, n // 128
        S.dma("pool", oscr[t0:t0 + nt, :, kc, :].rearrange("t p c -> p t c"), src_ap.rearrange("p (t c) -> p t c", c=128),
              reads=[ob_t.buf], writes=[b_oscr[g]], owner=ob_t.buf)

    CF = sb("CF", [128, NCF])
    CB = sb("CB", [128, NCB], BF16)
    PP = sb("PP", [128, NPP])
    BR = sb("BR", [128, 32])
    load("sp", CF, cf_d)
    for c0 in range(0, NCB, 1024):
        c1 = min(NCB, c0 + 1024)
        S.dma("pool", CB.ap[:, c0:c1], cb_d[:, c0:c1], writes=[CB.buf])
    load("sp", PP, pp_d)
    load("sp", BR, br_d.partition_broadcast(128))

    def cf(k):
        o, w = CFO[k]
        return CF[:, o:o + w]

    def cb(k):
        o, w = CBO[k]
        return CB[:, o:o + w]

    identf = cf("ident")
    identb = cb("ident")
    onesf = cf("ones")
    eps_t = sb("eps", [128, 1])
    memset("pool", eps_t, EPS)
    one_t = sb("one", [128, 1])
    memset("pool", one_t, 1.0)

    def pp(k, off=0, w=1):
        o = PPO[k] + off
        return PP[:, o:o + w]

    xnT_all = nc.alloc_sbuf_tensor("xnT", [128, 16, NTOK], BF16)[:]
    b_xn = [Buf(f"xn{g}") for g in range(NG)]

    def xn_st(g):
        o, n = st_range(g)
        return T(xnT_all[:, :, o:o + n], b_xn[g])

    def xn_tile(t):
        return T(xnT_all[:, :, t * 128:(t + 1) * 128], b_xn[min(t // 2, 8)])

    Wrot = Rot([128, 16, 512], BF16, 2, alloc=sb)

    def load_w(src_ap, ncols=512):
        w = Wrot.next()
        for q4 in range(4):
            S.dma("pool", w.ap[:, q4 * 4:(q4 + 1) * 4, 0:ncols], src_ap[:, q4 * 4:(q4 + 1) * 4, 0:ncols], writes=[w.buf])
        return w

    Sf = sb("Sf", [128, 256])
    Sb = sb("Sb", [128, 256], BF16)
    bas = sb("bas", [128, NT, 16])
    beta_all = sb("beta", [128, NT, 8])
    nbeta_all = sb("nbeta", [128, NT, 8])
    g_all = sb("gall", [128, NT, 8])
    eG_all = sb("eG", [128, NT, 8])
    eGr_all = sb("eGr", [128, NT, 8])
    eGl_p = sb("eGlp", [128, 16, 8])
    eGl_s = sb("eGls", [128, 8, 16])
    gsel = sb("gsel", [128, 8, 16])
    nea = sb("nea", [128, 8])
    lrT = sb("lrT", [16, NTOK], BF16)
    wlr_b = sb("wlrb", [16, 512], BF16)
    wsm_b = sb("wsmb", [128, 16, 32], BF16)
    c8 = sb("c8", [128, 32])
    hcar2 = sb("hcar", [128, 4])
    hl_col = sb("hlcol", [128, 16])

    ar_off[0] = 0
    ext_b = {k: ar([128, ST + 4]) for k in "qkv"}
    Fp = Rot([128, ST], F32, 12)
    Hp = Rot([128, ST], BF16, 12)
    sz_rot = Rot([128, 2, ST], BF16, 4)
    ob_rot = Rot([128, 2, ST], BF16, 3)
    hist_rot = Rot([48, 128], F32, 2, parts=48)
    stg_rot = Rot([48, 128], F32, 2, parts=48)
    f_rot = Rot([128, 128], F32, 7)
    b_rot = Rot([128, 128], BF16, 4)
    m_rot = Rot([128, 4, 128], BF16, 4)
    sm_rot = Rot([128, 8], F32, 4)
    wg_rot = Rot([128, 128], BF16, 8)
    h0_rot = Rot([16, 128], F32, 2, parts=16)
    ab_only_start = ar_off[0]

    class Lane:
        pass

    lanes = []
    for _ln in range(2):
        L_ = Lane()
        L_.f = [ar([128, 128]) for _ in range(5)]
        L_.atl = ar([128, 7, 128], BF16)
        L_.T = [ar([128, 128], BF16) for _ in range(2)]
        L_.Tt = [ar([128, 128], BF16) for _ in range(2)]
        L_.Yb = [ar([128, 128], BF16) for _ in range(2)]
        L_.kg = ar([128, 128], BF16)
        L_.vtok = ar([128, 128], BF16)
        L_.hand = [dict(qkT=ar([128, 128], BF16), kdec=ar([128, 128], BF16), xkT=ar([128, 128], BF16),
                        bXv=ar([128, 128])) for _ in range(2)]
        lanes.append(L_)
    r_o1 = Rot([128, 128], F32, 2)
    r_b = Rot([128, 128], BF16, 6)
    ob16_rot = Rot([128, 256], BF16, 4)
    vt_rot = Rot([128, 256], BF16, 2)
    Sbs = Rot([128, 16, 128], BF16, 1)
    Sfs = Rot([128, 4, 256], F32, 1)
    mixer_top = ar_off[0]
    ar_off[0] = ab_only_start
    FpL = [ar([128, ST]) for _ in range(12)]
    ext_l = [ar([128, ST + 4]) for _ in range(2)]
    assert ar_off[0] <= mixer_top
    ar_off[0] = 0
    XR = Rot([128, D], F32, 2)
    ot_rot = Rot([128, 16, 128], BF16, 2)
    ssq_rot = Rot([128, 8], F32, 4)
    fnw_t = ar([128, D])
    junkB = ar([128, D], BF16)

    def typ_of(t):
        return "s" if t == 16 else "p"

    def norm_tile(t, xt, layer_next):
        sq = ssq_rot.next()
        act(junkB, xt, AF.Square, accum=sq[:, 0:1])
        act(sq[:, 1:2], sq[:, 0:1], AF.Sqrt, scale=1.0 / D, bias=eps_t)
        recip(sq[:, 1:2], sq[:, 1:2])
        if layer_next < 4:
            act(xt, xt, AF.Identity, scale=sq[:, 1:2])
            nw = pp("nw", layer_next * 16, 16)
            dst = xn_tile(t)
            for hf in range(2):
                pn = ps_n.next()
                for k8 in range(8):
                    kc = hf * 8 + k8
                    tr(pn[:, k8 * 128:(k8 + 1) * 128], xt[:, kc * 128:(kc + 1) * 128], identf)
                tt("dve", dst[:, hf * 8:(hf + 1) * 8, :], pn.re("p (k t) -> p k t", k=8),
                   nw[:, hf * 8:(hf + 1) * 8].re("p (k o) -> p k o", o=1).bc([128, 8, 128]), ALU.mult)
        else:
            stt(xt, xt, sq[:, 1:2], fnw_t, ALU.mult, ALU.mult)
            store("pool", y_o[t * 128:(t + 1) * 128, :], xt)

    def phase_O(l, last):
        src = xin if l == 0 else xres
        steps = [(c, t) for c in range(4) for t in range(NT)]
        wgs = {0: load_w(wout[l, 0])}

        def prefetch(idx):
            c, t = steps[idx]
            rows = slice(t * 128, (t + 1) * 128)
            cs_ = slice(c * 512, (c + 1) * 512)
            ot = ot_rot.next()
            S.dma("sp", ot.ap, oscr[t], reads=[b_oscr[min(t // 2, 8)]], writes=[ot.buf])
            xt = XR.next()
            if c < 3:
                S.dma("act", xt.ap[:, cs_], src[rows, cs_], reads=[b_xres[t]], writes=[xt.buf])
            else:
                S.dma("act", xt.ap[:, 0:1536], xres[rows, 0:1536], reads=[b_xres[t]], writes=[xt.buf])
                S.dma("act", xt.ap[:, 1536:2048], src[rows, 1536:2048], reads=[b_xres[t]], writes=[xt.buf])
            return ot, xt

        def compute(idx, ot, xt):
            c, t = steps[idx]
            wg = wgs[c]
            rows = slice(t * 128, (t + 1) * 128)
            cs_ = slice(c * 512, (c + 1) * 512)
            pa = ps_acc.next()
            for ec in range(16):
                mm(pa, ot[:, ec, :], wg[:, ec, :], start=(ec == 0), stop=(ec == 15))
            if t == 0 and c + 1 < 4:
                wgs[c + 1] = load_w(wout[l, c + 1])
            tt("dve", xt[:, cs_], pa, xt[:, cs_], ALU.add)
            if c < 3 or not last:
                S.dma("pool", xres[rows, cs_], xt.ap[:, cs_], reads=[xt.buf], writes=[b_xres[t]], owner=xt.buf)
            if c == 3:
                norm_tile(t, xt, l + 1)

        pend = prefetch(0)
        for idx in range(len(steps)):
            cur = pend
            if idx + 1 < len(steps):
                pend = prefetch(idx + 1)
            compute(idx, *cur)

    def ab_layer(j, l):
        load("pool", wsm_b, wsm[j])
        load("pool", wlr_b, wlr[j])
        for t in range(NT):
            pa = ps_q.next()
            xt_ = xn_tile(t)
            for kc in range(16):
                mm(pa[:, 0:16], xt_[:, kc, :], wsm_b[:, kc, 0:16], start=(kc == 0), stop=(kc == 15))
            cp("act", bas[:, t, :], pa[:, 0:16])
        for g in range(NG):
            o0, n = st_range(g)
            pa = ps_acc.next()
            xg = xn_st(g)
            for kc in range(16):
                mm(pa[0:16, 0:n], wsm_b[:, kc, 16:32], xg[:, kc, :], start=(kc == 0), stop=(kc == 15))
            cp("act", lrT[:, o0:o0 + n], pa[0:16, 0:n])
        act(beta_all, bas[:, :, 0:8], AF.Sigmoid)
        ts("pool", nbeta_all, beta_all, -1.0, ALU.mult)
        act(nea, BR[:, j * 8:(j + 1) * 8], AF.Exp)
        ts("pool", nea, nea, -1.0, ALU.mult)
        tt("dve", g_all, bas[:, :, 8:16], BR[:, 16 + j * 8:16 + (j + 1) * 8].re("p (o h) -> p o h", o=1).bc([128, NT, 8]), ALU.add)
        act(g_all, g_all, AF.Exp)
        act(g_all, g_all, AF.Ln, bias=one_t)
        tt("dve", g_all, g_all, nea.re("p (o h) -> p o h", o=1).bc([128, NT, 8]), ALU.mult)
        for t in range(NT):
            ty = typ_of(t)
            pa = ps_q.next()
            mm(pa[:, 0:8], cf("U" + ty), g_all[:, t, :])
            act(eG_all[:, t, :], pa[:, 0:8], AF.Exp)
            pa = ps_q.next()
            mm(pa[:, 0:8], cf("SLT" + ty), g_all[:, t, :])
            act(eGr_all[:, t, :], pa[:, 0:8], AF.Exp)
            pa = ps_q.next()
            if ty == "p":
                mm(pa[:, 0:8], onesf, g_all[:, t, :])
                act(eGl_p[:, t, :], pa[:, 0:8], AF.Exp)
            else:
                tt("pool", gsel, g_all[:, t, :].re("p (h o) -> p h o", o=1).bc([128, 8, 16]),
                   cf("bsel").re("p (o s) -> p o s", o=1).bc([128, 8, 16]), ALU.mult)
                mm(pa, onesf, gsel.re("p h s -> p (h s)"))
                act(eGl_s.re("p h s -> p (h s)"), pa, AF.Exp)
        if stage >= 2:
            Wn = load_w(wab[j, 0])
            for h in range(8):
                Wc = Wn
                if h + 1 < 8:
                    Wn = load_w(wab[j, h + 1])
                tail = gdn_head(j, h, Wc, tail if h > 0 else None)
            drain(tail)
        if stage >= 3:
            for hb in range(4):
                gla_head(j, hb)

    def conv_feature(wfn, ext, g, bias=None):
        o0, n = st_range(g)
        co = Fp.next()
        if g < 8:
            src3 = lambda tap: ext[:, tap:tap + n]
            dst = co[:, 0:n]
        else:
            e3 = ext[:, 0:176].re("p (s c) -> p s c", c=11)
            src3 = lambda tap: e3[:, :, tap:tap + 8]
            dst = co[:, 0:128].re("p (s c) -> p s c", c=8)
        if bias is None:
            ts("dve", dst, src3(0), wfn(0), ALU.mult)
        else:
            ts("dve", dst, src3(0), wfn(0), ALU.mult, bias, ALU.add)
        for tap in range(1, 4):
            stt(dst, src3(tap), wfn(tap), dst, ALU.mult, ALU.add)
        return co

    def proj_fm(W, c0, g, dst_fn):
        o0, n = st_range(g)
        pa = ps_acc.next()
        xg = xn_st(g)
        for kc in range(16):
            mm(pa[:, 0:n], W[:, kc, c0:c0 + 128], xg[:, kc, :], start=(kc == 0), stop=(kc == 15))
        dst_fn(pa[:, 0:n])

    def fill_ext(hist_src, ext, g, psum_src):
        o0, n = st_range(g)
        if g < 8:
            if g == 0:
                memset("pool", ext[:, 0:3], 0.0)
            else:
                cp("pool", ext[:, 0:3], ext[:, ST:ST + 3])
            cp("act", ext[:, 3:3 + n], psum_src)
        else:
            e3 = ext[:, 0:176].re("p (s c) -> p s c", c=11)
            hs = hist_rot.next()
            load("sp", hs, hist_src)
            pq = ps_q.next()
            tr(pq[:, 0:48], hs, identf[0:48, 0:48])
            cp("dve", e3[:, :, 0:3], pq[:, 0:48].re("p (s c) -> p s c", c=3))
            cp("act", e3[:, :, 3:11], psum_src.re("p (s c) -> p s c", c=8))

    def save_conv_state(dst_p, dst_s, ext, g):
        if g == 7:
            pq = ps_q.next()
            tr(pq[0:3, :], ext[:, ST:ST + 3], identf)
            sg_ = stg_rot.next()
            cp("dve", sg_[0:3, :], pq[0:3, :])
            store("sp", dst_p, sg_[0:3, :])
        elif g == 8:
            e3 = ext[:, 0:176].re("p (s c) -> p s c", c=11)
            tmp = f_rot.next()
            cp("pool", tmp[:, 0:48].re("p (s c) -> p s c", c=3), e3[:, :, 8:11])
            pq = ps_q.next()
            tr(pq[0:48, :], tmp[:, 0:48], identf)
            sg_ = stg_rot.next()
            cp("dve", sg_, pq[0:48, :])
            store("sp", dst_s, sg_)

    def l2norm_fm(src, n):
        sq = Fp.next()
        act(sq[:, 0:n], src[:, 0:n], AF.Square)
        pa = ps_acc.next()
        mm(pa[:, 0:n], onesf, sq[:, 0:n])
        rs = Fp.next()
        act(rs[:, 0:n], pa[:, 0:n], AF.Sqrt, bias=eps_t)
        recip(rs[:, 0:n], rs[:, 0:n])
        return rs

    def interleave(gens):
        gens = [g_ for g_ in gens if g_ is not None]
        while gens:
            for g_ in list(gens):
                try:
                    next(g_)
                except StopIteration:
                    gens.remove(g_)

    def drain(gen):
        for _ in gen:
            pass

    def gdn_A(j, h, g, W, res):
        o0, n = st_range(g)
        for ci, k in enumerate("qkv"):
            chunk = ci * 8 + h
            csl = slice(chunk * 128, (chunk + 1) * 128)
            proj_fm(W, ci * 128, g, lambda p, k=k, csl=csl: fill_ext(sdc_in[j, :, csl], ext_b[k], g, p))
            yield
            save_conv_state(pdc_o[j, :, csl], sdc_o[j, :, csl], ext_b[k], g)
        sz = sz_rot.next()
        proj_fm(W, 384, g, lambda p: act(sz[:, 0, 0:n], p, AF.Silu))
        yield
        acts = {}
        for ci, k in enumerate("qkv"):
            chunk = ci * 8 + h
            co = conv_feature(lambda tap, chunk=chunk: pp("cwA", (j * 24 + chunk) * 4 + tap, 1), ext_b[k], g)
            a_ = Hp.next() if k == "v" else Fp.next()
            act(a_[:, 0:n], co[:, 0:n], AF.Silu)
            acts[k] = a_
            yield
        rsq = l2norm_fm(acts["q"], n)
        qT = Hp.next()
        stt(qT[:, 0:n], acts["q"][:, 0:n], 128.0 ** -0.5, rsq[:, 0:n], ALU.mult, ALU.mult)
        yield
        rsk = l2norm_fm(acts["k"], n)
        kT = Hp.next()
        tt("pool", kT[:, 0:n], acts["k"][:, 0:n], rsk[:, 0:n], ALU.mult)
        res[g] = dict(qT=qT, kT=kT, av=acts["v"], sz=sz)
        yield

    def gdn_prep(j, h, t, ti, A, lane, par):
        ty = typ_of(t)
        L = 7 if ty == "p" else 3
        cs = slice(ti * 128, (ti + 1) * 128)
        qT, kT, av = A["qT"], A["kT"], A["av"]
        H = lane.hand[par]
        bet = beta_all[:, t, h:h + 1]
        gU, E, Es, Ei, AT = lane.f
        ts("pool", gU, cf("U" + ty), g_all[:, t, h:h + 1], ALU.mult)
        pD = ps_q.next()
        mm(pD, cf("SLT" + ty), gU)
        act(E, pD, AF.Exp)
        pk = ps_q.next()
        pkb = pk.bitcast(BF16)[:, 0:128]
        tr(pkb, kT[:, cs], identb)
        act(lane.kg, pkb, AF.Identity, scale=eG_all[:, t, h:h + 1])
        ts("dve", H["kdec"], pkb, eGr_all[:, t, h:h + 1], ALU.mult)
        yield
        tt("pool", Es, E, cf("MTs" + ty), ALU.mult)
        tt("pool", Ei, E, cf("MTi" + ty), ALU.mult)
        pK = ps_q.next()
        mm(pK, kT[:, cs], kT[:, cs])
        stt(AT, pK, bet, Es, ALU.mult, ALU.mult)
        pv = ps_q.next()
        pvb = pv.bitcast(BF16)[:, 0:128]
        tr(pvb, av[:, cs], identb)
        cp("act", lane.vtok, pvb)
        yield
        atl = lane.atl
        tt("pool", atl[:, 0:L, :], AT.re("p (o i) -> p o i", o=1).bc([128, L, 128]),
           cb("LV" + ty).re("p (l i) -> p l i", l=L), ALU.mult)
        Tt = lane.Tt[0]
        Tm = lane.T[0]
        tt("pool", Tt, identb, atl[:, 0, :], ALU.subtract)
        pY = ps_q.next()
        mm(pY, atl[:, 0, :], identb)
        tt("dve", Tm, identb, pY, ALU.subtract)
        pQ = ps_q.next()
        mm(pQ, kT[:, cs], qT[:, cs])
        tt("dve", H["qkT"], pQ, Ei, ALU.mult)
        yield
        for lv in range(1, L):
            pY = ps_q.next()
            mm(pY, atl[:, lv, :], Tm)
            Yb = lane.Yb[lv % 2]
            cp("act", Yb, pY)
            yield
            pZt = ps_q.next()
            mm(pZt, Yb, Tt)
            Ttn = lane.Tt[lv % 2]
            tt("dve", Ttn, Tt, pZt, ALU.subtract)
            if lv < L - 1:
                pZ = ps_q.next()
                mm(pZ, Tt, Yb)
                Tn = lane.T[lv % 2]
                tt("dve", Tn, Tm, pZ, ALU.subtract)
                Tm = Tn
            Tt = Ttn
            yield
        pX = ps_q.next()
        mm(pX, Tt, lane.vtok)
        act(H["bXv"], pX, AF.Identity, scale=bet)
        pXk = ps_q.next()
        mm(pXk, lane.kg, Tt)
        cp("dve", H["xkT"], pXk)
        yield

    def gdn_R(j, h, g, A, hands):
        o0, n = st_range(g)
        qT, sz = A["qT"], A["sz"]
        ob = ob_rot.next()
        for ti in range(n // 128):
            t = o0 // 128 + ti
            ty = typ_of(t)
            H = hands[ti]
            cs = slice(ti * 128, (ti + 1) * 128)
            nbet = nbeta_all[:, t, h:h + 1]
            qTt = qT[:, cs]
            qkT, kdec, xkT, bXv = H["qkT"], H["kdec"], H["xkT"], H["bXv"]
            o1 = r_o1.next()
            vnew = r_b.next()
            if ty == "p":
                p2 = ps_g.next()
                mm(p2[:, 0:128], xkT, Sb[:, 0:128])
                stt(vnew, p2[:, 0:128], nbet, bXv, ALU.mult, ALU.add)
                pS = ps_q.next()
                mm(pS, kdec, vnew)
                p3 = ps_g.next()
                mm(p3[:, 0:128], qTt, Sb[:, 0:128])
                stt(Sf[:, 0:128], Sf[:, 0:128], eGl_p[:, t, h:h + 1], pS, ALU.mult, ALU.add)
                act(o1, p3[:, 0:128], AF.Identity, scale=eG_all[:, t, h:h + 1])
                cp("act", Sb[:, 0:128], Sf[:, 0:128])
                yield
            else:
                sbs = Sbs.next()
                for q4 in range(4):
                    S.dma("pool", sbs.ap[:, q4 * 4:(q4 + 1) * 4, :], sd_in[j, q4 * 4:(q4 + 1) * 4, h].rearrange("s k v -> k s v"), writes=[sbs.buf])
                p2 = ps_g.next()
                p3 = ps_g.next()
                for gq in range(4):
                    mk = m_rot.next()
                    mq = m_rot.next()
                    cm = cb("colmask")[:, gq * 512:(gq + 1) * 512].re("p (s t) -> p s t", s=4)
                    tt("pool", mk, xkT.re("p (o t) -> p o t", o=1).bc([128, 4, 128]), cm, ALU.mult)
                    tt("pool", mq, qTt.re("p (o t) -> p o t", o=1).bc([128, 4, 128]), cm, ALU.mult)
                    for s4 in range(4):
                        s_ = gq * 4 + s4
                        mm(p2[:, 0:128], mk[:, s4, :], sbs[:, s_, :], start=(s_ == 0), stop=(s_ == 15))
                    for s4 in range(4):
                        s_ = gq * 4 + s4
                        mm(p3[:, 0:128], mq[:, s4, :], sbs[:, s_, :], start=(s_ == 0), stop=(s_ == 15))
                    yield
                stt(vnew, p2[:, 0:128], nbet, bXv, ALU.mult, ALU.add)
                act(o1, p3[:, 0:128], AF.Identity, scale=eG_all[:, t, h:h + 1])
                for gq in range(4):
                    sf = Sfs.next()
                    S.dma("sp", sf.ap[:, :, 0:128], sd_in[j, gq * 4:(gq + 1) * 4, h].rearrange("s k v -> k s v"), writes=[sf.buf])
                    mkd = m_rot.next()
                    tt("pool", mkd, kdec.re("p (o k) -> p o k", o=1).bc([128, 4, 128]),
                       cb("bsel")[:, gq * 4:(gq + 1) * 4].re("p (s o) -> p s o", o=1).bc([128, 4, 128]), ALU.mult)
                    for s4 in range(4):
                        s_ = gq * 4 + s4
                        pS = ps_q.next()
                        mm(pS, mkd[:, s4, :], vnew)
                        stt(sf[:, s4, 0:128], sf[:, s4, 0:128], eGl_s[:, h, s_:s_ + 1], pS, ALU.mult, ALU.add)
                    S.dma("sp", sd_o[j, gq * 4:(gq + 1) * 4, h].rearrange("s k v -> k s v"), sf.ap[:, :, 0:128],
                          reads=[sf.buf], writes=[], owner=sf.buf)
                    yield
            p4 = ps_q.next()
            mm(p4, qkT, vnew)
            tt("dve", o1, o1, p4, ALU.add)
            yield
            sm = sm_rot.next()
            jk = r_b.next()
            act(jk, o1, AF.Square, accum=sm[:, 0:1])
            act(sm[:, 1:2], sm[:, 0:1], AF.Sqrt, scale=1.0 / 128, bias=eps_t)
            recip(sm[:, 1:2], sm[:, 1:2])
            osb = r_b.next()
            act(osb, o1, AF.Identity, scale=sm[:, 1:2])
            yield
            po = ps_q.next()
            pob = po.bitcast(BF16)[:, 0:128]
            tr(pob, osb, identb)
            stt(ob[:, 0, cs], pob, pp("normA", j, 1), sz[:, 0, cs], ALU.mult, ALU.mult)
            yield
        store_o(h, g, ob, ob.ap[:, 0, 0:n])
        if g == 7:
            store("sp", pd_o[j, h], Sf[:, 0:128])
        yield

    def gdn_head(j, h, W, tail_prev=None):
        memset("pool", Sf[:, 0:128], 0.0)
        memset("pool", Sb[:, 0:128], 0.0)
        res = {}
        interleave([tail_prev, gdn_A(j, h, 0, W, res)])
        hands_prev = None
        for g in range(NG):
            o0, n = st_range(g)
            streams = []
            if g + 1 < NG:
                streams.append(gdn_A(j, h, g + 1, W, res))
            hands = []
            for ti in range(n // 128):
                t = o0 // 128 + ti
                streams.append(gdn_prep(j, h, t, ti, res[g], lanes[ti], g % 2))
                hands.append(lanes[ti].hand[g % 2])
            if g >= 1:
                streams.append(gdn_R(j, h, g - 1, res[g - 1], hands_prev))
            interleave(streams)
            hands_prev = hands
        return gdn_R(j, h, NG - 1, res[NG - 1], hands_prev)

    def gla_A(j, hb, g, WA, WB, res):
        o0, n = st_range(g)
        pa = ps_acc.next()
        mm(pa[:, 0:n], wlr_b[:, hb * 128:(hb + 1) * 128], lrT[:, o0:o0 + n])
        e1 = Fp.next()
        act(e1[:, 0:n], pa[:, 0:n], AF.Exp, scale=-1.0, bias=pp("nblr", j * 4 + hb, 1))
        act(e1[:, 0:n], e1[:, 0:n], AF.Ln, bias=one_t)
        yield
        csum = Fp.next()
        if g < 8:
            for ti in range(2):
                scan(csum[:, ti * 128:(ti + 1) * 128], onesf, e1[:, ti * 128:(ti + 1) * 128], 0.0)
        else:
            scan(csum[:, 0:128], cf("notstart_s"), e1[:, 0:128], 0.0)
        EQ = Fp.next()
        EK = Fp.next()
        act(EQ[:, 0:n], csum[:, 0:n], AF.Exp, scale=-1.0 / 16)
        act(EK[:, 0:n], csum[:, 0:n], AF.Exp, scale=1.0 / 16)
        yield
        qgT = Hp.next()
        proj_fm(WA, 0, g, lambda p: stt(qgT[:, 0:n], p, 128.0 ** -0.5, EQ[:, 0:n], ALU.mult, ALU.mult))
        yield
        kg32 = Fp.next()
        proj_fm(WA, 128, g, lambda p: tt("dve", kg32[:, 0:n], p, EK[:, 0:n], ALU.mult))
        yield
        kgT = Hp.next()
        cp("pool", kgT[:, 0:n], kg32[:, 0:n])
        kdT = Hp.next()
        sbk = 128 if g < 8 else 8
        nb = n // sbk
        EQl = EQ[:, 0:n].re("p (b c) -> p b c", c=sbk)[:, :, sbk - 1:sbk]
        tt("pool", kdT[:, 0:n].re("p (b c) -> p b c", c=sbk), kg32[:, 0:n].re("p (b c) -> p b c", c=sbk),
           EQl.bc([128, nb, sbk]), ALU.mult)
        sz = sz_rot.next()
        proj_fm(WB, 0, g, lambda p: act(sz[:, 0, 0:n], p, AF.Silu))
        yield
        proj_fm(WB, 128, g, lambda p: act(sz[:, 1, 0:n], p, AF.Silu))
        res[g] = dict(qgT=qgT, kgT=kgT, kdT=kdT, EQ=EQ, sz=sz)
        yield

    def gla_tiles(j, hb, g, WA, A):
        o0, n = st_range(g)
        qgT, kgT, kdT, EQ, sz = A["qgT"], A["kgT"], A["kdT"], A["EQ"], A["sz"]
        ob = ob_rot.next()
        for ti in range(n // 128):
            t = o0 // 128 + ti
            ty = typ_of(t)
            cs = slice(ti * 128, (ti + 1) * 128)
            pv = ps_h.next()
            xt_ = xn_tile(t)
            for kc in range(16):
                mm(pv, xt_[:, kc, :], WA[:, kc, 256:512], start=(kc == 0), stop=(kc == 15))
            vtok = vt_rot.next()
            cp("act", vtok, pv)
            yield
            pA = ps_q.next()
            mm(pA, kgT[:, cs], qgT[:, cs])
            ATb = b_rot.next()
            tt("dve", ATb, pA, cf("MTi" + ty), ALU.mult)
            yield
            pk = ps_q.next()
            pkb = pk.bitcast(BF16)[:, 0:128]
            tr(pkb, kdT[:, cs], identb)
            kdec = b_rot.next()
            cp("act", kdec, pkb)
            yield
            P = ps_g.next()
            if ty == "p":
                mm(P[:, 0:256], qgT[:, cs], Sb, start=True, stop=False)
                mm(P[:, 0:256], ATb, vtok, start=False, stop=True)
                pS = ps_h.next()
                mm(pS, kdec, vtok)
                stt(Sf, Sf, EQ[:, ti * 128 + 127:ti * 128 + 128], pS, ALU.mult, ALU.add)
                cp("act", Sb, Sf)
                yield
            else:
                first = True
                for gq in range(4):
                    sbs = Sbs.next()
                    sview = sbs.ap.rearrange("p s v -> p (s v)")[:, 0:1024].rearrange("p (s v) -> p s v", s=4)
                    S.dma("pool", sview, sg_in[j, gq * 4:(gq + 1) * 4, hb].rearrange("s k v -> k s v"), writes=[sbs.buf])
                    sv = T(sview, sbs.buf)
                    mq = m_rot.next()
                    cm = cb("colmask")[:, gq * 512:(gq + 1) * 512].re("p (s t) -> p s t", s=4)
                    tt("pool", mq, qgT[:, cs].re("p (o t) -> p o t", o=1).bc([128, 4, 128]), cm, ALU.mult)
                    for s4 in range(4):
                        mm(P[:, 0:256], mq[:, s4, :], sv[:, s4, :], start=first, stop=False)
                        first = False
                    yield
                mm(P[:, 0:256], ATb, vtok, start=False, stop=True)
                for gq in range(4):
                    sf = Sfs.next()
                    S.dma("sp", sf.ap, sg_in[j, gq * 4:(gq + 1) * 4, hb].rearrange("s k v -> k s v"), writes=[sf.buf])
                    mkd = m_rot.next()
                    tt("pool", mkd, kdec.re("p (o k) -> p o k", o=1).bc([128, 4, 128]),
                       cb("bsel")[:, gq * 4:(gq + 1) * 4].re("p (s o) -> p s o", o=1).bc([128, 4, 128]), ALU.mult)
                    for s4 in range(4):
                        s_ = gq * 4 + s4
                        pS = ps_h.next()
                        mm(pS, mkd[:, s4, :], vtok)
                        stt(sf[:, s4, :], sf[:, s4, :], EQ[:, s_ * 8 + 7:s_ * 8 + 8], pS, ALU.mult, ALU.add)
                    S.dma("sp", sg_o[j, gq * 4:(gq + 1) * 4, hb].rearrange("s k v -> k s v"), sf.ap,
                          reads=[sf.buf], writes=[], owner=sf.buf)
                    yield
            sm = sm_rot.next()
            jk = ob16_rot.next()
            act(jk, P[:, 0:256], AF.Square, accum=sm[:, 0:1])
            act(sm[:, 1:2], sm[:, 0:1], AF.Sqrt, scale=1.0 / 256, bias=eps_t)
            recip(sm[:, 1:2], sm[:, 1:2])
            osb = ob16_rot.next()
            act(osb, P[:, 0:256], AF.Identity, scale=sm[:, 1:2])
            yield
            for c in range(2):
                po = ps_q.next()
                pob = po.bitcast(BF16)[:, 0:128]
                tr(pob, osb[:, c * 128:(c + 1) * 128], identb)
                stt(ob[:, c, cs], pob, pp("normB", j * 2 + c, 1), sz[:, c, cs], ALU.mult, ALU.mult)
                yield
        for c in range(2):
            store_o(8 + 2 * hb + c, g, ob, ob.ap[:, c, 0:n])
        if g == 7:
            store("sp", pg_o[j, hb], Sf)
        yield

    def gla_head(j, hb):
        WA = load_w(wab[j, 8 + 2 * hb])
        WB = load_w(wab[j, 9 + 2 * hb], ncols=256)
        memset("pool", Sf, 0.0)
        memset("pool", Sb, 0.0)
        res = {}
        drain(gla_A(j, hb, 0, WA, WB, res))
        for g in range(NG):
            streams = [gla_tiles(j, hb, g, WA, res[g])]
            if g + 1 < NG:
                streams.append(gla_A(j, hb, g + 1, WA, WB, res))
            interleave(streams)

    def lru_layer(j, l):
        lam = pp("lam", j * 16, 16)
        act(c8[:, 0:16], lam, AF.Exp, scale=-1.0)
        act(c8[:, 0:16], c8[:, 0:16], AF.Ln, bias=one_t)
        ts("pool", c8[:, 16:32], c8[:, 0:16], -16.0, ALU.mult)
        ts("pool", c8[:, 0:16], c8[:, 0:16], -8.0, ALU.mult)
        for pair in range(4):
            Ws = [load_w(wlru[j, 2 * pair]), load_w(wlru[j, 2 * pair + 1])]
            interleave([lru_chan(j, 4 * pair + k, Ws[k // 2], (k % 2) * 256, k) for k in range(4)])
        pq = ps_q.next()
        tr(pq[0:16, :], hl_col, identf)
        sg_ = stg_rot.next()
        cp("dve", sg_[0:16, :], pq[0:16, :])
        store("sp", pl_o[j].rearrange("(n p) -> n p", p=128), sg_[0:16, :])

    def lru_chan(j, n, W, c0, ln):
        ext = ext_b["qk"[ln]] if ln < 2 else ext_l[ln - 2]
        fb = Fp.ts[6 * ln:6 * ln + 6] if ln < 2 else FpL[6 * (ln - 2):6 * (ln - 2) + 6]
        hc = hcar2[:, ln:ln + 1]
        csl = slice(n * 128, (n + 1) * 128)
        wa_b = wg_rot.next()
        wx_b = wg_rot.next()
        load("pool", wa_b, lwa[j, n])
        load("pool", wx_b, lwx[j, n])
        memset("pool", hc, 0.0)
        for g in range(NG):
            o0, ntk = st_range(g)
            sg, xc, r, ig, a2, hsb = fb
            proj_fm(W, c0, g, lambda p: fill_ext(slc_in[j, :, csl], ext, g, p))
            yield
            save_conv_state(plc_o[j, :, csl], slc_o[j, :, csl], ext, g)
            proj_fm(W, c0 + 128, g, lambda p: act(sg[:, 0:ntk], p, AF.Silu))
            yield
            wfn = lambda tap: pp("cwL", (j * 16 + n) * 4 + tap, 1)
            cbias = pp("cbL", j * 16 + n, 1)
            if g < 8:
                src3 = lambda tap: ext[:, tap:tap + ntk]
                dst = xc[:, 0:ntk]
            else:
                e3 = ext[:, 0:176].re("p (s c) -> p s c", c=11)
                src3 = lambda tap: e3[:, :, tap:tap + 8]
                dst = xc[:, 0:128].re("p (s c) -> p s c", c=8)
            ts("dve", dst, src3(0), wfn(0), ALU.mult, cbias, ALU.add)
            for tap in range(1, 4):
                stt(dst, src3(tap), wfn(tap), dst, ALU.mult, ALU.add)
            xcb = Hp.next()
            cp("pool", xcb[:, 0:ntk], xc[:, 0:ntk])
            yield
            pr = ps_acc.next()
            mm(pr[:, 0:ntk], wa_b, xcb[:, 0:ntk])
            act(r[:, 0:ntk], pr[:, 0:ntk], AF.Sigmoid, bias=pp("ba", j * 16 + n, 1))
            yield
            pi = ps_acc.next()
            mm(pi[:, 0:ntk], wx_b, xcb[:, 0:ntk])
            act(ig[:, 0:ntk], pi[:, 0:ntk], AF.Sigmoid, bias=pp("bx", j * 16 + n, 1))
            yield
            act(a2[:, 0:ntk], r[:, 0:ntk], AF.Exp, scale=c8[:, 16 + n:17 + n])
            act(a2[:, 0:ntk], a2[:, 0:ntk], AF.Sqrt, scale=-1.0, bias=one_t)
            if g == 0:
                memset("pool", a2[:, 0:1], 1.0)
            a = r
            act(a[:, 0:ntk], r[:, 0:ntk], AF.Exp, scale=c8[:, n:n + 1])
            bx = ig
            tt("pool", bx[:, 0:ntk], xc[:, 0:ntk], ig[:, 0:ntk], ALU.mult)
            tt("pool", bx[:, 0:ntk], bx[:, 0:ntk], a2[:, 0:ntk], ALU.mult)
            yield
            if g < 8:
                scan(hsb[:, 0:ntk], a[:, 0:ntk], bx[:, 0:ntk], hc)
                cp("pool", hc, hsb[:, ST - 1:ST])
                if g == 7:
                    cp("pool", hl_col[:, n:n + 1], hsb[:, ST - 1:ST])
            else:
                h0 = h0_rot.next()
                load("sp", h0, sl_in[j, :, csl])
                pq = ps_q.next()
                tr(pq[:, 0:16], h0, identf[0:16, 0:16])
                h0T = f_rot.next()
                cp("dve", h0T[:, 0:16], pq[:, 0:16])
                a3 = a[:, 0:128].re("p (s c) -> p s c", c=8)
                b3 = bx[:, 0:128].re("p (s c) -> p s c", c=8)
                tmp16 = f_rot.next()
                tt("pool", tmp16[:, 0:16].re("p (s o) -> p s o", o=1), a3[:, :, 0:1], h0T[:, 0:16].re("p (s o) -> p s o", o=1), ALU.mult)
                tt("pool", b3[:, :, 0:1], b3[:, :, 0:1], tmp16[:, 0:16].re("p (s o) -> p s o", o=1), ALU.add)
                am = xc
                tt("pool", am[:, 0:128], a[:, 0:128], cf("notstart_s"), ALU.mult)
                yield
                scan(hsb[:, 0:128], am[:, 0:128], bx[:, 0:128], 0.0)
                t16 = f_rot.next()
                cp("pool", t16[:, 0:16].re("p (s o) -> p s o", o=1), hsb[:, 0:128].re("p (s c) -> p s c", c=8)[:, :, 7:8])
                pq2 = ps_q.next()
                tr(pq2[0:16, :], t16[:, 0:16], identf)
                sg_ = stg_rot.next()
                cp("dve", sg_[0:16, :], pq2[0:16, :])
                store("sp", sl_o[j, :, csl], sg_[0:16, :])
            ob = ob_rot.next()
            tt("pool", ob[:, 0, 0:ntk], hsb[:, 0:ntk], sg[:, 0:ntk], ALU.mult)
            store_o(n, g, ob, ob.ap[:, 0, 0:ntk])
            yield

    for t in range(NT):
        xt = XR.next()
        load("sp", xt, xin[t * 128:(t + 1) * 128, :])
        norm_tile(t, xt, 0)
    nlayers = {0: 0, 1: 1, 2: 1, 3: 1, 4: 2}.get(stage, 4)
    for l in range(nlayers):
        j = l // 2
        S.barrier()
        if l % 2 == 0:
            ab_layer(j, l)
        else:
            lru_layer(j, l)
        S.barrier()
        last = (l == 3)
        if last:
            load("sp", fnw_t, fn_d.partition_broadcast(128))
        if stage >= 99 or l < nlayers - 1:
            phase_O(l, last)
    S.finish()
    S.emit()
    return nc, S


_CACHE = {}


def _prep_weights(inp):
    f = np.float32
    ab_w_in = np.asarray(inp["ab_w_in"], f)
    out = {}

    def grp(mat):
        m = np.zeros((2048, 512), f)
        m[:, :mat.shape[1]] = mat
        return m.reshape(16, 128, 512).transpose(1, 0, 2)

    wab = np.zeros((2, 16, 128, 16, 512), f)
    wsm = np.zeros((2, 128, 16, 32), f)
    for j in range(2):
        W = ab_w_in[j]
        q0, k0, v0 = 0, 1024, 2048
        b0, a0, z0 = 3072, 3080, 3088
        qb0 = 4112
        kb0 = qb0 + 512
        vb0 = kb0 + 512
        lr0 = vb0 + 1024
        zb0 = lr0 + 16
        for h in range(8):
            cols = np.concatenate([np.arange(q0 + h * 128, q0 + (h + 1) * 128), np.arange(k0 + h * 128, k0 + (h + 1) * 128),
                                   np.arange(v0 + h * 128, v0 + (h + 1) * 128), np.arange(z0 + h * 128, z0 + (h + 1) * 128)])
            wab[j, h] = grp(W[:, cols])
        for hb in range(4):
            cols = np.concatenate([np.arange(qb0 + hb * 128, qb0 + (hb + 1) * 128), np.arange(kb0 + hb * 128, kb0 + (hb + 1) * 128),
                                   np.arange(vb0 + hb * 256, vb0 + (hb + 1) * 256)])
            wab[j, 8 + 2 * hb] = grp(W[:, cols])
            wab[j, 9 + 2 * hb] = grp(W[:, zb0 + hb * 256: zb0 + (hb + 1) * 256])
        sm = np.concatenate([W[:, b0:b0 + 8], W[:, a0:a0 + 8], W[:, lr0:lr0 + 16]], axis=1)
        wsm[j] = sm.reshape(16, 128, 32).transpose(1, 0, 2)
    out["wab"] = wab
    out["wsm"] = wsm
    lru_w_in = np.asarray(inp["lru_w_in"], f)
    wl = np.zeros((2, 8, 128, 16, 512), f)
    for j in range(2):
        W = lru_w_in[j]
        for n2 in range(8):
            cols = []
            for n in (2 * n2, 2 * n2 + 1):
                cols += [np.arange(n * 128, (n + 1) * 128), np.arange(2048 + n * 128, 2048 + (n + 1) * 128)]
            wl[j, n2] = grp(W[:, np.concatenate(cols)])
    out["wlru"] = wl
    wo = np.zeros((4, 4, 128, 16, 512), f)
    for l in range(4):
        W = np.asarray(inp["ab_w_out"] if l % 2 == 0 else inp["lru_w_out"], f)[l // 2]
        for c in range(4):
            wo[l, c] = grp(W[:, c * 512:(c + 1) * 512])
    out["wout"] = wo
    out["lwa"] = np.ascontiguousarray(np.asarray(inp["lru_w_a"], f))
    out["lwx"] = np.ascontiguousarray(np.asarray(inp["lru_w_x"], f))
    out["wlr"] = np.ascontiguousarray(np.asarray(inp["ab_gla_w_lr"], f))
    PPO, NPP = _pp_layout()
    PPa = np.zeros((128, NPP), f)

    def put(k, arr):
        arr = np.asarray(arr, f).reshape(128, -1)
        PPa[:, PPO[k]:PPO[k] + arr.shape[1]] = arr

    nw = np.stack([np.asarray(inp["ab_norm"], f)[0], np.asarray(inp["lru_norm"], f)[0],
                   np.asarray(inp["ab_norm"], f)[1], np.asarray(inp["lru_norm"], f)[1]])
    put("nw", _chan(nw, 16))
    cw = np.asarray(inp["ab_conv_w"], f)
    put("cwA", np.moveaxis(_chan(cw, 24), 2, 3))
    put("normA", np.asarray(inp["ab_norm_a"], f).T)
    put("normB", _chan(np.asarray(inp["ab_norm_b"], f), 2))
    put("nblr", -_chan(np.asarray(inp["ab_gla_b_lr"], f), 4))
    cwl = np.asarray(inp["lru_conv_w"], f)
    put("cwL", np.moveaxis(_chan(cwl, 16), 2, 3))
    put("cbL", _chan(np.asarray(inp["lru_conv_b"], f), 16))
    put("ba", _chan(np.asarray(inp["lru_b_a"], f), 16))
    put("bx", _chan(np.asarray(inp["lru_b_x"], f), 16))
    put("lam", _chan(np.asarray(inp["lru_lambda"], f), 16))
    out["pp"] = PPa
    out["br"] = np.concatenate([np.asarray(inp["ab_a_log"], f).reshape(-1), np.asarray(inp["ab_dt_bias"], f).reshape(-1)])[None, :]
    out["fnorm"] = np.asarray(inp["final_norm"], f)[None, :]
    cst = _masks()
    out["cf"], _ = _pack(CF_KEYS, cst)
    out["cb"], _ = _pack(CB_KEYS, cst)
    return out


def run(inputs, stage=99, cores=8, trace=False):
    f = np.float32
    key = stage
    if key not in _CACHE:
        _CACHE[key] = build_program(stage)
    nc, S = _CACHE[key]
    shared = _prep_weights(inputs)
    xp = np.asarray(inputs["x_prompt"], f)
    xs = np.asarray(inputs["x_sample"], f)
    in_maps = []
    for c in range(cores):
        m = dict(shared)
        sl = slice(16 * c, 16 * (c + 1))
        m["xin"] = np.concatenate([xp[c % 4], xs[sl].reshape(128, D)], axis=0)
        m["sd_in"] = np.ascontiguousarray(np.asarray(inputs["state_delta"], f)[:, sl])
        m["sdc_in"] = np.ascontiguousarray(np.asarray(inputs["state_delta_conv"], f)[:, sl]).reshape(2, 48, 3072)
        m["sg_in"] = np.ascontiguousarray(np.asarray(inputs["state_gla"], f)[:, sl])
        m["sl_in"] = np.ascontiguousarray(np.asarray(inputs["state_lru"], f)[:, sl])
        m["slc_in"] = np.ascontiguousarray(np.asarray(inputs["state_lru_conv"], f)[:, sl]).reshape(2, 48, 2048)
        in_maps.append(m)
    res = run_bass_kernel_spmd(nc, in_maps, core_ids=list(range(cores)), trace=trace)
    if trace:
        print('EXEC_TIME_NS', res.exec_time_ns)
        global LAST_RES
        LAST_RES = res
    R = list(res.results)
    while len(R) < 8:
        R.append(R[0])
    y_prompt = np.stack([R[c]["y_o"][:SEQ] for c in range(4)])
    y_sample = np.concatenate([R[c]["y_o"][SEQ:].reshape(16, 8, D) for c in range(8)], axis=0)
    p_delta = np.stack([R[c]["pd_o"] for c in range(4)], axis=1)
    p_dconv = np.stack([R[c]["pdc_o"] for c in range(4)], axis=1)
    p_gla = np.stack([R[c]["pg_o"] for c in range(4)], axis=1)
    p_lru = np.stack([R[c]["pl_o"] for c in range(4)], axis=1)
    p_lconv = np.stack([R[c]["plc_o"] for c in range(4)], axis=1)
    s_delta = np.concatenate([R[c]["sd_o"] for c in range(8)], axis=1)
    s_dconv = np.concatenate([R[c]["sdc_o"].reshape(2, 16, 3, 3072) for c in range(8)], axis=1)
    s_gla = np.concatenate([R[c]["sg_o"] for c in range(8)], axis=1)
    s_lru = np.concatenate([R[c]["sl_o"] for c in range(8)], axis=1)
    s_lconv = np.concatenate([R[c]["slc_o"].reshape(2, 16, 3, 2048) for c in range(8)], axis=1)
    outs = (y_prompt, y_sample, p_delta, p_dconv, p_gla, p_lru, p_lconv, s_delta, s_dconv, s_gla, s_lru, s_lconv)
    return tuple(np.ascontiguousarray(o, dtype=f) for o in outs)


def kernel(**inputs):
    return run(inputs, stage=99)
```
